# Optimizing a Trainium2 kernel written in Bass

```python
import math
import jax, jax.numpy as jnp
from jax import lax
import numpy as np

D_MODEL = 1024
BATCH = 2
SEQ = 8192
DEPTH = 2

HEAD_DIM = 64
N_BRANCH = 4
BRANCH_WIDTH = D_MODEL // 2

SWA_HEADS = BRANCH_WIDTH // HEAD_DIM
SWA_KV_HEADS = SWA_HEADS // 4
SWA_WINDOW = 128
SWA_BLOCK = 128

CONV_CH = BRANCH_WIDTH
CONV_WIDTH = 31

FOX_HEADS = BRANCH_WIDTH // HEAD_DIM
FOX_QBLOCK = 128
FOX_FGATE_BIAS = 3.0

MOBA_HEADS = BRANCH_WIDTH // HEAD_DIM
MOBA_BLOCK = 256
MOBA_TOPK = 3
MOBA_QCHUNK = 128

D_FF = 2816
FFN_CONV_WIDTH = 3

REL_BUCKETS = 32
REL_MAX_DIST = 128
N_BIAS_HEADS = SWA_HEADS + MOBA_HEADS

EPS = 1e-6
NEG = -1e30

IN_SIZES = (
    SWA_HEADS * HEAD_DIM, SWA_KV_HEADS * HEAD_DIM, SWA_KV_HEADS * HEAD_DIM,
    2 * CONV_CH,
    FOX_HEADS * HEAD_DIM, FOX_HEADS * HEAD_DIM, FOX_HEADS * HEAD_DIM, FOX_HEADS,
    MOBA_HEADS * HEAD_DIM, MOBA_HEADS * HEAD_DIM, MOBA_HEADS * HEAD_DIM,
    N_BRANCH * D_MODEL,
)
D_IN = sum(IN_SIZES)
IN_SPLITS = tuple(int(s) for s in np.cumsum(IN_SIZES)[:-1])

kernel_name = 'hybrid_gated_swa_conformer_fox_moba'


def rmsnorm(x, g):
    x32 = x.astype(jnp.float32)
    y = x32 * lax.rsqrt(jnp.mean(x32 * x32, axis=-1, keepdims=True) + EPS)
    return (y * g.astype(jnp.float32)).astype(x.dtype)


def layernorm(x, g, b):
    x32 = x.astype(jnp.float32)
    mu = jnp.mean(x32, axis=-1, keepdims=True)
    xc = x32 - mu
    var = jnp.mean(xc * xc, axis=-1, keepdims=True)
    return (xc * lax.rsqrt(var + EPS) * g.astype(jnp.float32) + b.astype(jnp.float32)).astype(x.dtype)


def causal_depthwise_conv(x, w, b):
    k_width, ch = w.shape
    y = lax.conv_general_dilated(
        x, w[:, None, :].astype(x.dtype), window_strides=(1,), padding=[(k_width - 1, 0)],
        dimension_numbers=('NWC', 'WIO', 'NWC'), feature_group_count=ch)
    return y + b.astype(x.dtype)


def rel_bucket(dist):
    n = jnp.maximum(dist, 0)
    max_exact = REL_BUCKETS // 2
    nf = jnp.maximum(n, 1).astype(jnp.float32)
    large = max_exact + (jnp.log(nf / max_exact) / math.log(REL_MAX_DIST / max_exact)
                         * (REL_BUCKETS - max_exact)).astype(jnp.int32)
    large = jnp.minimum(large, REL_BUCKETS - 1)
    return jnp.where(n < max_exact, n, large)


def sliding_window_gqa(q, k, v, sinks, bias_tab):
    bsz, seq = q.shape[0], q.shape[1]
    nb = seq // SWA_BLOCK
    grp = SWA_HEADS // SWA_KV_HEADS
    qb = q.reshape(bsz, nb, SWA_BLOCK, SWA_KV_HEADS, grp, HEAD_DIM)
    kb = k.reshape(bsz, nb, SWA_BLOCK, SWA_KV_HEADS, HEAD_DIM)
    vb = v.reshape(bsz, nb, SWA_BLOCK, SWA_KV_HEADS, HEAD_DIM)
    shift = lambda t: jnp.concatenate([jnp.zeros_like(t[:, :1]), t[:, :-1]], axis=1)
    kk = jnp.concatenate([shift(kb), kb], axis=2)
    vv = jnp.concatenate([shift(vb), vb], axis=2)
    logits = jnp.einsum('bnqkgd,bnskd->bnkgqs', qb, kk).astype(jnp.float32) * (HEAD_DIM ** -0.5)
    qpos = jnp.arange(SWA_BLOCK) + SWA_BLOCK
    kpos = jnp.arange(2 * SWA_BLOCK)
    dist = qpos[:, None] - kpos[None, :]
    bias = bias_tab.astype(jnp.float32)[rel_bucket(dist)]
    bias = bias.transpose(2, 0, 1).reshape(SWA_KV_HEADS, grp, SWA_BLOCK, 2 * SWA_BLOCK)
    in_window = (dist >= 0) & (dist < SWA_WINDOW)
    has_prev = (jnp.arange(nb)[:, None] > 0) | (kpos[None, :] >= SWA_BLOCK)
    mask = in_window[None, :, :] & has_prev[:, None, :]
    logits = jnp.where(mask[None, :, None, None], logits + bias, NEG)
    sink = sinks.astype(jnp.float32).reshape(SWA_KV_HEADS, grp, 1, 1)
    m = jnp.maximum(jnp.max(logits, axis=-1, keepdims=True), sink)
    p = jnp.exp(logits - m)
    denom = jnp.sum(p, axis=-1, keepdims=True) + jnp.exp(sink - m)
    probs = (p / denom).astype(v.dtype)
    out = jnp.einsum('bnkgqs,bnskd->bnqkgd', probs, vv)
    return out.reshape(bsz, seq, SWA_HEADS * HEAD_DIM)


def conformer_conv(u, w_dw, b_dw, ln_g, ln_b):
    a, gt = jnp.split(u, 2, axis=-1)
    h = a * jax.nn.sigmoid(gt)
    h = causal_depthwise_conv(h, w_dw, b_dw)
    h = layernorm(h, ln_g, ln_b)
    return jax.nn.silu(h)


def forgetting_attention(q, k, v, f_logit, b_f):
    bsz, seq = q.shape[0], q.shape[1]
    nb = seq // FOX_QBLOCK
    q = q.reshape(bsz, seq, FOX_HEADS, HEAD_DIM)
    k = k.reshape(bsz, seq, FOX_HEADS, HEAD_DIM)
    v = v.reshape(bsz, seq, FOX_HEADS, HEAD_DIM)
    log_f = jax.nn.log_sigmoid(f_logit.astype(jnp.float32) + b_f.astype(jnp.float32))
    c = jnp.cumsum(log_f, axis=1).transpose(0, 2, 1)
    qb = q.reshape(bsz, nb, FOX_QBLOCK, FOX_HEADS, HEAD_DIM).transpose(1, 0, 2, 3, 4)
    cb = c.reshape(bsz, FOX_HEADS, nb, FOX_QBLOCK).transpose(2, 0, 1, 3)
    kpos = jnp.arange(seq)

    def block(args):
        qi, ci, i = args
        logits = jnp.einsum('bqhd,bshd->bhqs', qi, k).astype(jnp.float32) * (HEAD_DIM ** -0.5)
        logits = logits + ci[..., None] - c[:, :, None, :]
        qpos = i * FOX_QBLOCK + jnp.arange(FOX_QBLOCK)
        logits = jnp.where(kpos[None, :] <= qpos[:, None], logits, NEG)
        p = jax.nn.softmax(logits, axis=-1).astype(v.dtype)
        return jnp.einsum('bhqs,bshd->bqhd', p, v)

    out = lax.map(block, (qb, cb, jnp.arange(nb)))
    return out.transpose(1, 0, 2, 3, 4).reshape(bsz, seq, FOX_HEADS * HEAD_DIM)


def moba_attention(q, k, v, bias_tab):
    bsz, seq = q.shape[0], q.shape[1]
    nblk = -(-seq // MOBA_BLOCK)
    s_pad = nblk * MOBA_BLOCK
    padw = ((0, 0), (0, s_pad - seq), (0, 0), (0, 0))
    to_heads = lambda t: jnp.pad(t.reshape(bsz, seq, MOBA_HEADS, HEAD_DIM), padw).transpose(0, 2, 1, 3)
    q, k, v = to_heads(q), to_heads(k), to_heads(v)
    kb = k.reshape(bsz, MOBA_HEADS, nblk, MOBA_BLOCK, HEAD_DIM)
    vb = v.reshape(bsz, MOBA_HEADS, nblk, MOBA_BLOCK, HEAD_DIM)
    kmean = jnp.mean(kb.astype(jnp.float32), axis=3)
    gate = jnp.einsum('bhsd,bhnd->bhsn', q.astype(jnp.float32), kmean)
    qblk = jnp.arange(s_pad) // MOBA_BLOCK
    past = jnp.arange(nblk)[None, :] < qblk[:, None]
    gate = jnp.where(past, gate, NEG)
    n_sel = min(MOBA_TOPK, nblk)
    _, sel = lax.top_k(gate, n_sel)
    sel_valid = sel < qblk[None, None, :, None]
    bt = bias_tab.astype(jnp.float32).T
    bidx = jnp.arange(bsz)[:, None, None, None]
    hidx = jnp.arange(MOBA_HEADS)[None, :, None, None]
    hidx5 = jnp.arange(MOBA_HEADS)[None, :, None, None, None]
    scale = HEAD_DIM ** -0.5
    nq = s_pad // MOBA_QCHUNK

    def chunk(i):
        start = i * MOBA_QCHUNK
        qc = lax.dynamic_slice_in_dim(q, start, MOBA_QCHUNK, axis=2)
        selc = lax.dynamic_slice_in_dim(sel, start, MOBA_QCHUNK, axis=2)
        validc = lax.dynamic_slice_in_dim(sel_valid, start, MOBA_QCHUNK, axis=2)
        qpos = start + jnp.arange(MOBA_QCHUNK)
        kg = kb[bidx, hidx, selc]
        vg = vb[bidx, hidx, selc]
        kpos_g = selc[..., None] * MOBA_BLOCK + jnp.arange(MOBA_BLOCK)
        lg = jnp.einsum('bhqd,bhqnkd->bhqnk', qc, kg).astype(jnp.float32) * scale
        lg = lg + bt[hidx5, rel_bucket(qpos[None, None, :, None, None] - kpos_g)]
        lg = jnp.where(validc[..., None], lg, NEG).reshape(bsz, MOBA_HEADS, MOBA_QCHUNK, n_sel * MOBA_BLOCK)
        own = start // MOBA_BLOCK
        ko = lax.dynamic_slice_in_dim(k, own * MOBA_BLOCK, MOBA_BLOCK, axis=2)
        vo = lax.dynamic_slice_in_dim(v, own * MOBA_BLOCK, MOBA_BLOCK, axis=2)
        kpos_o = own * MOBA_BLOCK + jnp.arange(MOBA_BLOCK)
        dist_o = qpos[:, None] - kpos_o[None, :]
        lo = jnp.einsum('bhqd,bhsd->bhqs', qc, ko).astype(jnp.float32) * scale + bt[:, rel_bucket(dist_o)]
        lo = jnp.where(dist_o >= 0, lo, NEG)
        p = jax.nn.softmax(jnp.concatenate([lg, lo], axis=-1), axis=-1).astype(v.dtype)
        pg = p[..., :n_sel * MOBA_BLOCK].reshape(bsz, MOBA_HEADS, MOBA_QCHUNK, n_sel, MOBA_BLOCK)
        po = p[..., n_sel * MOBA_BLOCK:]
        return jnp.einsum('bhqnk,bhqnkd->bhqd', pg, vg) + jnp.einsum('bhqs,bhsd->bhqd', po, vo)

    out = lax.map(chunk, jnp.arange(nq))
    out = out.transpose(1, 0, 3, 2, 4).reshape(bsz, s_pad, MOBA_HEADS * HEAD_DIM)
    return out[:, :seq]


def hybrid_layer(x, ln1_g, w_in, b_gate, b_fgate, sinks, conv_w, conv_b, conv_ln_g, conv_ln_b,
                 w_br, w_out, ln2_g, w_up, ffn_conv_w, ffn_conv_b, w_down, rel_bias):
    bsz, seq = x.shape[0], x.shape[1]
    h = rmsnorm(x, ln1_g)
    z = h @ w_in
    (qa, ka, va, ub, qc, kc, vc, fc, qd, kd, vd, gz) = jnp.split(z, IN_SPLITS, axis=-1)
    gates = jax.nn.sigmoid(gz.reshape(bsz, seq, N_BRANCH, D_MODEL) + b_gate)
    ya = sliding_window_gqa(qa, ka, va, sinks, rel_bias[:, :SWA_HEADS])
    yb = conformer_conv(ub, conv_w, conv_b, conv_ln_g, conv_ln_b)
    yc = forgetting_attention(qc, kc, vc, fc, b_fgate)
    yd = moba_attention(qd, kd, vd, rel_bias[:, SWA_HEADS:])
    branches = jnp.stack([ya, yb, yc, yd], axis=2)
    proj = jnp.einsum('bsnc,ncd->bsnd', branches, w_br)
    merged = jnp.einsum('bsnd,bsnd->bsd', gates, proj)
    x = x + merged @ w_out
    h2 = rmsnorm(x, ln2_g)
    u = causal_depthwise_conv(h2 @ w_up, ffn_conv_w, ffn_conv_b)
    g_ff, v_ff = jnp.split(u, 2, axis=-1)
    return x + (jax.nn.silu(g_ff) * v_ff) @ w_down


def setup_inputs(seed: int = 0) -> dict:
    key = jax.random.key(seed)
    ks = jax.random.split(key, 20)
    f32 = jnp.float32
    nrm = lambda kk, shape, s: jax.random.normal(kk, shape, f32) * s
    return {
        'x': nrm(ks[0], (BATCH, SEQ, D_MODEL), 1.0),
        'ln1_g': 1.0 + nrm(ks[1], (DEPTH, D_MODEL), 0.05),
        'w_in': nrm(ks[2], (DEPTH, D_MODEL, D_IN), D_MODEL ** -0.5),
        'b_gate': nrm(ks[3], (DEPTH, N_BRANCH, D_MODEL), 0.1),
        'b_fgate': FOX_FGATE_BIAS + nrm(ks[4], (DEPTH, FOX_HEADS), 0.1),
        'sinks': nrm(ks[5], (DEPTH, SWA_HEADS), 0.5),
        'conv_w': nrm(ks[6], (DEPTH, CONV_WIDTH, CONV_CH), CONV_WIDTH ** -0.5),
        'conv_b': nrm(ks[7], (DEPTH, CONV_CH), 0.02),
        'conv_ln_g': 1.0 + nrm(ks[8], (DEPTH, CONV_CH), 0.05),
        'conv_ln_b': nrm(ks[9], (DEPTH, CONV_CH), 0.02),
        'w_br': nrm(ks[10], (DEPTH, N_BRANCH, BRANCH_WIDTH, D_MODEL), BRANCH_WIDTH ** -0.5),
        'w_out': nrm(ks[11], (DEPTH, D_MODEL, D_MODEL), D_MODEL ** -0.5),
        'ln2_g': 1.0 + nrm(ks[12], (DEPTH, D_MODEL), 0.05),
        'w_up': nrm(ks[13], (DEPTH, D_MODEL, 2 * D_FF), D_MODEL ** -0.5),
        'ffn_conv_w': nrm(ks[14], (DEPTH, FFN_CONV_WIDTH, 2 * D_FF), FFN_CONV_WIDTH ** -0.5),
        'ffn_conv_b': nrm(ks[15], (DEPTH, 2 * D_FF), 0.02),
        'w_down': nrm(ks[16], (DEPTH, D_FF, D_MODEL), D_FF ** -0.5),
        'rel_bias': nrm(ks[17], (REL_BUCKETS, N_BIAS_HEADS), 0.5),
        'final_g': 1.0 + nrm(ks[18], (D_MODEL,), 0.05),
    }


def reference(x, ln1_g, w_in, b_gate, b_fgate, sinks, conv_w, conv_b, conv_ln_g, conv_ln_b,
              w_br, w_out, ln2_g, w_up, ffn_conv_w, ffn_conv_b, w_down, rel_bias, final_g):
    for layer in range(DEPTH):
        x = hybrid_layer(x, ln1_g[layer], w_in[layer], b_gate[layer], b_fgate[layer], sinks[layer],
                         conv_w[layer], conv_b[layer], conv_ln_g[layer], conv_ln_b[layer],
                         w_br[layer], w_out[layer], ln2_g[layer], w_up[layer], ffn_conv_w[layer],
                         ffn_conv_b[layer], w_down[layer], rel_bias)
    return rmsnorm(x, final_g)
```

```python
import contextlib
import math
import numpy as np
import ml_dtypes
import concourse.bass as bass
import concourse.mybir as mybir
from concourse.bass_utils import run_bass_kernel_spmd

F32 = mybir.dt.float32
BF16 = mybir.dt.bfloat16
AF = mybir.ActivationFunctionType
ALU = mybir.AluOpType
AX = mybir.AxisListType

COMPUTE = ("pe", "act", "dve", "pool")
DMAQ = ("sp", "pq")
NSEM_DMA = 8
NEGB = -30000.0
D = 1024
DIN = 8968
DFF = 2816
EPS = 1e-6


class Sched:
    def __init__(self, nc):
        self.nc = nc
        self.ins = []

    def op(self, eng, fn, reads=(), writes=()):
        self.ins.append(dict(eng=eng, fn=fn, reads=tuple(reads), writes=tuple(writes)))

    def barrier(self):
        self.ins.append(dict(eng="bar", fn=None, reads=(), writes=()))

    def finalize(self):
        nc = self.nc
        ins = self.ins
        last_writer = {}
        readers = {}
        real = []
        pend = None
        tails = {}
        seen_after = set()
        for it in ins:
            if it["eng"] == "bar":
                pend = set()
                for v in tails.values():
                    pend.update(v)
                seen_after = set()
                continue
            st = "pool" if it["eng"] == "pq" else it["eng"]
            it["bar_deps"] = set()
            if pend is not None and st not in seen_after:
                it["bar_deps"] = set(pend)
                seen_after.add(st)
            idx = len(real)
            real.append(it)
            if it["eng"] in DMAQ:
                tails.setdefault(it["eng"], []).append(idx)
                tails[it["eng"]] = tails[it["eng"]][-NSEM_DMA:]
            else:
                tails[it["eng"]] = [idx]
        ins = self.ins = real
        for i, it in enumerate(ins):
            deps = set(it["bar_deps"])
            for k in it["reads"]:
                if k in last_writer:
                    deps.add(last_writer[k])
            for k in it["writes"]:
                if k in last_writer:
                    deps.add(last_writer[k])
                deps.update(readers.get(k, ()))
            deps.discard(i)
            for k in it["reads"]:
                readers.setdefault(k, []).append(i)
            for k in it["writes"]:
                last_writer[k] = i
                readers[k] = []
            deps = set(j for j in deps if not (ins[j]["eng"] == "pe" and it["eng"] == "pe"))
            best = {}
            keep = set()
            for j in deps:
                ej = ins[j]["eng"]
                if ej in DMAQ:
                    keep.add(j)
                else:
                    best[ej] = max(best.get(ej, -1), j)
            keep.update(best.values())
            it["deps"] = keep

        def stream(e):
            return "pool" if e == "pq" else e

        signal = [False] * len(ins)
        for it in ins:
            for j in it["deps"]:
                signal[j] = True
        cnt = {e: 0 for e in COMPUTE}
        dcnt = {q: 0 for q in DMAQ}
        for i, it in enumerate(ins):
            e = it["eng"]
            if e in DMAQ:
                it["dma_k"] = dcnt[e]
                dcnt[e] += 1
            elif signal[i]:
                cnt[e] += 1
                it["sig"] = cnt[e]
        self.stats = dict(n=len(ins), sig=dict(cnt), dma=dict(dcnt))
        with contextlib.ExitStack() as es:
            sems = {e: es.enter_context(nc.semaphore("s_" + e)) for e in COMPUTE}
            dsems = {q: [es.enter_context(nc.semaphore("d_%s%d" % (q, n))) for n in range(NSEM_DMA)]
                     for q in DMAQ}
            block = es.enter_context(nc.Block())
            streams = {"pe": [], "act": [], "dve": [], "pool": [], "sp": []}
            for i, it in enumerate(ins):
                streams[stream(it["eng"])].append(i)

            def emit_stream(sname, eobj):
                waited = {}

                def wait(key, sem, val):
                    if waited.get(key, 0) >= val:
                        return
                    waited[key] = val
                    eobj.wait_ge(sem, val)

                for i in streams[sname]:
                    it = ins[i]
                    for j in sorted(it["deps"]):
                        jt = ins[j]
                        if jt["eng"] in DMAQ:
                            k = jt["dma_k"]
                            wait((jt["eng"], k % NSEM_DMA), dsems[jt["eng"]][k % NSEM_DMA],
                                 16 * (k // NSEM_DMA + 1))
                        else:
                            wait(jt["eng"], sems[jt["eng"]], jt["sig"])
                    if it["eng"] in DMAQ:
                        q = it["eng"]
                        k = it["dma_k"]
                        if k >= NSEM_DMA:
                            wait((q, k % NSEM_DMA), dsems[q][k % NSEM_DMA], 16 * (k // NSEM_DMA))
                        it["fn"](eobj).then_inc(dsems[q][k % NSEM_DMA], 16)
                    else:
                        h = it["fn"](eobj)
                        if "sig" in it:
                            h.then_inc(sems[it["eng"]], 1)
                for q in DMAQ:
                    if stream(q) != sname:
                        continue
                    n = dcnt[q]
                    for s_ in range(min(n, NSEM_DMA)):
                        kmax = ((n - 1 - s_) // NSEM_DMA) * NSEM_DMA + s_
                        wait((q, s_), dsems[q][s_], 16 * (kmax // NSEM_DMA + 1))

            @block.tensor
            def _(e):
                emit_stream("pe", e)

            @block.scalar
            def _(e):
                emit_stream("act", e)

            @block.vector
            def _(e):
                emit_stream("dve", e)

            @block.gpsimd
            def _(e):
                emit_stream("pool", e)

            @block.sync
            def _(e):
                emit_stream("sp", e)


class Rot:
    def __init__(self, items):
        self.items = list(items)
        self.i = 0

    def next(self):
        it = self.items[self.i % len(self.items)]
        self.i += 1
        return it


def rel_bucket_np(d):
    n = np.maximum(d, 0)
    nf = np.maximum(n, 1).astype(np.float32)
    large = 16 + (np.log(nf / np.float32(16)) / np.float32(math.log(128 / 16)) * np.float32(16)).astype(np.int32)
    large = np.minimum(large, 31)
    return np.where(n < 16, n, large)


def host_constants(S):
    NT, NBLK = S // 128, S // 256
    L = 1152
    dist = np.arange(L) - 511
    onehot = np.zeros((32, L), np.float32)
    bk = rel_bucket_np(dist)
    for dd in range(L):
        if dist[dd] >= 0:
            onehot[bk[dd], dd] = 1.0
    mmul = np.zeros((17, L), np.float32)
    madd = np.zeros((17, L), np.float32)
    swa_ok = (dist >= 0) & (dist < 128)
    cau_ok = dist >= 0
    mmul[0:8] = swa_ok
    madd[0:8] = np.where(swa_ok, 0.0, NEGB)
    mmul[8:16] = cau_ok
    madd[8:16] = np.where(cau_ok, 0.0, NEGB)
    madd[16] = np.where(cau_ok, 0.0, NEGB)
    ismoba = np.zeros((17, 1), np.float32)
    ismoba[8:16] = 1.0
    own = (np.arange(NT) // 2)[:, None]
    nn = np.arange(NBLK)[None, :]
    vmask = np.where(nn < own, 0.0, -1e30).astype(np.float32).reshape(1, NT * NBLK)
    omask = np.where(nn == own, 1.0, 0.0).astype(np.float32).reshape(1, NT * NBLK)
    return dict(c_onehot=onehot, c_mmul=mmul, c_madd=madd, c_ismoba=ismoba, c_vmask=vmask, c_omask=omask)


def build(S, DEPTH, dbg=False):
    nc = bass.Bass("TRN2", target_bir_lowering=False)
    NT, NB, NBLK = S // 128, S // 512, S // 256
    SBK = 1024
    NSB = S // SBK
    TPS = SBK // 128
    BPS = SBK // 512

    def din(name, shape, dt=F32):
        return nc.dram_tensor(name, list(shape), dt, kind="ExternalInput").ap()

    def dscr(name, shape, dt):
        return nc.dram_tensor(name, list(shape), dt, kind=("ExternalOutput" if dbg else "Internal")).ap()

    x_in = din("x", [S, D])
    ln1_g = din("ln1_g", [DEPTH, D]); w_in = din("w_in", [DEPTH, D, DIN])
    b_gate = din("b_gate", [DEPTH, 4, D]); b_fgate = din("b_fgate", [DEPTH, 8]); sinks = din("sinks", [DEPTH, 8])
    conv_w = din("conv_w", [DEPTH, 31, 512]); conv_b = din("conv_b", [DEPTH, 512])
    conv_ln_g = din("conv_ln_g", [DEPTH, 512]); conv_ln_b = din("conv_ln_b", [DEPTH, 512])
    w_br = din("w_br", [DEPTH, 4, 512, D]); w_out = din("w_out", [DEPTH, D, D]); ln2_g = din("ln2_g", [DEPTH, D])
    w_up = din("w_up", [DEPTH, D, 2 * DFF]); ffn_conv_w = din("ffn_conv_w", [DEPTH, 3, 2 * DFF])
    ffn_conv_b = din("ffn_conv_b", [DEPTH, 2 * DFF]); w_down = din("w_down", [DEPTH, DFF, D])
    rel_bias = din("rel_bias", [32, 16]); final_g = din("final_g", [1, D])
    c_onehot = din("c_onehot", [32, 1152]); c_mmul = din("c_mmul", [17, 1152]); c_madd = din("c_madd", [17, 1152])
    c_ismoba = din("c_ismoba", [17, 1]); c_vmask = din("c_vmask", [1, NT * NBLK]); c_omask = din("c_omask", [1, NT * NBLK])
    out = nc.dram_tensor("out", [S, D], F32, kind="ExternalOutput").ap()

    qTa = dscr("qTa", [512, S], BF16); kTa = dscr("kTa", [128, S], BF16); va = dscr("va", [S, 128], BF16)
    uTb = dscr("uTb", [512, S], F32); cvT = dscr("cvT", [512, S], F32)
    qTc = dscr("qTc", [512, S], BF16); kTc = dscr("kTc", [512, S], BF16); vc = dscr("vc", [S, 512], BF16)
    fTc = dscr("fTc", [8, S], F32)
    qTd = dscr("qTd", [512, S], BF16); kTd = dscr("kTd", [512, S], BF16); vd = dscr("vd", [S, 512], BF16)
    brT = dscr("brT", [4, 512, S], BF16)
    xmid = dscr("xmid", [S, D], F32)
    xbuf = [dscr("xbuf0", [S, D], F32), dscr("xbuf1", [S, D], F32)]
    Fd = dscr("Fd", [17, 1152], BF16)

    s = Sched(nc)
    es = contextlib.ExitStack()
    with es:
        def T(name, shape, dt):
            return es.enter_context(nc.sbuf_tensor(name, list(shape), dt))

        banks = [es.enter_context(nc.psum_tensor("bank%d" % i, [128, 512], F32)) for i in range(8)]
        bankrot = Rot(range(8))
        ARENA = 44544
        arena = T("arena", [128, ARENA], F32)

        class Carver:
            def __init__(self):
                self.off = 0

            def get(self, P, n, dt):
                nb = n * (2 if dt == BF16 else 4)
                nb = (nb + 63) // 64 * 64
                a, b_ = self.off // 4, (self.off + nb) // 4
                self.off += nb
                assert self.off <= ARENA * 4, (self.off, ARENA * 4)
                v = arena[0:P, a:b_]
                if dt == BF16:
                    v = v.bitcast(BF16)
                return v[:, 0:n]

        cd = Carver()
        ca = Carver()

        identb = T("identb", [128, 128], BF16)
        antib = T("antib", [128, 128], BF16)
        identf = T("identf", [128, 128], F32)
        onesm = T("onesm", [128, 128], F32)
        onesf = T("onesf", [128, 128], F32)
        e127 = T("e127", [128, 128], F32)
        lstrict = T("lstrict", [128, 128], F32)
        for (t, nm, val) in ((identb, "identb", 1.0), (identf, "identf", 1.0), (e127, "e127", 1.0), (lstrict, "lstrict", 1.0)):
            s.op("pool", lambda e, t=t, val=val: e.memset(t[:], val), writes=[nm])
        s.op("pool", lambda e: e.memset(onesm[:], 1.0 / 512), writes=["onesm"])
        s.op("pool", lambda e: e.memset(antib[:], 1.0), writes=["antib"])
        s.op("pool", lambda e: e.affine_select(out=antib[:], in_=antib[:], pattern=[[1, 128]], compare_op=ALU.is_equal,
                                               fill=0.0, base=-127, channel_multiplier=1), reads=["antib"], writes=["antib"])
        s.op("pool", lambda e: e.memset(onesf[:], 1.0), writes=["onesf"])
        s.op("pool", lambda e: e.affine_select(out=identb[:], in_=identb[:], pattern=[[-1, 128]], compare_op=ALU.is_equal,
                                               fill=0.0, base=0, channel_multiplier=1), reads=["identb"], writes=["identb"])
        s.op("pool", lambda e: e.affine_select(out=identf[:], in_=identf[:], pattern=[[-1, 128]], compare_op=ALU.is_equal,
                                               fill=0.0, base=0, channel_multiplier=1), reads=["identf"], writes=["identf"])
        s.op("pool", lambda e: e.affine_select(out=e127[:], in_=e127[:], pattern=[[0, 128]], compare_op=ALU.is_ge,
                                               fill=0.0, base=-127, channel_multiplier=1), reads=["e127"], writes=["e127"])
        s.op("pool", lambda e: e.affine_select(out=lstrict[:], in_=lstrict[:], pattern=[[1, 128]], compare_op=ALU.is_ge,
                                               fill=0.0, base=-1, channel_multiplier=-1), reads=["lstrict"], writes=["lstrict"])

        def transpose_rows(dst, dst_key, src_ap, R, eng="sp"):
            stg = ldstage_rot.next()
            b = bankrot.next()
            s.op(eng, lambda e: e.dma_start(out=stg[0:R, :], in_=src_ap), writes=[("ldstage", id(stg))])
            s.op("pe", lambda e: e.transpose(out=banks[b][:, 0:R], in_=stg[0:R, :], identity=identf[0:R, 0:R]),
                 reads=[("ldstage", id(stg)), "identf"], writes=[("bank", b)])
            s.op("dve", lambda e: e.tensor_copy(out=dst, in_=banks[b][:, 0:R]), reads=[("bank", b)], writes=[dst_key])

        ldstage = [T("ldstage%d" % i, [128, 128], F32) for i in range(2)]
        ldstage_rot = Rot(ldstage)

        cs = Carver()
        relb = T("relb", [32, 17], F32)
        oneh = cs.get(32, 1152, F32)
        mmul = cs.get(17, 1152, F32)
        madd = cs.get(17, 1152, F32)
        sub31 = T("sub31", [17, 2], F32)
        ftmp = cs.get(17, 1152, F32)
        fbf = cs.get(17, 1152, BF16)
        s.op("pool", lambda e: e.memset(relb[:], 0.0), writes=["relb"])
        s.op("pool", lambda e: e.memset(sub31[:], 0.0), writes=["sub31"])
        s.op("sp", lambda e: e.dma_start(out=relb[:, 0:16], in_=rel_bias), reads=["relb"], writes=["relb"])
        s.op("sp", lambda e: e.dma_start(out=oneh[:], in_=c_onehot), writes=["oneh"])
        s.op("sp", lambda e: e.dma_start(out=mmul[:], in_=c_mmul), writes=["mmul"])
        s.op("sp", lambda e: e.dma_start(out=madd[:], in_=c_madd), writes=["madd"])
        s.op("sp", lambda e: e.dma_start(out=sub31[0:16, 0:1], in_=rel_bias[31:32, :].rearrange("o h -> h o")),
             reads=["sub31"], writes=["sub31"])
        s.op("sp", lambda e: e.dma_start(out=sub31[:, 1:2], in_=c_ismoba), reads=["sub31"], writes=["sub31"])
        s.op("dve", lambda e: e.tensor_tensor(out=sub31[:, 0:1], in0=sub31[:, 0:1], in1=sub31[:, 1:2], op=ALU.mult),
             reads=["sub31"], writes=["sub31"])
        for c0 in (0, 512, 1024):
            wdt = min(512, 1152 - c0)
            b = bankrot.next()
            s.op("pe", lambda e, b=b, c0=c0, wdt=wdt: e.matmul(banks[b][0:17, 0:wdt], lhsT=relb[:, :], rhs=oneh[:, c0:c0 + wdt],
                                                               start=True, stop=True), reads=["relb", "oneh"], writes=[("bank", b)])
            s.op("dve", lambda e, b=b, c0=c0, wdt=wdt: e.tensor_scalar(out=ftmp[:, c0:c0 + wdt], in0=banks[b][0:17, 0:wdt],
                                                                      scalar1=sub31[:, 0:1], scalar2=None, op0=ALU.subtract),
                 reads=[("bank", b), "sub31"], writes=["ftmp"])
        s.op("dve", lambda e: e.tensor_tensor(out=ftmp[:], in0=ftmp[:], in1=mmul[:], op=ALU.mult), reads=["ftmp", "mmul"], writes=["ftmp"])
        s.op("dve", lambda e: e.tensor_tensor(out=fbf[:], in0=ftmp[:], in1=madd[:], op=ALU.add), reads=["ftmp", "madd"], writes=["fbf"])
        s.op("sp", lambda e: e.dma_start(out=Fd, in_=fbf[:]), reads=["fbf"], writes=["Fd"])
        s.barrier()

        gbc = T("gbc", [128, D], F32)
        ssm = T("ssm", [128, 8], F32)
        colv = T("colv", [128, 512], F32)
        xs_t = [cd.get(128, D, F32) for i in range(2)]
        xs_rot = Rot(range(2))
        junk = cd.get(128, D, F32)
        hb_t = [cd.get(128, D, BF16) for i in range(2)]
        hb_rot = Rot(range(2))
        hT = cd.get(128, 8 * (2 + SBK), BF16).rearrange("p (a b) -> p a b", a=8)
        wbuf = [cd.get(128, 16 * 256, BF16) for i in range(5)]
        wrot = Rot(range(5))
        stg = [cd.get(128, 1024, F32) for i in range(3)]
        stgrot = Rot(range(3))
        big = cd.get(128, 22 * SBK, BF16)
        big2 = cd.get(128, 8 * SBK, BF16)
        rec = cd.get(128, 512, F32)
        osb2 = cd.get(128, 512, F32)
        gsig = [cd.get(128, 512, F32) for i in range(2)]
        gsrot = Rot(range(2))
        ug = cd.get(128, 2 + SBK, F32); uv = cd.get(128, 2 + SBK, F32)
        tg = cd.get(128, SBK, F32); tv = cd.get(128, SBK, F32)
        fcw = T("fcw", [128, 4 * 44], F32)

        def load_gain(g_ap):
            s.op("sp", lambda e: e.dma_start(out=gbc[:], in_=g_ap.partition_broadcast(128)), writes=["gbc"])

        def norm_tiles(src, sb, to_hT=True, dst_dram=None):
            for i in range(TPS):
                xi = xs_rot.next(); hi = hb_rot.next()
                r0 = sb * SBK + i * 128
                s.op("sp", lambda e, xi=xi, r0=r0: e.dma_start(out=xs_t[xi][:], in_=src[r0:r0 + 128, :]), writes=[("xs", xi)])
                s.op("act", lambda e, xi=xi: e.activation(out=junk[:], in_=xs_t[xi][:], func=AF.Square, accum_out=ssm[:, 0:1]),
                     reads=[("xs", xi)], writes=["junk", "ssm"])
                s.op("act", lambda e: e.activation(out=ssm[:, 1:2], in_=ssm[:, 0:1], func=AF.Sqrt, scale=1.0 / D, bias=EPS),
                     reads=["ssm"], writes=["ssm"])
                s.op("dve", lambda e: e.reciprocal(out=ssm[:, 2:3], in_=ssm[:, 1:2]), reads=["ssm"], writes=["ssm"])
                if not to_hT:
                    s.op("dve", lambda e, xi=xi: e.scalar_tensor_tensor(out=junk[:], in0=xs_t[xi][:], scalar=ssm[:, 2:3], in1=gbc[:],
                                                                        op0=ALU.mult, op1=ALU.mult),
                         reads=[("xs", xi), "ssm", "gbc", "junk"], writes=["junk"])
                    s.op("sp", lambda e, r0=r0: e.dma_start(out=dst_dram[r0:r0 + 128, :], in_=junk[:]), reads=["junk"])
                    continue
                s.op("dve", lambda e, xi=xi, hi=hi: e.scalar_tensor_tensor(out=hb_t[hi][:], in0=xs_t[xi][:], scalar=ssm[:, 2:3], in1=gbc[:],
                                                                           op0=ALU.mult, op1=ALU.mult),
                     reads=[("xs", xi), "ssm", "gbc"], writes=[("hb", hi)])
                for half in range(2):
                    b = bankrot.next()
                    pb = banks[b][:].bitcast(BF16)
                    for q in range(4):
                        dc = half * 4 + q
                        s.op("pe", lambda e, pb=pb, q=q, dc=dc, hi=hi: e.transpose(out=pb[:, q * 128:(q + 1) * 128],
                                                                                  in_=hb_t[hi][:, dc * 128:(dc + 1) * 128], identity=identb[:]),
                             reads=[("hb", hi), "identb"], writes=[("bank", b)])
                    eng = "act" if half == 0 else "dve"
                    dstv = hT[:, half * 4:half * 4 + 4, 2 + i * 128:2 + (i + 1) * 128]
                    srcv = pb[:, 0:512].rearrange("p (q c) -> p q c", q=4)
                    if eng == "act":
                        s.op("act", lambda e, dstv=dstv, srcv=srcv: e.copy(out=dstv, in_=srcv), reads=[("bank", b)], writes=[("hT", i)])
                    else:
                        s.op("dve", lambda e, dstv=dstv, srcv=srcv: e.tensor_copy(out=dstv, in_=srcv), reads=[("bank", b)], writes=[("hT", i)])

        hT_all = [("hT", i) for i in range(TPS)]

        def wload(wi, src3, ncols, nk):
            v = wbuf[wi][:, 0:nk * ncols].rearrange("p (k c) -> p k c", k=nk)
            s.op("pq", lambda e: e.dma_start(out=v, in_=src3), writes=[("w", wi)])
            return v

        def phase_A(l, xsrc):
            load_gain(ln1_g[l:l + 1, :])
            wv = w_in[l].rearrange("(dc p) c -> p dc c", p=128)
            segs = [(0, 512, qTa, 0.125), (512, 128, kTa, 1.0), (1792, 512, qTc, 0.125), (2304, 512, kTc, 1.0),
                    (3336, 512, qTd, 0.125), (3848, 512, kTd, 1.0)]
            for sb in range(NSB):
                norm_tiles(xsrc, sb)
                t0 = sb * SBK
                for (c0, wd, dst, scl) in segs:
                    wi = wrot.next()
                    wt = wload(wi, wv[:, :, c0:c0 + wd], wd, 8)
                    for ct in range(wd // 128):
                        si = stgrot.next()
                        sv = stg[si][:].bitcast(BF16)
                        for tb in range(BPS):
                            b = bankrot.next()
                            for dc in range(8):
                                s.op("pe", lambda e, b=b, wt=wt, dc=dc, ct=ct, tb=tb: e.matmul(
                                    banks[b][:, :], lhsT=wt[:, dc, ct * 128:(ct + 1) * 128], rhs=hT[:, dc, 2 + tb * 512:2 + (tb + 1) * 512],
                                    start=(dc == 0), stop=(dc == 7)), reads=[("w", wi)] + hT_all, writes=[("bank", b)])
                            s.op("act", lambda e, b=b, sv=sv, tb=tb, scl=scl: e.activation(out=sv[:, tb * 512:(tb + 1) * 512], in_=banks[b][:, :],
                                                                                             func=AF.Copy, scale=scl),
                                 reads=[("bank", b)], writes=[("stg", si)])
                        s.op("sp", lambda e, dst=dst, ct=ct, sv=sv, t0=t0: e.dma_start(out=dst[ct * 128:(ct + 1) * 128, t0:t0 + SBK], in_=sv[:, 0:SBK]),
                             reads=[("stg", si)], writes=[(id(dst.tensor) if False else dst.tensor.name)])
                wi = wrot.next()
                wt = wload(wi, wv[:, :, 3328:3336], 8, 8)
                si = stgrot.next()
                for tb in range(BPS):
                    b = bankrot.next()
                    for dc in range(8):
                        s.op("pe", lambda e, b=b, wt=wt, dc=dc, tb=tb: e.matmul(banks[b][0:8, :], lhsT=wt[:, dc, 0:8],
                                                                               rhs=hT[:, dc, 2 + tb * 512:2 + (tb + 1) * 512], start=(dc == 0), stop=(dc == 7)),
                             reads=[("w", wi)] + hT_all, writes=[("bank", b)])
                    s.op("act", lambda e, b=b, si=si, tb=tb: e.copy(out=stg[si][0:8, tb * 512:(tb + 1) * 512], in_=banks[b][0:8, :]),
                         reads=[("bank", b)], writes=[("stg", si)])
                s.op("sp", lambda e, si=si, t0=t0: e.dma_start(out=fTc[:, t0:t0 + SBK], in_=stg[si][0:8, 0:SBK]), reads=[("stg", si)], writes=["fTc"])
                wia = wrot.next(); wta = wload(wia, wv[:, :, 768:1280], 512, 8)
                wig = wrot.next(); wtg = wload(wig, wv[:, :, 1280:1792], 512, 8)
                for ct in range(4):
                    si = stgrot.next()
                    for tb in range(BPS):
                        ba = bankrot.next(); bg = bankrot.next()
                        for (bb, wt_, wi_) in ((ba, wta, wia), (bg, wtg, wig)):
                            for dc in range(8):
                                s.op("pe", lambda e, bb=bb, wt_=wt_, dc=dc, ct=ct, tb=tb: e.matmul(
                                    banks[bb][:, :], lhsT=wt_[:, dc, ct * 128:(ct + 1) * 128], rhs=hT[:, dc, 2 + tb * 512:2 + (tb + 1) * 512],
                                    start=(dc == 0), stop=(dc == 7)), reads=[("w", wi_)] + hT_all, writes=[("bank", bb)])
                        sj = stgrot.next()
                        s.op("act", lambda e, bg=bg, sj=sj: e.activation(out=stg[sj][:, 0:512], in_=banks[bg][:, :], func=AF.Sigmoid),
                             reads=[("bank", bg)], writes=[("stg", sj)])
                        s.op("dve", lambda e, ba=ba, sj=sj, si=si, tb=tb: e.tensor_tensor(out=stg[si][:, tb * 512:(tb + 1) * 512],
                                                                                         in0=banks[ba][:, :], in1=stg[sj][:, 0:512], op=ALU.mult),
                             reads=[("bank", ba), ("stg", sj)], writes=[("stg", si)])
                    s.op("sp", lambda e, si=si, ct=ct, t0=t0: e.dma_start(out=uTb[ct * 128:(ct + 1) * 128, t0:t0 + SBK], in_=stg[si][:, 0:SBK]),
                         reads=[("stg", si)], writes=["uTb"])
                for (c0, wd, dst) in ((640, 128, va), (2816, 512, vc), (4360, 512, vd)):
                    wi = wrot.next()
                    wt = wload(wi, wv[:, :, c0:c0 + wd], wd, 8)
                    for i in range(TPS):
                        b = bankrot.next(); si = stgrot.next()
                        sv = stg[si][:].bitcast(BF16)
                        for dc in range(8):
                            s.op("pe", lambda e, b=b, wt=wt, dc=dc, i=i, wd=wd: e.matmul(banks[b][:, 0:wd], lhsT=hT[:, dc, 2 + i * 128:2 + (i + 1) * 128],
                                                                                       rhs=wt[:, dc, 0:wd], start=(dc == 0), stop=(dc == 7)),
                                 reads=[("w", wi), ("hT", i)], writes=[("bank", b)])
                        s.op("dve", lambda e, b=b, sv=sv, wd=wd: e.tensor_copy(out=sv[:, 0:wd], in_=banks[b][:, 0:wd]), reads=[("bank", b)], writes=[("stg", si)])
                        r0 = t0 + i * 128
                        s.op("sp", lambda e, dst=dst, sv=sv, wd=wd, r0=r0: e.dma_start(out=dst[r0:r0 + 128, :], in_=sv[:, 0:wd]),
                             reads=[("stg", si)], writes=[dst.tensor.name])

        QT = [ca.get(96, S, BF16) for i in range(2)]
        KT = [ca.get(96, S, BF16) for i in range(2)]
        VA = [ca.get(128, NT * 65, BF16).rearrange("p (a b) -> p a b", b=65) for i in range(2)]
        TAB = [ca.get(128, 1024, BF16) for i in range(2)]
        PT = [ca.get(128, 512, BF16) for i in range(3)]
        RES = [ca.get(64, 512, BF16) for i in range(2)]
        hrot = Rot(range(2)); ptrot = Rot(range(3)); resrot = Rot(range(2))
        arec = ca.get(128, 512, F32)
        osb = ca.get(64, 512, F32)
        esink = T("esink", [128, 8], F32)
        bfb = T("bfb", [128, 8], F32)
        fx = ca.get(NT, 128, F32); fl = ca.get(NT, 128, F32); fcs = ca.get(NT, 128, F32)
        cT = ca.get(128, NT, F32); cref = ca.get(128, NB, F32); cbq = ca.get(128, NT, F32)
        foff = T("foff", [NT, 2], F32)
        vmask = T("vmask", [128, NT * NBLK], BF16); omask = T("omask", [128, NT * NBLK], BF16)
        GW = min(NT, 512 // NBLK, 16)
        gm = ca.get(128, GW * NBLK, F32); sel = ca.get(128, GW * NBLK, F32)
        selb = ca.get(128, GW * 96, BF16).rearrange("p (a b) -> p a b", b=96)
        mx8 = T("mx8", [128, 8], F32)
        kmean = ca.get(64, NBLK, F32); kmh = ca.get(64, NBLK, BF16); kml = ca.get(64, NBLK, BF16)
        CH = min(S, 4096)
        ubuf = ca.get(128, 30 + CH, F32); cacc = ca.get(128, CH, F32)
        cwT = T("cwT", [128, 124], F32)
        amsk = ca.get(128, NT * NBLK, F32)

        s.op("sp", lambda e: e.dma_start(out=amsk[:], in_=c_vmask.partition_broadcast(128)), writes=["amsk"])
        s.op("dve", lambda e: e.tensor_copy(out=vmask[:], in_=amsk[:]), reads=["amsk"], writes=["vmask"])
        s.op("sp", lambda e: e.dma_start(out=amsk[:], in_=c_omask.partition_broadcast(128)), reads=["amsk"], writes=["amsk"])
        s.op("dve", lambda e: e.tensor_copy(out=omask[:], in_=amsk[:]), reads=["amsk"], writes=["omask"])
        s.barrier()

        def attn_init():
            for i in range(2):
                s.op("pool", lambda e, i=i: e.memset(VA[i][:, :, 64:65], 1.0), writes=[("VA", i)])
                s.op("pool", lambda e, i=i: e.memset(KT[i][64:96, :], 1.0), writes=[("KT", i)])
                s.op("pool", lambda e, i=i: e.affine_select(out=KT[i][64:96, :], in_=KT[i][64:96, :], pattern=[[1, S]], compare_op=ALU.is_ge,
                                                           fill=0.0, base=0, channel_multiplier=-256), reads=[("KT", i)], writes=[("KT", i)])
                s.op("pool", lambda e, i=i: e.affine_select(out=KT[i][64:96, :], in_=KT[i][64:96, :], pattern=[[-1, S]], compare_op=ALU.is_ge,
                                                           fill=0.0, base=255, channel_multiplier=256), reads=[("KT", i)], writes=[("KT", i)])
            s.op("pool", lambda e: e.memset(selb[:], 0.0), writes=["selb"])

        cTs = [cT, ca.get(128, NT, F32)]
        crefs = [cref, ca.get(128, NB, F32)]
        cbqs = [cbq, ca.get(128, NT, F32)]
        cbqrot = Rot(range(2))
        sb_rot = Rot([0, 1, 2]); ob_rot = Rot([3, 4])

        def head_params(kind, h):
            qsrc, ksrc, vsrc = {0: (qTa, kTa, va), 2: (qTc, kTc, vc), 3: (qTd, kTd, vd)}[kind]
            kvh = h // 4 if kind == 0 else h
            KA = 96 if kind == 3 else 64
            frow = {0: h, 2: 16, 3: 8 + h}[kind]
            return qsrc, ksrc, vsrc, kvh, KA, frow

        def attn_loads(kind, h, hi):
            qsrc, ksrc, vsrc, kvh, KA, frow = head_params(kind, h)
            s.op("sp", lambda e: e.dma_start(out=QT[hi][0:64, :], in_=qsrc[h * 64:(h + 1) * 64, :]), reads=[qsrc.tensor.name], writes=[("QT", hi)])
            s.op("sp", lambda e: e.dma_start(out=KT[hi][0:64, :], in_=ksrc[kvh * 64:(kvh + 1) * 64, :]), reads=[ksrc.tensor.name], writes=[("KT", hi)])
            s.op("sp", lambda e: e.dma_start(out=VA[hi][:, :, 0:64], in_=vsrc[:, kvh * 64:(kvh + 1) * 64].rearrange("(kt p) c -> p kt c", p=128)),
                 reads=[vsrc.tensor.name], writes=[("VA", hi)])
            tsrc = bass.AP(tensor=Fd.tensor, offset=frow * 1152, ap=[[1, 128], [1, 1024]])
            s.op("sp", lambda e: e.dma_start(out=TAB[hi][:], in_=tsrc), reads=["Fd"], writes=[("TAB", hi)])
            if kind == 2:
                s.op("sp", lambda e: e.dma_start(out=fx[:], in_=fTc[h:h + 1, :].rearrange("o (kt j) -> (o kt) j", j=128)), reads=["fTc"], writes=["fx"])

        def attn_prep(kind, h, hi):
            cT_, cref_ = cTs[hi], crefs[hi]
            if kind == 2:
                s.op("act", lambda e: e.activation(out=fl[:], in_=fx[:], func=AF.Sigmoid, bias=bfb[0:NT, h:h + 1]), reads=["fx", "bfb"], writes=["fl"])
                s.op("act", lambda e: e.activation(out=fl[:], in_=fl[:], func=AF.Ln), reads=["fl"], writes=["fl"])
                s.op("dve", lambda e: e.tensor_tensor_scan(out=fcs[:], data0=onesf[0:NT, :], data1=fl[:], initial=0.0, op0=ALU.mult, op1=ALU.add),
                     reads=["fl", "onesf"], writes=["fcs"])
                b = bankrot2.next()
                s.op("pe", lambda e, b=b: e.matmul(banks[b][0:NT, 0:1], lhsT=lstrict[0:NT, 0:NT], rhs=fcs[:, 127:128], start=True, stop=True),
                     reads=["lstrict", "fcs"], writes=[("bank", b)])
                s.op("dve", lambda e, b=b: e.tensor_copy(out=foff[:, 0:1], in_=banks[b][0:NT, 0:1]), reads=[("bank", b)], writes=["foff"])
                s.op("dve", lambda e: e.tensor_scalar(out=fcs[:], in0=fcs[:], scalar1=foff[:, 0:1], scalar2=None, op0=ALU.add),
                     reads=["fcs", "foff"], writes=["fcs"])
                b2 = bankrot2.next()
                s.op("pe", lambda e, b2=b2: e.transpose(out=banks[b2][:, 0:NT], in_=fcs[:], identity=identf[0:NT, 0:NT]),
                     reads=["fcs", "identf"], writes=[("bank", b2)])
                s.op("dve", lambda e, b2=b2: e.tensor_copy(out=cT_[:], in_=banks[b2][:, 0:NT]), reads=[("bank", b2)], writes=[("cT", hi)])
                b3 = bankrot2.next()
                s.op("pe", lambda e, b3=b3: e.matmul(banks[b3][:, 0:NB], lhsT=e127[:], rhs=cT_[:, 3::4], start=True, stop=True),
                     reads=["e127", ("cT", hi)], writes=[("bank", b3)])
                s.op("dve", lambda e, b3=b3: e.tensor_copy(out=cref_[:], in_=banks[b3][:, 0:NB]), reads=[("bank", b3)], writes=[("cref", hi)])
            if kind == 3:
                s.op("dve", lambda e: e.tensor_reduce(out=kmean[:], in_=KT[hi][0:64, :].rearrange("p (n j) -> p n j", j=256), axis=AX.X, op=ALU.add),
                     reads=[("KT", hi)], writes=["kmean"])
                s.op("dve", lambda e: e.tensor_scalar(out=kmean[:], in0=kmean[:], scalar1=1.0 / 256, scalar2=None, op0=ALU.mult), reads=["kmean"], writes=["kmean"])
                s.op("dve", lambda e: e.tensor_copy(out=kmh[:], in_=kmean[:]), reads=["kmean"], writes=["kmh"])
                s.op("dve", lambda e: e.tensor_tensor(out=kml[:], in0=kmean[:], in1=kmh[:], op=ALU.subtract), reads=["kmean", "kmh"], writes=["kml"])
                for g0 in range(0, NT, GW):
                    b = bankrot2.next()
                    for ii in range(GW):
                        i = g0 + ii
                        s.op("pe", lambda e, b=b, ii=ii, i=i: e.matmul(banks[b][:, ii * NBLK:(ii + 1) * NBLK], lhsT=QT[hi][0:64, i * 128:(i + 1) * 128],
                                                                      rhs=kmh[:, :], start=True, stop=False), reads=[("QT", hi), "kmh"], writes=[("bank", b)])
                        s.op("pe", lambda e, b=b, ii=ii, i=i: e.matmul(banks[b][:, ii * NBLK:(ii + 1) * NBLK], lhsT=QT[hi][0:64, i * 128:(i + 1) * 128],
                                                                      rhs=kml[:, :], start=False, stop=True), reads=[("QT", hi), "kml"], writes=[("bank", b)])
                    s.op("dve", lambda e, b=b, g0=g0: e.tensor_tensor(out=gm[:], in0=banks[b][:, 0:GW * NBLK], in1=vmask[:, g0 * NBLK:(g0 + GW) * NBLK], op=ALU.add),
                         reads=[("bank", b), "vmask"], writes=["gm"])
                    for ii in range(GW):
                        s.op("dve", lambda e, ii=ii: e.max(out=mx8[:], in_=gm[:, ii * NBLK:(ii + 1) * NBLK]), reads=["gm"], writes=["mx8"])
                        s.op("dve", lambda e, ii=ii: e.tensor_scalar(out=sel[:, ii * NBLK:(ii + 1) * NBLK], in0=gm[:, ii * NBLK:(ii + 1) * NBLK],
                                                                    scalar1=mx8[:, 2:3], scalar2=None, op0=ALU.is_ge), reads=["gm", "mx8"], writes=["sel"])
                    s.op("dve", lambda e, g0=g0: e.tensor_tensor(out=sel[:], in0=sel[:], in1=omask[:, g0 * NBLK:(g0 + GW) * NBLK], op=ALU.max),
                         reads=["sel", "omask"], writes=["sel"])
                    s.op("dve", lambda e: e.tensor_scalar(out=selb[:, :, 64:64 + NBLK], in0=sel[:].rearrange("p (g n) -> p g n", n=NBLK),
                                                          scalar1=-NEGB, scalar2=NEGB, op0=ALU.mult, op1=ALU.add), reads=["sel"], writes=["selb"])
                    for i4 in range(0, GW, 4):
                        b2 = bankrot2.next()
                        pb = banks[b2][:].bitcast(BF16)
                        for q in range(4):
                            s.op("pe", lambda e, pb=pb, q=q, i4=i4: e.transpose(out=pb[0:96, q * 128:(q + 1) * 128], in_=selb[:, i4 + q, :], identity=identb[:]),
                                 reads=["selb", "identb"], writes=[("bank", b2)])
                        c0 = (g0 + i4) * 128
                        s.op("act", lambda e, pb=pb, c0=c0: e.copy(out=QT[hi][64:96, c0:c0 + 512], in_=pb[64:96, 0:512]), reads=[("bank", b2)], writes=[("QT", hi)])

        bankrot2 = Rot([5, 6])

        def attn_main(kind, h, hi, mid_fn=None, conv_it=None):
            qsrc, ksrc, vsrc, kvh, KA, frow = head_params(kind, h)
            cT_, cref_ = cTs[hi], crefs[hi]
            pending = [None]

            def fin2():
                ob, qb = pending[0]
                pending[0] = None
                s.op("pe", lambda e: e.matmul(banks[7][0:64, :], lhsT=onesf[64:65, 0:64], rhs=arec[64:65, :], start=True, stop=True),
                     reads=["onesf", "arec"], writes=[("bank", 7)])
                s.op("act", lambda e, ob=ob: e.copy(out=osb[:], in_=banks[ob][0:64, :]), reads=[("bank", ob)], writes=["osb"])
                ri = resrot.next()
                s.op("dve", lambda e, ri=ri: e.tensor_tensor(out=RES[ri][:], in0=osb[:], in1=banks[7][0:64, :], op=ALU.mult),
                     reads=["osb", ("bank", 7)], writes=[("RES", ri)])
                s.op("sp", lambda e, ri=ri, qb=qb: e.dma_start(out=brT[kind, h * 64:(h + 1) * 64, qb * 512:(qb + 1) * 512], in_=RES[ri][:]),
                     reads=[("RES", ri)], writes=["brT"])

            for qb in range(NB):
                kt_lo = max(0, 4 * qb - 1) if kind == 0 else 0
                kt_hi = 4 * qb + 3
                tiles = list(range(kt_lo, kt_hi + 1))
                cq = None
                if kind == 2:
                    cqi = cbqrot.next()
                    cq = cbqs[cqi]
                    s.op("dve", lambda e, qb=qb, cq=cq: e.tensor_scalar(out=cq[:], in0=cT_[:], scalar1=-1.0, scalar2=cref_[:, qb:qb + 1], op0=ALU.mult, op1=ALU.add),
                         reads=[("cT", hi), ("cref", hi)], writes=[("cbq", cqi)])
                ob = ob_rot.next()
                sbank = {}

                def issue_S(kt, qb=qb, sbank=sbank):
                    near = kt >= 4 * qb - 1
                    sbk_ = sb_rot.next()
                    sbank[kt] = sbk_
                    s.op("pe", lambda e, sbk_=sbk_, kt=kt, qb=qb, near=near: e.matmul(banks[sbk_][:, :], lhsT=KT[hi][0:KA, kt * 128:(kt + 1) * 128],
                                                                                    rhs=QT[hi][0:KA, qb * 512:(qb + 1) * 512], start=True, stop=not near),
                         reads=[("KT", hi), ("QT", hi)], writes=[("bank", sbk_)])
                    if near:
                        off = 512 * qb - 128 * kt + 384
                        s.op("pe", lambda e, sbk_=sbk_, off=off: e.matmul(banks[sbk_][:, :], lhsT=antib[:], rhs=TAB[hi][:, off:off + 512], start=False, stop=True),
                             reads=[("TAB", hi), "antib"], writes=[("bank", sbk_)])

                issue_S(tiles[0])
                issue_S(tiles[1])
                for idx, kt in enumerate(tiles):
                    sbk_ = sbank[kt]
                    pi = ptrot.next()
                    if kind == 2:
                        s.op("act", lambda e, sbk_=sbk_, pi=pi, kt=kt, cq=cq: e.activation(out=PT[pi][:], in_=banks[sbk_][:, :], func=AF.Exp, bias=cq[:, kt:kt + 1]),
                             reads=[("bank", sbk_), ("cbq", cqi)], writes=[("PT", pi)])
                    else:
                        s.op("act", lambda e, sbk_=sbk_, pi=pi: e.activation(out=PT[pi][:], in_=banks[sbk_][:, :], func=AF.Exp),
                             reads=[("bank", sbk_)], writes=[("PT", pi)])
                    if idx + 2 < len(tiles):
                        issue_S(tiles[idx + 2])
                    s.op("pe", lambda e, ob=ob, pi=pi, kt=kt, kt_lo=kt_lo, kt_hi=kt_hi: e.matmul(banks[ob][0:65, :], lhsT=VA[hi][:, kt, 0:65], rhs=PT[pi][:],
                                                                      start=(kt == kt_lo), stop=(kt == kt_hi)),
                         reads=[("VA", hi), ("PT", pi)], writes=[("bank", ob)])
                    if kind == 2:
                        s.op("pe", lambda e, kt=kt: e.matmul(banks[6][0:64, :], lhsT=KT[hi][0:64, kt * 128:kt * 128 + 64], rhs=QT[hi][0:64, 0:512],
                                                             start=True, stop=True), reads=[("KT", hi), ("QT", hi)], writes=[("bank", 6)])
                    if idx == 1 and pending[0] is not None:
                        fin2()
                if kind == 0:
                    s.op("dve", lambda e, ob=ob: e.tensor_scalar(out=arec[64:65, :], in0=banks[ob][64:65, :], scalar1=esink[64:65, h:h + 1], scalar2=None, op0=ALU.add),
                         reads=[("bank", ob), "esink"], writes=["arec"])
                    s.op("dve", lambda e: e.reciprocal(out=arec[64:65, :], in_=arec[64:65, :]), reads=["arec"], writes=["arec"])
                else:
                    s.op("dve", lambda e, ob=ob: e.reciprocal(out=arec[64:65, :], in_=banks[ob][64:65, :]), reads=[("bank", ob)], writes=["arec"])
                pending[0] = (ob, qb)
                if conv_it is not None and kind != 0:
                    next(conv_it, None)
                if mid_fn is not None and qb == NB // 2 - 1:
                    mid_fn()
            fin2()

        def conv_branch_gen(l):
            for cc in range(4):
                for c0 in range(0, S, CH):
                    if c0 > 0:
                        s.op("dve", lambda e: e.tensor_copy(out=ubuf[:, 0:30], in_=ubuf[:, CH:CH + 30]), reads=["ubuf"], writes=["ubuf"])
                    else:
                        s.op("dve", lambda e: e.memset(ubuf[:, 0:30], 0.0), reads=["ubuf"], writes=["ubuf"])
                    s.op("sp", lambda e, cc=cc, c0=c0: e.dma_start(out=ubuf[:, 30:30 + CH], in_=uTb[cc * 128:(cc + 1) * 128, c0:c0 + CH]),
                         reads=["uTb", "ubuf"], writes=["ubuf"])
                    s.op("dve", lambda e, cc=cc: e.tensor_scalar(out=cacc[:], in0=ubuf[:, 0:CH], scalar1=cwT[:, cc:cc + 1], scalar2=colv[:, cc:cc + 1],
                                                                op0=ALU.mult, op1=ALU.add), reads=["ubuf", "cwT", "colv"], writes=["cacc"])
                    yield
                    for k in range(1, 31):
                        s.op("dve", lambda e, cc=cc, k=k: e.scalar_tensor_tensor(out=cacc[:], in0=ubuf[:, k:k + CH], scalar=cwT[:, k * 4 + cc:k * 4 + cc + 1],
                                                                                in1=cacc[:], op0=ALU.mult, op1=ALU.add), reads=["ubuf", "cwT", "cacc"], writes=["cacc"])
                        if k < 30:
                            yield
                    s.op("sp", lambda e, cc=cc, c0=c0: e.dma_start(out=cvT[cc * 128:(cc + 1) * 128, c0:c0 + CH], in_=cacc[:]), reads=["cacc"], writes=["cvT"])
                    yield

        def phase_B(l):
            attn_init()
            s.op("sp", lambda e: e.dma_start(out=esink[:], in_=sinks[l:l + 1, :].partition_broadcast(128)), writes=["esink"])
            s.op("act", lambda e: e.activation(out=esink[:], in_=esink[:], func=AF.Exp), reads=["esink"], writes=["esink"])
            s.op("sp", lambda e: e.dma_start(out=bfb[:], in_=b_fgate[l:l + 1, :].partition_broadcast(128)), writes=["bfb"])
            transpose_rows(cwT[:, 0:124], "cwT", conv_w[l].rearrange("k (cc p) -> (k cc) p", p=128), 124)
            transpose_rows(colv[:, 0:4], "colv", conv_b[l:l + 1, :].rearrange("o (cc p) -> (o cc) p", p=128), 4)
            s.barrier()
            conv_it = conv_branch_gen(l)
            heads = [(2, h) for h in range(8)] + [(3, h) for h in range(8)] + [(0, h) for h in range(8)]
            his = [i % 2 for i in range(len(heads))]
            attn_loads(heads[0][0], heads[0][1], his[0])
            attn_prep(heads[0][0], heads[0][1], his[0])
            for i, (kind, h) in enumerate(heads):
                mid = None
                if i + 1 < len(heads):
                    nk, nh = heads[i + 1]
                    attn_loads(nk, nh, his[i + 1])
                    mid = (lambda nk=nk, nh=nh, nhi=his[i + 1]: attn_prep(nk, nh, nhi))
                attn_main(kind, h, his[i], mid_fn=mid, conv_it=conv_it)
            for _ in conv_it:
                pass

        def phase_C(l, xsrc):
            load_gain(ln1_g[l:l + 1, :])
            transpose_rows(colv[:, 0:32], "colv", b_gate[l].rearrange("n (dc p) -> (n dc) p", p=128), 32)
            transpose_rows(colv[:, 32:36], "colv", conv_ln_g[l:l + 1, :].rearrange("o (cc p) -> (o cc) p", p=128), 4)
            transpose_rows(colv[:, 36:40], "colv", conv_ln_b[l:l + 1, :].rearrange("o (cc p) -> (o cc) p", p=128), 4)
            wgv = w_in[l].rearrange("(dc p) c -> p dc c", p=128)
            wbv = w_br[l].rearrange("n (cc p) d -> p (n cc) d", p=128)
            wov = w_out[l].rearrange("(dc p) d -> p dc d", p=128)
            brv = big[:, 0:16 * SBK].rearrange("p (k t) -> p k t", k=16)
            mgv = big2[:].rearrange("p (k t) -> p k t", k=8)
            for sb in range(NSB):
                t0 = sb * SBK
                norm_tiles(xsrc, sb)
                for n in (0, 2, 3):
                    for cc in range(4):
                        s.op("sp", lambda e, n=n, cc=cc, t0=t0: e.dma_start(out=brv[:, n * 4 + cc, :], in_=brT[n, cc * 128:(cc + 1) * 128, t0:t0 + SBK]),
                             reads=["brT"], writes=[("br", n * 4 + cc)])
                for tb in range(BPS):
                    ys = []
                    for cc in range(4):
                        si = stgrot.next() if cc < 3 else None
                        ys.append(si)
                    ytile = [stg[ys[0]][:, 0:512], stg[ys[1]][:, 0:512], stg[ys[2]][:, 0:512], junk[:, 0:512]]
                    ykey = [("stg", ys[0]), ("stg", ys[1]), ("stg", ys[2]), "junk"]
                    sq = [stg[ys[0]][:, 512:1024], stg[ys[1]][:, 512:1024], stg[ys[2]][:, 512:1024], junk[:, 512:1024]]
                    bm = bankrot.next(); bq = bankrot.next()
                    for cc in range(4):
                        s.op("sp", lambda e, cc=cc, tb=tb, t0=t0, ytile=ytile: e.dma_start(out=ytile[cc], in_=cvT[cc * 128:(cc + 1) * 128, t0 + tb * 512:t0 + (tb + 1) * 512]),
                             reads=["cvT"], writes=[ykey[cc]])
                        s.op("act", lambda e, cc=cc, sq=sq, ytile=ytile: e.activation(out=sq[cc], in_=ytile[cc], func=AF.Square), reads=[ykey[cc]], writes=[ykey[cc]])
                    for cc in range(4):
                        s.op("pe", lambda e, cc=cc, bm=bm, ytile=ytile: e.matmul(banks[bm][:, :], lhsT=onesm[:], rhs=ytile[cc], start=(cc == 0), stop=(cc == 3)),
                             reads=["onesm", ykey[cc]], writes=[("bank", bm)])
                    for cc in range(4):
                        s.op("pe", lambda e, cc=cc, bq=bq, sq=sq: e.matmul(banks[bq][:, :], lhsT=onesm[:], rhs=sq[cc], start=(cc == 0), stop=(cc == 3)),
                             reads=["onesm", ykey[cc]], writes=[("bank", bq)])
                    s.op("act", lambda e, bm=bm: e.activation(out=rec[:], in_=banks[bm][:, :], func=AF.Square), reads=[("bank", bm)], writes=["rec"])
                    s.op("dve", lambda e, bq=bq: e.tensor_tensor(out=rec[:], in0=banks[bq][:, :], in1=rec[:], op=ALU.subtract), reads=[("bank", bq), "rec"], writes=["rec"])
                    s.op("act", lambda e: e.activation(out=rec[:], in_=rec[:], func=AF.Sqrt, bias=EPS), reads=["rec"], writes=["rec"])
                    s.op("dve", lambda e: e.reciprocal(out=rec[:], in_=rec[:]), reads=["rec"], writes=["rec"])
                    for cc in range(4):
                        s.op("dve", lambda e, cc=cc, bm=bm, ytile=ytile: e.tensor_tensor(out=ytile[cc], in0=ytile[cc], in1=banks[bm][:, :], op=ALU.subtract),
                             reads=[ykey[cc], ("bank", bm)], writes=[ykey[cc]])
                        s.op("dve", lambda e, cc=cc, ytile=ytile: e.tensor_tensor(out=ytile[cc], in0=ytile[cc], in1=rec[:], op=ALU.mult), reads=[ykey[cc], "rec"], writes=[ykey[cc]])
                        s.op("act", lambda e, cc=cc, tb=tb, ytile=ytile: e.activation(out=brv[:, 4 + cc, tb * 512:(tb + 1) * 512], in_=ytile[cc], func=AF.Silu,
                                                                        scale=colv[:, 32 + cc:33 + cc], bias=colv[:, 36 + cc:37 + cc]),
                             reads=[ykey[cc], "colv"], writes=[("br", 4 + cc)])
                for dg in range(4):
                    for pr in range(2):
                        wi = wrot.next()
                        wbt = wload(wi, wbv[:, pr * 8:pr * 8 + 8, dg * 256:(dg + 1) * 256], 256, 8)
                        wj = wrot.next()
                        c0 = 4872 + pr * 2048 + dg * 256
                        va_ = wbuf[wj][:, 0:16 * 256].rearrange("p (k c) -> p k c", k=16)
                        s.op("pq", lambda e, va_=va_, c0=c0: e.dma_start(out=va_[:, 0:8, :], in_=wgv[:, :, c0:c0 + 256]), writes=[("w", wj)])
                        s.op("pq", lambda e, va_=va_, c0=c0: e.dma_start(out=va_[:, 8:16, :], in_=wgv[:, :, c0 + 1024:c0 + 1280]), reads=[("w", wj)], writes=[("w", wj)])
                        for n in (2 * pr, 2 * pr + 1):
                            k0 = (n % 2) * 8
                            for dt in range(2):
                                dcol = dg * 2 + dt
                                for tb in range(BPS):
                                    bp = bankrot.next(); bg = bankrot.next()
                                    for cc in range(4):
                                        s.op("pe", lambda e, bp=bp, n=n, cc=cc, dt=dt, tb=tb, wbt=wbt: e.matmul(
                                            banks[bp][:, :], lhsT=wbt[:, (n % 2) * 4 + cc, dt * 128:(dt + 1) * 128], rhs=brv[:, n * 4 + cc, tb * 512:(tb + 1) * 512],
                                            start=(cc == 0), stop=(cc == 3)), reads=[("w", wi), ("br", n * 4 + cc)], writes=[("bank", bp)])
                                    for dc in range(8):
                                        s.op("pe", lambda e, bg=bg, dc=dc, dt=dt, tb=tb, va_=va_, k0=k0: e.matmul(
                                            banks[bg][:, :], lhsT=va_[:, k0 + dc, dt * 128:(dt + 1) * 128], rhs=hT[:, dc, 2 + tb * 512:2 + (tb + 1) * 512],
                                            start=(dc == 0), stop=(dc == 7)), reads=[("w", wj)] + hT_all, writes=[("bank", bg)])
                                    gs = gsrot.next()
                                    s.op("act", lambda e, bg=bg, n=n, dcol=dcol, gs=gs: e.activation(out=gsig[gs][:], in_=banks[bg][:, :], func=AF.Sigmoid,
                                                                                                     bias=colv[:, n * 8 + dcol:n * 8 + dcol + 1]),
                                         reads=[("bank", bg), "colv"], writes=[("gsig", gs)])
                                    mslice = mgv[:, dcol, tb * 512:(tb + 1) * 512]
                                    if n == 0:
                                        s.op("dve", lambda e, bp=bp, mslice=mslice, gs=gs: e.tensor_tensor(out=mslice, in0=banks[bp][:, :], in1=gsig[gs][:], op=ALU.mult),
                                             reads=[("bank", bp), ("gsig", gs)], writes=[("mg", dcol)])
                                    else:
                                        s.op("dve", lambda e, bp=bp, gs=gs: e.tensor_tensor(out=osb2[:], in0=banks[bp][:, :], in1=gsig[gs][:], op=ALU.mult),
                                             reads=[("bank", bp), ("gsig", gs)], writes=["osb2"])
                                        s.op("pool", lambda e, mslice=mslice: e.tensor_tensor(out=mslice, in0=mslice, in1=osb2[:], op=ALU.add),
                                             reads=["osb2", ("mg", dcol)], writes=[("mg", dcol)])
                for dh in range(2):
                    wi = wrot.next()
                    wot = wload(wi, wov[:, :, dh * 512:(dh + 1) * 512], 512, 8)
                    for i in range(TPS):
                        b = bankrot.next(); xi = xs_rot.next()
                        r0 = t0 + i * 128
                        s.op("sp", lambda e, xi=xi, r0=r0, dh=dh: e.dma_start(out=xs_t[xi][:, 0:512], in_=xsrc[r0:r0 + 128, dh * 512:(dh + 1) * 512]), writes=[("xs", xi)])
                        for dc in range(8):
                            s.op("pe", lambda e, b=b, dc=dc, i=i, wot=wot: e.matmul(banks[b][:, :], lhsT=mgv[:, dc, i * 128:(i + 1) * 128], rhs=wot[:, dc, :],
                                                                                   start=(dc == 0), stop=(dc == 7)), reads=[("w", wi), ("mg", dc)], writes=[("bank", b)])
                        s.op("dve", lambda e, b=b, xi=xi: e.tensor_tensor(out=xs_t[xi][:, 0:512], in0=xs_t[xi][:, 0:512], in1=banks[b][:, :], op=ALU.add),
                             reads=[("bank", b), ("xs", xi)], writes=[("xs", xi)])
                        s.op("sp", lambda e, xi=xi, r0=r0, dh=dh: e.dma_start(out=xmid[r0:r0 + 128, dh * 512:(dh + 1) * 512], in_=xs_t[xi][:, 0:512]),
                             reads=[("xs", xi)], writes=["xmid"])

        def phase_D(l, xdst):
            load_gain(ln2_g[l:l + 1, :])
            for k in range(3):
                transpose_rows(fcw[:, k * 44:(k + 1) * 44], "fcw", ffn_conv_w[l, k:k + 1, :].rearrange("o (j p) -> (o j) p", p=128), 44)
            transpose_rows(fcw[:, 132:176], "fcw", ffn_conv_b[l:l + 1, :].rearrange("o (j p) -> (o j) p", p=128), 44)
            wuv = w_up[l].rearrange("(dc p) c -> p dc c", p=128)
            wdv = w_down[l].rearrange("(j p) d -> p j d", p=128)
            actv = big[:, 0:22 * SBK].rearrange("p (j t) -> p j t", j=22)
            for sb in range(NSB):
                t0 = sb * SBK
                if sb == 0:
                    s.op("dve", lambda e: e.memset(hT[:, :, 0:2], 0.0), reads=hT_all + ["hThalo"], writes=["hThalo"])
                else:
                    s.op("dve", lambda e: e.tensor_copy(out=hT[:, :, 0:2], in_=hT[:, :, SBK:SBK + 2]), reads=hT_all + ["hThalo"], writes=["hThalo"])
                norm_tiles(xmid, sb)
                for j in range(22):
                    wi = wrot.next()
                    wv2 = wbuf[wi][:, 0:16 * 128].rearrange("p (k c) -> p k c", k=16)
                    s.op("pq", lambda e, wv2=wv2, j=j: e.dma_start(out=wv2[:, 0:8, :], in_=wuv[:, :, j * 128:(j + 1) * 128]), writes=[("w", wi)])
                    s.op("pq", lambda e, wv2=wv2, j=j: e.dma_start(out=wv2[:, 8:16, :], in_=wuv[:, :, DFF + j * 128:DFF + (j + 1) * 128]), reads=[("w", wi)], writes=[("w", wi)])
                    for (gi, ub, ukey) in ((0, ug, "ug"), (1, uv, "uv")):
                        b = bankrot.next()
                        for dc in range(8):
                            s.op("pe", lambda e, b=b, dc=dc, gi=gi, wv2=wv2: e.matmul(banks[b][:, 0:2], lhsT=wv2[:, gi * 8 + dc, :], rhs=hT[:, dc, 0:2],
                                                                                     start=(dc == 0), stop=(dc == 7)), reads=[("w", wi), "hThalo"], writes=[("bank", b)])
                        s.op("act", lambda e, b=b, ub=ub: e.copy(out=ub[:, 0:2], in_=banks[b][:, 0:2]), reads=[("bank", b)], writes=[ukey])
                        for tb in range(BPS):
                            b = bankrot.next()
                            for dc in range(8):
                                s.op("pe", lambda e, b=b, dc=dc, gi=gi, tb=tb, wv2=wv2: e.matmul(banks[b][:, :], lhsT=wv2[:, gi * 8 + dc, :],
                                                                                                rhs=hT[:, dc, 2 + tb * 512:2 + (tb + 1) * 512], start=(dc == 0), stop=(dc == 7)),
                                     reads=[("w", wi)] + hT_all, writes=[("bank", b)])
                            s.op("act", lambda e, b=b, ub=ub, tb=tb: e.copy(out=ub[:, 2 + tb * 512:2 + (tb + 1) * 512], in_=banks[b][:, :]), reads=[("bank", b)], writes=[ukey])
                        idx = gi * 22 + j
                        tt, tkey = (tg, "tg") if gi == 0 else (tv, "tv")
                        s.op("dve", lambda e, ub=ub, tt=tt, idx=idx: e.tensor_scalar(out=tt[:], in0=ub[:, 0:SBK], scalar1=fcw[:, idx:idx + 1], scalar2=fcw[:, 132 + idx:133 + idx],
                                                                                    op0=ALU.mult, op1=ALU.add), reads=[ukey, "fcw"], writes=[tkey])
                        for k in (1, 2):
                            s.op("dve", lambda e, ub=ub, tt=tt, idx=idx, k=k: e.scalar_tensor_tensor(out=tt[:], in0=ub[:, k:k + SBK], scalar=fcw[:, k * 44 + idx:k * 44 + idx + 1],
                                                                                                    in1=tt[:], op0=ALU.mult, op1=ALU.add), reads=[ukey, "fcw", tkey], writes=[tkey])
                    s.op("act", lambda e: e.activation(out=tg[:], in_=tg[:], func=AF.Silu), reads=["tg"], writes=["tg"])
                    s.op("pool", lambda e, j=j: e.tensor_tensor(out=actv[:, j, :], in0=tg[:], in1=tv[:], op=ALU.mult), reads=["tg", "tv"], writes=[("act", j)])
                for dh in range(2):
                    wdt = big2[:, 0:22 * 256].rearrange("p (j c) -> p j c", j=22)
                    for q in range(2):
                        s.op("pq", lambda e, dh=dh, q=q, wdt=wdt: e.dma_start(out=wdt, in_=wdv[:, :, dh * 512 + q * 256:dh * 512 + (q + 1) * 256]), writes=["wd"])
                        for i in range(TPS):
                            b = bankrot.next(); xi = xs_rot.next()
                            r0 = t0 + i * 128
                            c0 = dh * 512 + q * 256
                            s.op("sp", lambda e, xi=xi, r0=r0, c0=c0: e.dma_start(out=xs_t[xi][:, 0:256], in_=xmid[r0:r0 + 128, c0:c0 + 256]), reads=["xmid"], writes=[("xs", xi)])
                            for j in range(22):
                                s.op("pe", lambda e, b=b, j=j, i=i, wdt=wdt: e.matmul(banks[b][:, 0:256], lhsT=actv[:, j, i * 128:(i + 1) * 128], rhs=wdt[:, j, :],
                                                                                     start=(j == 0), stop=(j == 21)), reads=["wd", ("act", j)], writes=[("bank", b)])
                            s.op("dve", lambda e, b=b, xi=xi: e.tensor_tensor(out=xs_t[xi][:, 0:256], in0=xs_t[xi][:, 0:256], in1=banks[b][:, 0:256], op=ALU.add),
                                 reads=[("bank", b), ("xs", xi)], writes=[("xs", xi)])
                            s.op("sp", lambda e, xi=xi, r0=r0, c0=c0: e.dma_start(out=xdst[r0:r0 + 128, c0:c0 + 256], in_=xs_t[xi][:, 0:256]),
                                 reads=[("xs", xi)], writes=[xdst.tensor.name])

        xcur = x_in
        for l in range(DEPTH):
            phase_A(l, xcur)
            s.barrier()
            phase_B(l)
            s.barrier()
            phase_C(l, xcur)
            s.barrier()
            phase_D(l, xbuf[l % 2])
            s.barrier()
            xcur = xbuf[l % 2]
        load_gain(final_g[0:1, :])
        for sb in range(NSB):
            norm_tiles(xcur, sb, to_hT=False, dst_dram=out)
        s.finalize()
    return nc, s


_CACHE = {}


def kernel(**inputs):
    x = np.ascontiguousarray(np.asarray(inputs["x"], dtype=np.float32))
    B, S, _ = x.shape
    depth = int(np.asarray(inputs["ln1_g"]).shape[0])
    key = (S, depth)
    if key not in _CACHE:
        _CACHE[key] = build(S, depth)[0]
    nc = _CACHE[key]
    consts = host_constants(S)
    shared = {}
    for k, v in inputs.items():
        if k == "x":
            continue
        a = np.ascontiguousarray(np.asarray(v, dtype=np.float32))
        if k == "final_g":
            a = a.reshape(1, -1)
        shared[k] = a
    shared.update(consts)
    in_maps = []
    for c in range(8):
        m = dict(shared)
        m["x"] = x[c % B]
        in_maps.append(m)
    res = run_bass_kernel_spmd(nc, in_maps, core_ids=list(range(8)))
    return np.stack([np.asarray(res.results[b]["out"], dtype=np.float32) for b in range(B)], axis=0)
```

```python
import contextlib
import math
import numpy as np
import ml_dtypes
import concourse.bass as bass
import concourse.mybir as mybir
from concourse.bass_utils import run_bass_kernel_spmd

F32 = mybir.dt.float32
BF16 = mybir.dt.bfloat16
AF = mybir.ActivationFunctionType
ALU = mybir.AluOpType
AX = mybir.AxisListType

COMPUTE = ("pe", "act", "dve", "pool")
DMAQ = ("sp", "pq")
NSEM_DMA = 8
NEGB = -30000.0
D = 1024
DIN = 8968
DFF = 2816
EPS = 1e-6


class Sched:
    def __init__(self, nc):
        self.nc = nc
        self.ins = []

    def op(self, eng, fn, reads=(), writes=()):
        self.ins.append(dict(eng=eng, fn=fn, reads=tuple(reads), writes=tuple(writes)))

    def barrier(self):
        self.ins.append(dict(eng="bar", fn=None, reads=(), writes=()))

    def finalize(self):
        nc = self.nc
        ins = self.ins
        last_writer = {}
        readers = {}
        real = []
        pend = None
        tails = {}
        seen_after = set()
        for it in ins:
            if it["eng"] == "bar":
                pend = set()
                for v in tails.values():
                    pend.update(v)
                seen_after = set()
                continue
            st = "pool" if it["eng"] == "pq" else it["eng"]
            it["bar_deps"] = set()
            if pend is not None and st not in seen_after:
                it["bar_deps"] = set(pend)
                seen_after.add(st)
            idx = len(real)
            real.append(it)
            if it["eng"] in DMAQ:
                tails.setdefault(it["eng"], []).append(idx)
                tails[it["eng"]] = tails[it["eng"]][-NSEM_DMA:]
            else:
                tails[it["eng"]] = [idx]
        ins = self.ins = real
        for i, it in enumerate(ins):
            deps = set(it["bar_deps"])
            for k in it["reads"]:
                if k in last_writer:
                    deps.add(last_writer[k])
            for k in it["writes"]:
                if k in last_writer:
                    deps.add(last_writer[k])
                deps.update(readers.get(k, ()))
            deps.discard(i)
            for k in it["reads"]:
                readers.setdefault(k, []).append(i)
            for k in it["writes"]:
                last_writer[k] = i
                readers[k] = []
            deps = set(j for j in deps if not (ins[j]["eng"] == "pe" and it["eng"] == "pe"))
            best = {}
            keep = set()
            for j in deps:
                ej = ins[j]["eng"]
                if ej in DMAQ:
                    keep.add(j)
                else:
                    best[ej] = max(best.get(ej, -1), j)
            keep.update(best.values())
            it["deps"] = keep

        def stream(e):
            return "pool" if e == "pq" else e

        signal = [False] * len(ins)
        for it in ins:
            for j in it["deps"]:
                signal[j] = True
        cnt = {e: 0 for e in COMPUTE}
        dcnt = {q: 0 for q in DMAQ}
        for i, it in enumerate(ins):
            e = it["eng"]
            if e in DMAQ:
                it["dma_k"] = dcnt[e]
                dcnt[e] += 1
            elif signal[i]:
                cnt[e] += 1
                it["sig"] = cnt[e]
        self.stats = dict(n=len(ins), sig=dict(cnt), dma=dict(dcnt))
        with contextlib.ExitStack() as es:
            sems = {e: es.enter_context(nc.semaphore("s_" + e)) for e in COMPUTE}
            dsems = {q: [es.enter_context(nc.semaphore("d_%s%d" % (q, n))) for n in range(NSEM_DMA)]
                     for q in DMAQ}
            block = es.enter_context(nc.Block())
            streams = {"pe": [], "act": [], "dve": [], "pool": [], "sp": []}
            for i, it in enumerate(ins):
                streams[stream(it["eng"])].append(i)

            def emit_stream(sname, eobj):
                waited = {}

                def wait(key, sem, val):
                    if waited.get(key, 0) >= val:
                        return
                    waited[key] = val
                    eobj.wait_ge(sem, val)

                for i in streams[sname]:
                    it = ins[i]
                    for j in sorted(it["deps"]):
                        jt = ins[j]
                        if jt["eng"] in DMAQ:
                            k = jt["dma_k"]
                            wait((jt["eng"], k % NSEM_DMA), dsems[jt["eng"]][k % NSEM_DMA],
                                 16 * (k // NSEM_DMA + 1))
                        else:
                            wait(jt["eng"], sems[jt["eng"]], jt["sig"])
                    if it["eng"] in DMAQ:
                        q = it["eng"]
                        k = it["dma_k"]
                        if k >= NSEM_DMA:
                            wait((q, k % NSEM_DMA), dsems[q][k % NSEM_DMA], 16 * (k // NSEM_DMA))
                        it["fn"](eobj).then_inc(dsems[q][k % NSEM_DMA], 16)
                    else:
                        h = it["fn"](eobj)
                        if "sig" in it:
                            h.then_inc(sems[it["eng"]], 1)
                for q in DMAQ:
                    if stream(q) != sname:
                        continue
                    n = dcnt[q]
                    for s_ in range(min(n, NSEM_DMA)):
                        kmax = ((n - 1 - s_) // NSEM_DMA) * NSEM_DMA + s_
                        wait((q, s_), dsems[q][s_], 16 * (kmax // NSEM_DMA + 1))

            @block.tensor
            def _(e):
                emit_stream("pe", e)

            @block.scalar
            def _(e):
                emit_stream("act", e)

            @block.vector
            def _(e):
                emit_stream("dve", e)

            @block.gpsimd
            def _(e):
                emit_stream("pool", e)

            @block.sync
            def _(e):
                emit_stream("sp", e)


class Rot:
    def __init__(self, items):
        self.items = list(items)
        self.i = 0

    def next(self):
        it = self.items[self.i % len(self.items)]
        self.i += 1
        return it


def rel_bucket_np(d):
    n = np.maximum(d, 0)
    nf = np.maximum(n, 1).astype(np.float32)
    large = 16 + (np.log(nf / np.float32(16)) / np.float32(math.log(128 / 16)) * np.float32(16)).astype(np.int32)
    large = np.minimum(large, 31)
    return np.where(n < 16, n, large)


def host_constants(S):
    NT, NBLK = S // 128, S // 256
    L = 1152
    dist = np.arange(L) - 511
    onehot = np.zeros((32, L), np.float32)
    bk = rel_bucket_np(dist)
    for dd in range(L):
        if dist[dd] >= 0:
            onehot[bk[dd], dd] = 1.0
    mmul = np.zeros((17, L), np.float32)
    madd = np.zeros((17, L), np.float32)
    swa_ok = (dist >= 0) & (dist < 128)
    cau_ok = dist >= 0
    mmul[0:8] = swa_ok
    madd[0:8] = np.where(swa_ok, 0.0, NEGB)
    mmul[8:16] = cau_ok
    madd[8:16] = np.where(cau_ok, 0.0, NEGB)
    madd[16] = np.where(cau_ok, 0.0, NEGB)
    ismoba = np.zeros((17, 1), np.float32)
    ismoba[8:16] = 1.0
    own = (np.arange(NT) // 2)[:, None]
    nn = np.arange(NBLK)[None, :]
    vmask = np.where(nn < own, 0.0, -1e30).astype(np.float32).reshape(1, NT * NBLK)
    omask = np.where(nn == own, 1.0, 0.0).astype(np.float32).reshape(1, NT * NBLK)
    return dict(c_onehot=onehot, c_mmul=mmul, c_madd=madd, c_ismoba=ismoba, c_vmask=vmask, c_omask=omask)


def build(S, DEPTH, dbg=False):
    nc = bass.Bass("TRN2", target_bir_lowering=False)
    NT, NB, NBLK = S // 128, S // 512, S // 256
    SBK = 1024
    NSB = S // SBK
    TPS = SBK // 128
    BPS = SBK // 512

    def din(name, shape, dt=F32):
        return nc.dram_tensor(name, list(shape), dt, kind="ExternalInput").ap()

    def dscr(name, shape, dt):
        return nc.dram_tensor(name, list(shape), dt, kind=("ExternalOutput" if dbg else "Internal")).ap()

    x_in = din("x", [S, D])
    ln1_g = din("ln1_g", [DEPTH, D]); w_in = din("w_in", [DEPTH, D, DIN])
    b_gate = din("b_gate", [DEPTH, 4, D]); b_fgate = din("b_fgate", [DEPTH, 8]); sinks = din("sinks", [DEPTH, 8])
    conv_w = din("conv_w", [DEPTH, 31, 512]); conv_b = din("conv_b", [DEPTH, 512])
    conv_ln_g = din("conv_ln_g", [DEPTH, 512]); conv_ln_b = din("conv_ln_b", [DEPTH, 512])
    w_br = din("w_br", [DEPTH, 4, 512, D]); w_out = din("w_out", [DEPTH, D, D]); ln2_g = din("ln2_g", [DEPTH, D])
    w_up = din("w_up", [DEPTH, D, 2 * DFF]); ffn_conv_w = din("ffn_conv_w", [DEPTH, 3, 2 * DFF])
    ffn_conv_b = din("ffn_conv_b", [DEPTH, 2 * DFF]); w_down = din("w_down", [DEPTH, DFF, D])
    rel_bias = din("rel_bias", [32, 16]); final_g = din("final_g", [1, D])
    c_onehot = din("c_onehot", [32, 1152]); c_mmul = din("c_mmul", [17, 1152]); c_madd = din("c_madd", [17, 1152])
    c_ismoba = din("c_ismoba", [17, 1]); c_vmask = din("c_vmask", [1, NT * NBLK]); c_omask = din("c_omask", [1, NT * NBLK])
    out = nc.dram_tensor("out", [S, D], F32, kind="ExternalOutput").ap()

    qTa = dscr("qTa", [512, S], BF16); kTa = dscr("kTa", [128, S], BF16); va = dscr("va", [S, 128], BF16)
    uTb = dscr("uTb", [512, S], F32); cvT = dscr("cvT", [512, S], F32)
    qTc = dscr("qTc", [512, S], BF16); kTc = dscr("kTc", [512, S], BF16); vc = dscr("vc", [S, 512], BF16)
    fTc = dscr("fTc", [8, S], F32)
    qTd = dscr("qTd", [512, S], BF16); kTd = dscr("kTd", [512, S], BF16); vd = dscr("vd", [S, 512], BF16)
    brT = dscr("brT", [4, 512, S], BF16)
    xmid = dscr("xmid", [S, D], F32)
    xbuf = [dscr("xbuf0", [S, D], F32), dscr("xbuf1", [S, D], F32)]
    Fd = dscr("Fd", [17, 1152], BF16)

    s = Sched(nc)
    es = contextlib.ExitStack()
    with es:
        def T(name, shape, dt):
            return es.enter_context(nc.sbuf_tensor(name, list(shape), dt))

        banks = [es.enter_context(nc.psum_tensor("bank%d" % i, [128, 512], F32)) for i in range(8)]
        bankrot = Rot(range(8))
        ARENA = 44544
        arena = T("arena", [128, ARENA], F32)

        class Carver:
            def __init__(self):
                self.off = 0

            def get(self, P, n, dt):
                nb = n * (2 if dt == BF16 else 4)
                nb = (nb + 63) // 64 * 64
                a, b_ = self.off // 4, (self.off + nb) // 4
                self.off += nb
                assert self.off <= ARENA * 4, (self.off, ARENA * 4)
                v = arena[0:P, a:b_]
                if dt == BF16:
                    v = v.bitcast(BF16)
                return v[:, 0:n]

        cd = Carver()
        ca = Carver()

        identb = T("identb", [128, 128], BF16)
        antib = T("antib", [128, 128], BF16)
        identf = T("identf", [128, 128], F32)
        onesm = T("onesm", [128, 128], F32)
        onesf = T("onesf", [128, 128], F32)
        e127 = T("e127", [128, 128], F32)
        lstrict = T("lstrict", [128, 128], F32)
        for (t, nm, val) in ((identb, "identb", 1.0), (identf, "identf", 1.0), (e127, "e127", 1.0), (lstrict, "lstrict", 1.0)):
            s.op("pool", lambda e, t=t, val=val: e.memset(t[:], val), writes=[nm])
        s.op("pool", lambda e: e.memset(onesm[:], 1.0 / 512), writes=["onesm"])
        s.op("pool", lambda e: e.memset(antib[:], 1.0), writes=["antib"])
        s.op("pool", lambda e: e.affine_select(out=antib[:], in_=antib[:], pattern=[[1, 128]], compare_op=ALU.is_equal,
                                               fill=0.0, base=-127, channel_multiplier=1), reads=["antib"], writes=["antib"])
        s.op("pool", lambda e: e.memset(onesf[:], 1.0), writes=["onesf"])
        s.op("pool", lambda e: e.affine_select(out=identb[:], in_=identb[:], pattern=[[-1, 128]], compare_op=ALU.is_equal,
                                               fill=0.0, base=0, channel_multiplier=1), reads=["identb"], writes=["identb"])
        s.op("pool", lambda e: e.affine_select(out=identf[:], in_=identf[:], pattern=[[-1, 128]], compare_op=ALU.is_equal,
                                               fill=0.0, base=0, channel_multiplier=1), reads=["identf"], writes=["identf"])
        s.op("pool", lambda e: e.affine_select(out=e127[:], in_=e127[:], pattern=[[0, 128]], compare_op=ALU.is_ge,
                                               fill=0.0, base=-127, channel_multiplier=1), reads=["e127"], writes=["e127"])
        s.op("pool", lambda e: e.affine_select(out=lstrict[:], in_=lstrict[:], pattern=[[1, 128]], compare_op=ALU.is_ge,
                                               fill=0.0, base=-1, channel_multiplier=-1), reads=["lstrict"], writes=["lstrict"])

        def transpose_rows(dst, dst_key, src_ap, R, eng="sp"):
            stg = ldstage_rot.next()
            b = bankrot.next()
            s.op(eng, lambda e: e.dma_start(out=stg[0:R, :], in_=src_ap), writes=[("ldstage", id(stg))])
            s.op("pe", lambda e: e.transpose(out=banks[b][:, 0:R], in_=stg[0:R, :], identity=identf[0:R, 0:R]),
                 reads=[("ldstage", id(stg)), "identf"], writes=[("bank", b)])
            s.op("dve", lambda e: e.tensor_copy(out=dst, in_=banks[b][:, 0:R]), reads=[("bank", b)], writes=[dst_key])

        ldstage = [T("ldstage%d" % i, [128, 128], F32) for i in range(2)]
        ldstage_rot = Rot(ldstage)

        cs = Carver()
        relb = T("relb", [32, 17], F32)
        oneh = cs.get(32, 1152, F32)
        mmul = cs.get(17, 1152, F32)
        madd = cs.get(17, 1152, F32)
        sub31 = T("sub31", [17, 2], F32)
        ftmp = cs.get(17, 1152, F32)
        fbf = cs.get(17, 1152, BF16)
        s.op("pool", lambda e: e.memset(relb[:], 0.0), writes=["relb"])
        s.op("pool", lambda e: e.memset(sub31[:], 0.0), writes=["sub31"])
        s.op("sp", lambda e: e.dma_start(out=relb[:, 0:16], in_=rel_bias), reads=["relb"], writes=["relb"])
        s.op("sp", lambda e: e.dma_start(out=oneh[:], in_=c_onehot), writes=["oneh"])
        s.op("sp", lambda e: e.dma_start(out=mmul[:], in_=c_mmul), writes=["mmul"])
        s.op("sp", lambda e: e.dma_start(out=madd[:], in_=c_madd), writes=["madd"])
        s.op("sp", lambda e: e.dma_start(out=sub31[0:16, 0:1], in_=rel_bias[31:32, :].rearrange("o h -> h o")),
             reads=["sub31"], writes=["sub31"])
        s.op("sp", lambda e: e.dma_start(out=sub31[:, 1:2], in_=c_ismoba), reads=["sub31"], writes=["sub31"])
        s.op("dve", lambda e: e.tensor_tensor(out=sub31[:, 0:1], in0=sub31[:, 0:1], in1=sub31[:, 1:2], op=ALU.mult),
             reads=["sub31"], writes=["sub31"])
        for c0 in (0, 512, 1024):
            wdt = min(512, 1152 - c0)
            b = bankrot.next()
            s.op("pe", lambda e, b=b, c0=c0, wdt=wdt: e.matmul(banks[b][0:17, 0:wdt], lhsT=relb[:, :], rhs=oneh[:, c0:c0 + wdt],
                                                               start=True, stop=True), reads=["relb", "oneh"], writes=[("bank", b)])
            s.op("dve", lambda e, b=b, c0=c0, wdt=wdt: e.tensor_scalar(out=ftmp[:, c0:c0 + wdt], in0=banks[b][0:17, 0:wdt],
                                                                      scalar1=sub31[:, 0:1], scalar2=None, op0=ALU.subtract),
                 reads=[("bank", b), "sub31"], writes=["ftmp"])
        s.op("dve", lambda e: e.tensor_tensor(out=ftmp[:], in0=ftmp[:], in1=mmul[:], op=ALU.mult), reads=["ftmp", "mmul"], writes=["ftmp"])
        s.op("dve", lambda e: e.tensor_tensor(out=fbf[:], in0=ftmp[:], in1=madd[:], op=ALU.add), reads=["ftmp", "madd"], writes=["fbf"])
        s.op("sp", lambda e: e.dma_start(out=Fd, in_=fbf[:]), reads=["fbf"], writes=["Fd"])
        s.barrier()

        gbc = T("gbc", [128, D], F32)
        ssm = T("ssm", [128, 8], F32)
        colv = T("colv", [128, 512], F32)
        xs_t = [cd.get(128, D, F32) for i in range(2)]
        xs_rot = Rot(range(2))
        junk = cd.get(128, D, F32)
        hb_t = [cd.get(128, D, BF16) for i in range(2)]
        hb_rot = Rot(range(2))
        hT = cd.get(128, 8 * (2 + SBK), BF16).rearrange("p (a b) -> p a b", a=8)
        wbuf = [cd.get(128, 16 * 256, BF16) for i in range(5)]
        wrot = Rot(range(5))
        stg = [cd.get(128, 1024, F32) for i in range(3)]
        stgrot = Rot(range(3))
        big = cd.get(128, 22 * SBK, BF16)
        big2 = cd.get(128, 8 * SBK, BF16)
        rec = cd.get(128, 512, F32)
        osb2 = cd.get(128, 512, F32)
        gsig = [cd.get(128, 512, F32) for i in range(2)]
        gsrot = Rot(range(2))
        ug = cd.get(128, 2 + SBK, F32); uv = cd.get(128, 2 + SBK, F32)
        tg = cd.get(128, SBK, F32); tv = cd.get(128, SBK, F32)
        fcw = T("fcw", [128, 4 * 44], F32)

        def load_gain(g_ap):
            s.op("sp", lambda e: e.dma_start(out=gbc[:], in_=g_ap.partition_broadcast(128)), writes=["gbc"])

        def norm_tiles(src, sb, to_hT=True, dst_dram=None):
            for i in range(TPS):
                xi = xs_rot.next(); hi = hb_rot.next()
                r0 = sb * SBK + i * 128
                s.op("sp", lambda e, xi=xi, r0=r0: e.dma_start(out=xs_t[xi][:], in_=src[r0:r0 + 128, :]), writes=[("xs", xi)])
                s.op("act", lambda e, xi=xi: e.activation(out=junk[:], in_=xs_t[xi][:], func=AF.Square, accum_out=ssm[:, 0:1]),
                     reads=[("xs", xi)], writes=["junk", "ssm"])
                s.op("act", lambda e: e.activation(out=ssm[:, 1:2], in_=ssm[:, 0:1], func=AF.Sqrt, scale=1.0 / D, bias=EPS),
                     reads=["ssm"], writes=["ssm"])
                s.op("dve", lambda e: e.reciprocal(out=ssm[:, 2:3], in_=ssm[:, 1:2]), reads=["ssm"], writes=["ssm"])
                if not to_hT:
                    s.op("dve", lambda e, xi=xi: e.scalar_tensor_tensor(out=junk[:], in0=xs_t[xi][:], scalar=ssm[:, 2:3], in1=gbc[:],
                                                                        op0=ALU.mult, op1=ALU.mult),
                         reads=[("xs", xi), "ssm", "gbc", "junk"], writes=["junk"])
                    s.op("sp", lambda e, r0=r0: e.dma_start(out=dst_dram[r0:r0 + 128, :], in_=junk[:]), reads=["junk"])
                    continue
                s.op("dve", lambda e, xi=xi, hi=hi: e.scalar_tensor_tensor(out=hb_t[hi][:], in0=xs_t[xi][:], scalar=ssm[:, 2:3], in1=gbc[:],
                                                                           op0=ALU.mult, op1=ALU.mult),
                     reads=[("xs", xi), "ssm", "gbc"], writes=[("hb", hi)])
                for half in range(2):
                    b = bankrot.next()
                    pb = banks[b][:].bitcast(BF16)
                    for q in range(4):
                        dc = half * 4 + q
                        s.op("pe", lambda e, pb=pb, q=q, dc=dc, hi=hi: e.transpose(out=pb[:, q * 128:(q + 1) * 128],
                                                                                  in_=hb_t[hi][:, dc * 128:(dc + 1) * 128], identity=identb[:]),
                             reads=[("hb", hi), "identb"], writes=[("bank", b)])
                    eng = "act" if half == 0 else "dve"
                    dstv = hT[:, half * 4:half * 4 + 4, 2 + i * 128:2 + (i + 1) * 128]
                    srcv = pb[:, 0:512].rearrange("p (q c) -> p q c", q=4)
                    if eng == "act":
                        s.op("act", lambda e, dstv=dstv, srcv=srcv: e.copy(out=dstv, in_=srcv), reads=[("bank", b)], writes=[("hT", i)])
                    else:
                        s.op("dve", lambda e, dstv=dstv, srcv=srcv: e.tensor_copy(out=dstv, in_=srcv), reads=[("bank", b)], writes=[("hT", i)])

        hT_all = [("hT", i) for i in range(TPS)]

        def wload(wi, src3, ncols, nk):
            v = wbuf[wi][:, 0:nk * ncols].rearrange("p (k c) -> p k c", k=nk)
            s.op("pq", lambda e: e.dma_start(out=v, in_=src3), writes=[("w", wi)])
            return v

        def phase_A(l, xsrc):
            load_gain(ln1_g[l:l + 1, :])
            wv = w_in[l].rearrange("(dc p) c -> p dc c", p=128)
            segs = [(0, 512, qTa, 0.125), (512, 128, kTa, 1.0), (1792, 512, qTc, 0.125), (2304, 512, kTc, 1.0),
                    (3336, 512, qTd, 0.125), (3848, 512, kTd, 1.0)]
            for sb in range(NSB):
                norm_tiles(xsrc, sb)
                t0 = sb * SBK
                for (c0, wd, dst, scl) in segs:
                    wi = wrot.next()
                    wt = wload(wi, wv[:, :, c0:c0 + wd], wd, 8)
                    for ct in range(wd // 128):
                        si = stgrot.next()
                        sv = stg[si][:].bitcast(BF16)
                        for tb in range(BPS):
                            b = bankrot.next()
                            for dc in range(8):
                                s.op("pe", lambda e, b=b, wt=wt, dc=dc, ct=ct, tb=tb: e.matmul(
                                    banks[b][:, :], lhsT=wt[:, dc, ct * 128:(ct + 1) * 128], rhs=hT[:, dc, 2 + tb * 512:2 + (tb + 1) * 512],
                                    start=(dc == 0), stop=(dc == 7)), reads=[("w", wi)] + hT_all, writes=[("bank", b)])
                            s.op("act", lambda e, b=b, sv=sv, tb=tb, scl=scl: e.activation(out=sv[:, tb * 512:(tb + 1) * 512], in_=banks[b][:, :],
                                                                                             func=AF.Copy, scale=scl),
                                 reads=[("bank", b)], writes=[("stg", si)])
                        s.op("sp", lambda e, dst=dst, ct=ct, sv=sv, t0=t0: e.dma_start(out=dst[ct * 128:(ct + 1) * 128, t0:t0 + SBK], in_=sv[:, 0:SBK]),
                             reads=[("stg", si)], writes=[(id(dst.tensor) if False else dst.tensor.name)])
                wi = wrot.next()
                wt = wload(wi, wv[:, :, 3328:3336], 8, 8)
                si = stgrot.next()
                for tb in range(BPS):
                    b = bankrot.next()
                    for dc in range(8):
                        s.op("pe", lambda e, b=b, wt=wt, dc=dc, tb=tb: e.matmul(banks[b][0:8, :], lhsT=wt[:, dc, 0:8],
                                                                               rhs=hT[:, dc, 2 + tb * 512:2 + (tb + 1) * 512], start=(dc == 0), stop=(dc == 7)),
                             reads=[("w", wi)] + hT_all, writes=[("bank", b)])
                    s.op("act", lambda e, b=b, si=si, tb=tb: e.copy(out=stg[si][0:8, tb * 512:(tb + 1) * 512], in_=banks[b][0:8, :]),
                         reads=[("bank", b)], writes=[("stg", si)])
                s.op("sp", lambda e, si=si, t0=t0: e.dma_start(out=fTc[:, t0:t0 + SBK], in_=stg[si][0:8, 0:SBK]), reads=[("stg", si)], writes=["fTc"])
                wia = wrot.next(); wta = wload(wia, wv[:, :, 768:1280], 512, 8)
                wig = wrot.next(); wtg = wload(wig, wv[:, :, 1280:1792], 512, 8)
                for ct in range(4):
                    si = stgrot.next()
                    for tb in range(BPS):
                        ba = bankrot.next(); bg = bankrot.next()
                        for (bb, wt_, wi_) in ((ba, wta, wia), (bg, wtg, wig)):
                            for dc in range(8):
                                s.op("pe", lambda e, bb=bb, wt_=wt_, dc=dc, ct=ct, tb=tb: e.matmul(
                                    banks[bb][:, :], lhsT=wt_[:, dc, ct * 128:(ct + 1) * 128], rhs=hT[:, dc, 2 + tb * 512:2 + (tb + 1) * 512],
                                    start=(dc == 0), stop=(dc == 7)), reads=[("w", wi_)] + hT_all, writes=[("bank", bb)])
                        sj = stgrot.next()
                        s.op("act", lambda e, bg=bg, sj=sj: e.activation(out=stg[sj][:, 0:512], in_=banks[bg][:, :], func=AF.Sigmoid),
                             reads=[("bank", bg)], writes=[("stg", sj)])
                        s.op("dve", lambda e, ba=ba, sj=sj, si=si, tb=tb: e.tensor_tensor(out=stg[si][:, tb * 512:(tb + 1) * 512],
                                                                                         in0=banks[ba][:, :], in1=stg[sj][:, 0:512], op=ALU.mult),
                             reads=[("bank", ba), ("stg", sj)], writes=[("stg", si)])
                    s.op("sp", lambda e, si=si, ct=ct, t0=t0: e.dma_start(out=uTb[ct * 128:(ct + 1) * 128, t0:t0 + SBK], in_=stg[si][:, 0:SBK]),
                         reads=[("stg", si)], writes=["uTb"])
                for (c0, wd, dst) in ((640, 128, va), (2816, 512, vc), (4360, 512, vd)):
                    wi = wrot.next()
                    wt = wload(wi, wv[:, :, c0:c0 + wd], wd, 8)
                    for i in range(TPS):
                        b = bankrot.next(); si = stgrot.next()
                        sv = stg[si][:].bitcast(BF16)
                        for dc in range(8):
                            s.op("pe", lambda e, b=b, wt=wt, dc=dc, i=i, wd=wd: e.matmul(banks[b][:, 0:wd], lhsT=hT[:, dc, 2 + i * 128:2 + (i + 1) * 128],
                                                                                       rhs=wt[:, dc, 0:wd], start=(dc == 0), stop=(dc == 7)),
                                 reads=[("w", wi), ("hT", i)], writes=[("bank", b)])
                        s.op("dve", lambda e, b=b, sv=sv, wd=wd: e.tensor_copy(out=sv[:, 0:wd], in_=banks[b][:, 0:wd]), reads=[("bank", b)], writes=[("stg", si)])
                        r0 = t0 + i * 128
                        s.op("sp", lambda e, dst=dst, sv=sv, wd=wd, r0=r0: e.dma_start(out=dst[r0:r0 + 128, :], in_=sv[:, 0:wd]),
                             reads=[("stg", si)], writes=[dst.tensor.name])

        QT = [ca.get(96, S, BF16) for i in range(2)]
        KT = [ca.get(96, S, BF16) for i in range(2)]
        VA = [ca.get(128, NT * 65, BF16).rearrange("p (a b) -> p a b", b=65) for i in range(2)]
        TAB = [ca.get(128, 1024, BF16) for i in range(2)]
        PT = [ca.get(128, 512, BF16) for i in range(3)]
        RES = [ca.get(64, 512, BF16) for i in range(2)]
        hrot = Rot(range(2)); ptrot = Rot(range(3)); resrot = Rot(range(2))
        arec = ca.get(128, 512, F32)
        osb = ca.get(64, 512, F32)
        esink = T("esink", [128, 8], F32)
        bfb = T("bfb", [128, 8], F32)
        fx = ca.get(NT, 128, F32); fl = ca.get(NT, 128, F32); fcs = ca.get(NT, 128, F32)
        cT = ca.get(128, NT, F32); cref = ca.get(128, NB, F32); cbq = ca.get(128, NT, F32)
        foff = T("foff", [NT, 2], F32)
        vmask = T("vmask", [128, NT * NBLK], BF16); omask = T("omask", [128, NT * NBLK], BF16)
        GW = min(NT, 512 // NBLK, 16)
        gm = ca.get(128, GW * NBLK, F32); sel = ca.get(128, GW * NBLK, F32)
        selb = ca.get(128, GW * 96, BF16).rearrange("p (a b) -> p a b", b=96)
        mx8 = T("mx8", [128, 8], F32)
        kmean = ca.get(64, NBLK, F32); kmh = ca.get(64, NBLK, BF16); kml = ca.get(64, NBLK, BF16)
        CH = min(S, 4096)
        ubuf = ca.get(128, 30 + CH, F32); cacc = ca.get(128, CH, F32)
        cwT = T("cwT", [128, 124], F32)
        amsk = ca.get(128, NT * NBLK, F32)

        s.op("sp", lambda e: e.dma_start(out=amsk[:], in_=c_vmask.partition_broadcast(128)), writes=["amsk"])
        s.op("dve", lambda e: e.tensor_copy(out=vmask[:], in_=amsk[:]), reads=["amsk"], writes=["vmask"])
        s.op("sp", lambda e: e.dma_start(out=amsk[:], in_=c_omask.partition_broadcast(128)), reads=["amsk"], writes=["amsk"])
        s.op("dve", lambda e: e.tensor_copy(out=omask[:], in_=amsk[:]), reads=["amsk"], writes=["omask"])
        s.barrier()

        def attn_init():
            for i in range(2):
                s.op("pool", lambda e, i=i: e.memset(VA[i][:, :, 64:65], 1.0), writes=[("VA", i)])
                s.op("pool", lambda e, i=i: e.memset(KT[i][64:96, :], 1.0), writes=[("KT", i)])
                s.op("pool", lambda e, i=i: e.affine_select(out=KT[i][64:96, :], in_=KT[i][64:96, :], pattern=[[1, S]], compare_op=ALU.is_ge,
                                                           fill=0.0, base=0, channel_multiplier=-256), reads=[("KT", i)], writes=[("KT", i)])
                s.op("pool", lambda e, i=i: e.affine_select(out=KT[i][64:96, :], in_=KT[i][64:96, :], pattern=[[-1, S]], compare_op=ALU.is_ge,
                                                           fill=0.0, base=255, channel_multiplier=256), reads=[("KT", i)], writes=[("KT", i)])
            s.op("pool", lambda e: e.memset(selb[:], 0.0), writes=["selb"])

        cTs = [cT, ca.get(128, NT, F32)]
        crefs = [cref, ca.get(128, NB, F32)]
        cbqs = [cbq, ca.get(128, NT, F32)]
        cbqrot = Rot(range(2))
        sb_rot = Rot([0, 1, 2]); ob_rot = Rot([3, 4])

        def head_params(kind, h):
            qsrc, ksrc, vsrc = {0: (qTa, kTa, va), 2: (qTc, kTc, vc), 3: (qTd, kTd, vd)}[kind]
            kvh = h // 4 if kind == 0 else h
            KA = 96
            frow = {0: h, 2: 16, 3: 8 + h}[kind]
            return qsrc, ksrc, vsrc, kvh, KA, frow

        def attn_loads(kind, h, hi):
            qsrc, ksrc, vsrc, kvh, KA, frow = head_params(kind, h)
            s.op("sp", lambda e: e.dma_start(out=QT[hi][0:64, :], in_=qsrc[h * 64:(h + 1) * 64, :]), reads=[qsrc.tensor.name], writes=[("QT", hi)])
            s.op("sp", lambda e: e.dma_start(out=KT[hi][0:64, :], in_=ksrc[kvh * 64:(kvh + 1) * 64, :]), reads=[ksrc.tensor.name], writes=[("KT", hi)])
            s.op("sp", lambda e: e.dma_start(out=VA[hi][:, :, 0:64], in_=vsrc[:, kvh * 64:(kvh + 1) * 64].rearrange("(kt p) c -> p kt c", p=128)),
                 reads=[vsrc.tensor.name], writes=[("VA", hi)])
            tsrc = bass.AP(tensor=Fd.tensor, offset=frow * 1152, ap=[[1, 128], [1, 1024]])
            s.op("sp", lambda e: e.dma_start(out=TAB[hi][:], in_=tsrc), reads=["Fd"], writes=[("TAB", hi)])
            if kind != 3:
                s.op("pool", lambda e: e.memset(QT[hi][64:96, :], 0.0), writes=[("QT", hi)])
            if kind == 2:
                s.op("sp", lambda e: e.dma_start(out=fx[:], in_=fTc[h:h + 1, :].rearrange("o (kt j) -> (o kt) j", j=128)), reads=["fTc"], writes=["fx"])

        def attn_prep(kind, h, hi):
            cT_, cref_ = cTs[hi], crefs[hi]
            if kind == 2:
                s.op("act", lambda e: e.activation(out=fl[:], in_=fx[:], func=AF.Sigmoid, bias=bfb[0:NT, h:h + 1]), reads=["fx", "bfb"], writes=["fl"])
                s.op("act", lambda e: e.activation(out=fl[:], in_=fl[:], func=AF.Ln), reads=["fl"], writes=["fl"])
                s.op("dve", lambda e: e.tensor_tensor_scan(out=fcs[:], data0=onesf[0:NT, :], data1=fl[:], initial=0.0, op0=ALU.mult, op1=ALU.add),
                     reads=["fl", "onesf"], writes=["fcs"])
                b = bankrot2.next()
                s.op("pe", lambda e, b=b: e.matmul(banks[b][0:NT, 0:1], lhsT=lstrict[0:NT, 0:NT], rhs=fcs[:, 127:128], start=True, stop=True),
                     reads=["lstrict", "fcs"], writes=[("bank", b)])
                s.op("dve", lambda e, b=b: e.tensor_copy(out=foff[:, 0:1], in_=banks[b][0:NT, 0:1]), reads=[("bank", b)], writes=["foff"])
                s.op("dve", lambda e: e.tensor_scalar(out=fcs[:], in0=fcs[:], scalar1=foff[:, 0:1], scalar2=None, op0=ALU.add),
                     reads=["fcs", "foff"], writes=["fcs"])
                b2 = bankrot2.next()
                s.op("pe", lambda e, b2=b2: e.transpose(out=banks[b2][:, 0:NT], in_=fcs[:], identity=identf[0:NT, 0:NT]),
                     reads=["fcs", "identf"], writes=[("bank", b2)])
                s.op("dve", lambda e, b2=b2: e.tensor_copy(out=cT_[:], in_=banks[b2][:, 0:NT]), reads=[("bank", b2)], writes=[("cT", hi)])
                b3 = bankrot2.next()
                s.op("pe", lambda e, b3=b3: e.matmul(banks[b3][:, 0:NB], lhsT=e127[:], rhs=cT_[:, 3::4], start=True, stop=True),
                     reads=["e127", ("cT", hi)], writes=[("bank", b3)])
                s.op("dve", lambda e, b3=b3: e.tensor_copy(out=cref_[:], in_=banks[b3][:, 0:NB]), reads=[("bank", b3)], writes=[("cref", hi)])
            if kind == 3:
                s.op("dve", lambda e: e.tensor_reduce(out=kmean[:], in_=KT[hi][0:64, :].rearrange("p (n j) -> p n j", j=256), axis=AX.X, op=ALU.add),
                     reads=[("KT", hi)], writes=["kmean"])
                s.op("dve", lambda e: e.tensor_scalar(out=kmean[:], in0=kmean[:], scalar1=1.0 / 256, scalar2=None, op0=ALU.mult), reads=["kmean"], writes=["kmean"])
                s.op("dve", lambda e: e.tensor_copy(out=kmh[:], in_=kmean[:]), reads=["kmean"], writes=["kmh"])
                s.op("dve", lambda e: e.tensor_tensor(out=kml[:], in0=kmean[:], in1=kmh[:], op=ALU.subtract), reads=["kmean", "kmh"], writes=["kml"])
                for g0 in range(0, NT, GW):
                    b = bankrot2.next()
                    for ii in range(GW):
                        i = g0 + ii
                        s.op("pe", lambda e, b=b, ii=ii, i=i: e.matmul(banks[b][:, ii * NBLK:(ii + 1) * NBLK], lhsT=QT[hi][0:64, i * 128:(i + 1) * 128],
                                                                      rhs=kmh[:, :], start=True, stop=False), reads=[("QT", hi), "kmh"], writes=[("bank", b)])
                        s.op("pe", lambda e, b=b, ii=ii, i=i: e.matmul(banks[b][:, ii * NBLK:(ii + 1) * NBLK], lhsT=QT[hi][0:64, i * 128:(i + 1) * 128],
                                                                      rhs=kml[:, :], start=False, stop=True), reads=[("QT", hi), "kml"], writes=[("bank", b)])
                    s.op("dve", lambda e, b=b, g0=g0: e.tensor_tensor(out=gm[:], in0=banks[b][:, 0:GW * NBLK], in1=vmask[:, g0 * NBLK:(g0 + GW) * NBLK], op=ALU.add),
                         reads=[("bank", b), "vmask"], writes=["gm"])
                    for ii in range(GW):
                        s.op("dve", lambda e, ii=ii: e.max(out=mx8[:], in_=gm[:, ii * NBLK:(ii + 1) * NBLK]), reads=["gm"], writes=["mx8"])
                        s.op("dve", lambda e, ii=ii: e.tensor_scalar(out=sel[:, ii * NBLK:(ii + 1) * NBLK], in0=gm[:, ii * NBLK:(ii + 1) * NBLK],
                                                                    scalar1=mx8[:, 2:3], scalar2=None, op0=ALU.is_ge), reads=["gm", "mx8"], writes=["sel"])
                    s.op("dve", lambda e, g0=g0: e.tensor_tensor(out=sel[:], in0=sel[:], in1=omask[:, g0 * NBLK:(g0 + GW) * NBLK], op=ALU.max),
                         reads=["sel", "omask"], writes=["sel"])
                    s.op("dve", lambda e: e.tensor_scalar(out=selb[:, :, 64:64 + NBLK], in0=sel[:].rearrange("p (g n) -> p g n", n=NBLK),
                                                          scalar1=-NEGB, scalar2=NEGB, op0=ALU.mult, op1=ALU.add), reads=["sel"], writes=["selb"])
                    for i4 in range(0, GW, 4):
                        b2 = bankrot2.next()
                        pb = banks[b2][:].bitcast(BF16)
                        for q in range(4):
                            s.op("pe", lambda e, pb=pb, q=q, i4=i4: e.transpose(out=pb[0:96, q * 128:(q + 1) * 128], in_=selb[:, i4 + q, :], identity=identb[:]),
                                 reads=["selb", "identb"], writes=[("bank", b2)])
                        c0 = (g0 + i4) * 128
                        s.op("act", lambda e, pb=pb, c0=c0: e.copy(out=QT[hi][64:96, c0:c0 + 512], in_=pb[64:96, 0:512]), reads=[("bank", b2)], writes=[("QT", hi)])

        bankrot2 = Rot([5, 6])

        def attn_main(kind, h, hi, mid_fn=None, conv_it=None):
            qsrc, ksrc, vsrc, kvh, KA, frow = head_params(kind, h)
            cT_, cref_ = cTs[hi], crefs[hi]
            pending = [None]

            def fin2():
                ob, qb = pending[0]
                pending[0] = None
                s.op("pe", lambda e: e.matmul(banks[7][0:64, :], lhsT=onesf[64:65, 0:64], rhs=arec[64:65, :], start=True, stop=True),
                     reads=["onesf", "arec"], writes=[("bank", 7)])
                s.op("act", lambda e, ob=ob: e.copy(out=osb[:], in_=banks[ob][0:64, :]), reads=[("bank", ob)], writes=["osb"])
                ri = resrot.next()
                s.op("dve", lambda e, ri=ri: e.tensor_tensor(out=RES[ri][:], in0=osb[:], in1=banks[7][0:64, :], op=ALU.mult),
                     reads=["osb", ("bank", 7)], writes=[("RES", ri)])
                s.op("sp", lambda e, ri=ri, qb=qb: e.dma_start(out=brT[kind, h * 64:(h + 1) * 64, qb * 512:(qb + 1) * 512], in_=RES[ri][:]),
                     reads=[("RES", ri)], writes=["brT"])

            for qb in range(NB):
                kt_lo = max(0, 4 * qb - 1) if kind == 0 else 0
                kt_hi = 4 * qb + 3
                tiles = list(range(kt_lo, kt_hi + 1))
                cq = None
                if kind == 2:
                    cqi = cbqrot.next()
                    cq = cbqs[cqi]
                    s.op("dve", lambda e, qb=qb, cq=cq: e.tensor_scalar(out=cq[:], in0=cT_[:], scalar1=-1.0, scalar2=cref_[:, qb:qb + 1], op0=ALU.mult, op1=ALU.add),
                         reads=[("cT", hi), ("cref", hi)], writes=[("cbq", cqi)])
                ob = ob_rot.next()
                sbank = {}

                def issue_S(kt, qb=qb, sbank=sbank):
                    near = kt >= 4 * qb - 1
                    sbk_ = sb_rot.next()
                    sbank[kt] = sbk_
                    s.op("pe", lambda e, sbk_=sbk_, kt=kt, qb=qb, near=near: e.matmul(banks[sbk_][:, :], lhsT=KT[hi][0:KA, kt * 128:(kt + 1) * 128],
                                                                                    rhs=QT[hi][0:KA, qb * 512:(qb + 1) * 512], start=True, stop=not near),
                         reads=[("KT", hi), ("QT", hi)], writes=[("bank", sbk_)])
                    if near:
                        off = 512 * qb - 128 * kt + 384
                        s.op("pe", lambda e, sbk_=sbk_, off=off: e.matmul(banks[sbk_][:, :], lhsT=antib[:], rhs=TAB[hi][:, off:off + 512], start=False, stop=True),
                             reads=[("TAB", hi), "antib"], writes=[("bank", sbk_)])

                issue_S(tiles[0])
                issue_S(tiles[1])
                for idx, kt in enumerate(tiles):
                    sbk_ = sbank[kt]
                    pi = ptrot.next()
                    if kind == 2:
                        s.op("act", lambda e, sbk_=sbk_, pi=pi, kt=kt, cq=cq: e.activation(out=PT[pi][:], in_=banks[sbk_][:, :], func=AF.Exp, bias=cq[:, kt:kt + 1]),
                             reads=[("bank", sbk_), ("cbq", cqi)], writes=[("PT", pi)])
                    else:
                        s.op("act", lambda e, sbk_=sbk_, pi=pi: e.activation(out=PT[pi][:], in_=banks[sbk_][:, :], func=AF.Exp),
                             reads=[("bank", sbk_)], writes=[("PT", pi)])
                    if idx + 2 < len(tiles):
                        issue_S(tiles[idx + 2])
                    s.op("pe", lambda e, ob=ob, pi=pi, kt=kt, kt_lo=kt_lo, kt_hi=kt_hi: e.matmul(banks[ob][0:65, :], lhsT=VA[hi][:, kt, 0:65], rhs=PT[pi][:],
                                                                      start=(kt == kt_lo), stop=(kt == kt_hi)),
                         reads=[("VA", hi), ("PT", pi)], writes=[("bank", ob)])
                    if idx == 1 and pending[0] is not None:
                        fin2()
                if kind == 0:
                    s.op("dve", lambda e, ob=ob: e.tensor_scalar(out=arec[64:65, :], in0=banks[ob][64:65, :], scalar1=esink[64:65, h:h + 1], scalar2=None, op0=ALU.add),
                         reads=[("bank", ob), "esink"], writes=["arec"])
                    s.op("dve", lambda e: e.reciprocal(out=arec[64:65, :], in_=arec[64:65, :]), reads=["arec"], writes=["arec"])
                else:
                    s.op("dve", lambda e, ob=ob: e.reciprocal(out=arec[64:65, :], in_=banks[ob][64:65, :]), reads=[("bank", ob)], writes=["arec"])
                pending[0] = (ob, qb)
                if conv_it is not None and kind != 0:
                    next(conv_it, None)
                if mid_fn is not None and qb == NB // 2 - 1:
                    mid_fn()
            fin2()

        def conv_branch_gen(l):
            for cc in range(4):
                for c0 in range(0, S, CH):
                    if c0 > 0:
                        s.op("dve", lambda e: e.tensor_copy(out=ubuf[:, 0:30], in_=ubuf[:, CH:CH + 30]), reads=["ubuf"], writes=["ubuf"])
                    else:
                        s.op("dve", lambda e: e.memset(ubuf[:, 0:30], 0.0), reads=["ubuf"], writes=["ubuf"])
                    s.op("sp", lambda e, cc=cc, c0=c0: e.dma_start(out=ubuf[:, 30:30 + CH], in_=uTb[cc * 128:(cc + 1) * 128, c0:c0 + CH]),
                         reads=["uTb", "ubuf"], writes=["ubuf"])
                    s.op("dve", lambda e, cc=cc: e.tensor_scalar(out=cacc[:], in0=ubuf[:, 0:CH], scalar1=cwT[:, cc:cc + 1], scalar2=colv[:, cc:cc + 1],
                                                                op0=ALU.mult, op1=ALU.add), reads=["ubuf", "cwT", "colv"], writes=["cacc"])
                    yield
                    for k in range(1, 31):
                        s.op("dve", lambda e, cc=cc, k=k: e.scalar_tensor_tensor(out=cacc[:], in0=ubuf[:, k:k + CH], scalar=cwT[:, k * 4 + cc:k * 4 + cc + 1],
                                                                                in1=cacc[:], op0=ALU.mult, op1=ALU.add), reads=["ubuf", "cwT", "cacc"], writes=["cacc"])
                        if k < 30:
                            yield
                    s.op("sp", lambda e, cc=cc, c0=c0: e.dma_start(out=cvT[cc * 128:(cc + 1) * 128, c0:c0 + CH], in_=cacc[:]), reads=["cacc"], writes=["cvT"])
                    yield

        def phase_B(l):
            attn_init()
            s.op("sp", lambda e: e.dma_start(out=esink[:], in_=sinks[l:l + 1, :].partition_broadcast(128)), writes=["esink"])
            s.op("act", lambda e: e.activation(out=esink[:], in_=esink[:], func=AF.Exp), reads=["esink"], writes=["esink"])
            s.op("sp", lambda e: e.dma_start(out=bfb[:], in_=b_fgate[l:l + 1, :].partition_broadcast(128)), writes=["bfb"])
            transpose_rows(cwT[:, 0:124], "cwT", conv_w[l].rearrange("k (cc p) -> (k cc) p", p=128), 124)
            transpose_rows(colv[:, 0:4], "colv", conv_b[l:l + 1, :].rearrange("o (cc p) -> (o cc) p", p=128), 4)
            s.barrier()
            conv_it = conv_branch_gen(l)
            heads = [(2, h) for h in range(8)] + [(3, h) for h in range(8)] + [(0, h) for h in range(8)]
            his = [i % 2 for i in range(len(heads))]
            attn_loads(heads[0][0], heads[0][1], his[0])
            attn_prep(heads[0][0], heads[0][1], his[0])
            for i, (kind, h) in enumerate(heads):
                mid = None
                if i + 1 < len(heads):
                    nk, nh = heads[i + 1]
                    attn_loads(nk, nh, his[i + 1])
                    mid = (lambda nk=nk, nh=nh, nhi=his[i + 1]: attn_prep(nk, nh, nhi))
                attn_main(kind, h, his[i], mid_fn=mid, conv_it=conv_it)
            for _ in conv_it:
                pass

        def phase_C(l, xsrc):
            load_gain(ln1_g[l:l + 1, :])
            transpose_rows(colv[:, 0:32], "colv", b_gate[l].rearrange("n (dc p) -> (n dc) p", p=128), 32)
            transpose_rows(colv[:, 32:36], "colv", conv_ln_g[l:l + 1, :].rearrange("o (cc p) -> (o cc) p", p=128), 4)
            transpose_rows(colv[:, 36:40], "colv", conv_ln_b[l:l + 1, :].rearrange("o (cc p) -> (o cc) p", p=128), 4)
            wgv = w_in[l].rearrange("(dc p) c -> p dc c", p=128)
            wbv = w_br[l].rearrange("n (cc p) d -> p (n cc) d", p=128)
            wov = w_out[l].rearrange("(dc p) d -> p dc d", p=128)
            brv = big[:, 0:16 * SBK].rearrange("p (k t) -> p k t", k=16)
            mgv = big2[:].rearrange("p (k t) -> p k t", k=8)
            for sb in range(NSB):
                t0 = sb * SBK
                norm_tiles(xsrc, sb)
                for n in (0, 2, 3):
                    for cc in range(4):
                        s.op("sp", lambda e, n=n, cc=cc, t0=t0: e.dma_start(out=brv[:, n * 4 + cc, :], in_=brT[n, cc * 128:(cc + 1) * 128, t0:t0 + SBK]),
                             reads=["brT"], writes=[("br", n * 4 + cc)])
                for tb in range(BPS):
                    ys = []
                    for cc in range(4):
                        si = stgrot.next() if cc < 3 else None
                        ys.append(si)
                    ytile = [stg[ys[0]][:, 0:512], stg[ys[1]][:, 0:512], stg[ys[2]][:, 0:512], junk[:, 0:512]]
                    ykey = [("stg", ys[0]), ("stg", ys[1]), ("stg", ys[2]), "junk"]
                    sq = [stg[ys[0]][:, 512:1024], stg[ys[1]][:, 512:1024], stg[ys[2]][:, 512:1024], junk[:, 512:1024]]
                    bm = bankrot.next(); bq = bankrot.next()
                    for cc in range(4):
                        s.op("sp", lambda e, cc=cc, tb=tb, t0=t0, ytile=ytile: e.dma_start(out=ytile[cc], in_=cvT[cc * 128:(cc + 1) * 128, t0 + tb * 512:t0 + (tb + 1) * 512]),
                             reads=["cvT"], writes=[ykey[cc]])
                        s.op("act", lambda e, cc=cc, sq=sq, ytile=ytile: e.activation(out=sq[cc], in_=ytile[cc], func=AF.Square), reads=[ykey[cc]], writes=[ykey[cc]])
                    for cc in range(4):
                        s.op("pe", lambda e, cc=cc, bm=bm, ytile=ytile: e.matmul(banks[bm][:, :], lhsT=onesm[:], rhs=ytile[cc], start=(cc == 0), stop=(cc == 3)),
                             reads=["onesm", ykey[cc]], writes=[("bank", bm)])
                    for cc in range(4):
                        s.op("pe", lambda e, cc=cc, bq=bq, sq=sq: e.matmul(banks[bq][:, :], lhsT=onesm[:], rhs=sq[cc], start=(cc == 0), stop=(cc == 3)),
                             reads=["onesm", ykey[cc]], writes=[("bank", bq)])
                    s.op("act", lambda e, bm=bm: e.activation(out=rec[:], in_=banks[bm][:, :], func=AF.Square), reads=[("bank", bm)], writes=["rec"])
                    s.op("dve", lambda e, bq=bq: e.tensor_tensor(out=rec[:], in0=banks[bq][:, :], in1=rec[:], op=ALU.subtract), reads=[("bank", bq), "rec"], writes=["rec"])
                    s.op("act", lambda e: e.activation(out=rec[:], in_=rec[:], func=AF.Sqrt, bias=EPS), reads=["rec"], writes=["rec"])
                    s.op("dve", lambda e: e.reciprocal(out=rec[:], in_=rec[:]), reads=["rec"], writes=["rec"])
                    for cc in range(4):
                        s.op("dve", lambda e, cc=cc, bm=bm, ytile=ytile: e.tensor_tensor(out=ytile[cc], in0=ytile[cc], in1=banks[bm][:, :], op=ALU.subtract),
                             reads=[ykey[cc], ("bank", bm)], writes=[ykey[cc]])
                        s.op("dve", lambda e, cc=cc, ytile=ytile: e.tensor_tensor(out=ytile[cc], in0=ytile[cc], in1=rec[:], op=ALU.mult), reads=[ykey[cc], "rec"], writes=[ykey[cc]])
                        s.op("act", lambda e, cc=cc, tb=tb, ytile=ytile: e.activation(out=brv[:, 4 + cc, tb * 512:(tb + 1) * 512], in_=ytile[cc], func=AF.Silu,
                                                                        scale=colv[:, 32 + cc:33 + cc], bias=colv[:, 36 + cc:37 + cc]),
                             reads=[ykey[cc], "colv"], writes=[("br", 4 + cc)])
                for dg in range(4):
                    for pr in range(2):
                        wi = wrot.next()
                        wbt = wload(wi, wbv[:, pr * 8:pr * 8 + 8, dg * 256:(dg + 1) * 256], 256, 8)
                        wj = wrot.next()
                        c0 = 4872 + pr * 2048 + dg * 256
                        va_ = wbuf[wj][:, 0:16 * 256].rearrange("p (k c) -> p k c", k=16)
                        s.op("pq", lambda e, va_=va_, c0=c0: e.dma_start(out=va_[:, 0:8, :], in_=wgv[:, :, c0:c0 + 256]), writes=[("w", wj)])
                        s.op("pq", lambda e, va_=va_, c0=c0: e.dma_start(out=va_[:, 8:16, :], in_=wgv[:, :, c0 + 1024:c0 + 1280]), reads=[("w", wj)], writes=[("w", wj)])
                        for n in (2 * pr, 2 * pr + 1):
                            k0 = (n % 2) * 8
                            for dt in range(2):
                                dcol = dg * 2 + dt
                                for tb in range(BPS):
                                    bp = bankrot.next(); bg = bankrot.next()
                                    for cc in range(4):
                                        s.op("pe", lambda e, bp=bp, n=n, cc=cc, dt=dt, tb=tb, wbt=wbt: e.matmul(
                                            banks[bp][:, :], lhsT=wbt[:, (n % 2) * 4 + cc, dt * 128:(dt + 1) * 128], rhs=brv[:, n * 4 + cc, tb * 512:(tb + 1) * 512],
                                            start=(cc == 0), stop=(cc == 3)), reads=[("w", wi), ("br", n * 4 + cc)], writes=[("bank", bp)])
                                    for dc in range(8):
                                        s.op("pe", lambda e, bg=bg, dc=dc, dt=dt, tb=tb, va_=va_, k0=k0: e.matmul(
                                            banks[bg][:, :], lhsT=va_[:, k0 + dc, dt * 128:(dt + 1) * 128], rhs=hT[:, dc, 2 + tb * 512:2 + (tb + 1) * 512],
                                            start=(dc == 0), stop=(dc == 7)), reads=[("w", wj)] + hT_all, writes=[("bank", bg)])
                                    gs = gsrot.next()
                                    s.op("act", lambda e, bg=bg, n=n, dcol=dcol, gs=gs: e.activation(out=gsig[gs][:], in_=banks[bg][:, :], func=AF.Sigmoid,
                                                                                                     bias=colv[:, n * 8 + dcol:n * 8 + dcol + 1]),
                                         reads=[("bank", bg), "colv"], writes=[("gsig", gs)])
                                    mslice = mgv[:, dcol, tb * 512:(tb + 1) * 512]
                                    if n == 0:
                                        s.op("dve", lambda e, bp=bp, mslice=mslice, gs=gs: e.tensor_tensor(out=mslice, in0=banks[bp][:, :], in1=gsig[gs][:], op=ALU.mult),
                                             reads=[("bank", bp), ("gsig", gs)], writes=[("mg", dcol)])
                                    else:
                                        s.op("dve", lambda e, bp=bp, gs=gs: e.tensor_tensor(out=osb2[:], in0=banks[bp][:, :], in1=gsig[gs][:], op=ALU.mult),
                                             reads=[("bank", bp), ("gsig", gs)], writes=["osb2"])
                                        s.op("dve", lambda e, mslice=mslice: e.tensor_tensor(out=mslice, in0=mslice, in1=osb2[:], op=ALU.add),
                                             reads=["osb2", ("mg", dcol)], writes=[("mg", dcol)])
                for dh in range(2):
                    wi = wrot.next()
                    wot = wload(wi, wov[:, :, dh * 512:(dh + 1) * 512], 512, 8)

                    def ld(i, dh=dh):
                        xi = i % 2
                        r0 = t0 + i * 128
                        s.op("sp", lambda e, xi=xi, r0=r0, dh=dh: e.dma_start(out=xs_t[xi][:, 0:512], in_=xsrc[r0:r0 + 128, dh * 512:(dh + 1) * 512]), writes=[("xs", xi)])
                    ld(0)
                    for i in range(TPS):
                        if i + 1 < TPS:
                            ld(i + 1)
                        b = bankrot.next(); xi = i % 2
                        r0 = t0 + i * 128
                        for dc in range(8):
                            s.op("pe", lambda e, b=b, dc=dc, i=i, wot=wot: e.matmul(banks[b][:, :], lhsT=mgv[:, dc, i * 128:(i + 1) * 128], rhs=wot[:, dc, :],
                                                                                   start=(dc == 0), stop=(dc == 7)), reads=[("w", wi), ("mg", dc)], writes=[("bank", b)])
                        s.op("dve", lambda e, b=b, xi=xi: e.tensor_tensor(out=xs_t[xi][:, 0:512], in0=xs_t[xi][:, 0:512], in1=banks[b][:, :], op=ALU.add),
                             reads=[("bank", b), ("xs", xi)], writes=[("xs", xi)])
                        s.op("sp", lambda e, xi=xi, r0=r0, dh=dh: e.dma_start(out=xmid[r0:r0 + 128, dh * 512:(dh + 1) * 512], in_=xs_t[xi][:, 0:512]),
                             reads=[("xs", xi)], writes=["xmid"])

        def phase_D(l, xdst):
            load_gain(ln2_g[l:l + 1, :])
            for k in range(3):
                transpose_rows(fcw[:, k * 44:(k + 1) * 44], "fcw", ffn_conv_w[l, k:k + 1, :].rearrange("o (j p) -> (o j) p", p=128), 44)
            transpose_rows(fcw[:, 132:176], "fcw", ffn_conv_b[l:l + 1, :].rearrange("o (j p) -> (o j) p", p=128), 44)
            wuv = w_up[l].rearrange("(dc p) c -> p dc c", p=128)
            wdv = w_down[l].rearrange("(j p) d -> p j d", p=128)
            actv = big[:, 0:22 * SBK].rearrange("p (j t) -> p j t", j=22)
            for sb in range(NSB):
                t0 = sb * SBK
                if sb == 0:
                    s.op("dve", lambda e: e.memset(hT[:, :, 0:2], 0.0), reads=hT_all + ["hThalo"], writes=["hThalo"])
                else:
                    s.op("dve", lambda e: e.tensor_copy(out=hT[:, :, 0:2], in_=hT[:, :, SBK:SBK + 2]), reads=hT_all + ["hThalo"], writes=["hThalo"])
                norm_tiles(xmid, sb)
                for j in range(22):
                    wi = wrot.next()
                    wv2 = wbuf[wi][:, 0:16 * 128].rearrange("p (k c) -> p k c", k=16)
                    s.op("pq", lambda e, wv2=wv2, j=j: e.dma_start(out=wv2[:, 0:8, :], in_=wuv[:, :, j * 128:(j + 1) * 128]), writes=[("w", wi)])
                    s.op("pq", lambda e, wv2=wv2, j=j: e.dma_start(out=wv2[:, 8:16, :], in_=wuv[:, :, DFF + j * 128:DFF + (j + 1) * 128]), reads=[("w", wi)], writes=[("w", wi)])
                    for (gi, ub, ukey) in ((0, ug, "ug"), (1, uv, "uv")):
                        b = bankrot.next()
                        for dc in range(8):
                            s.op("pe", lambda e, b=b, dc=dc, gi=gi, wv2=wv2: e.matmul(banks[b][:, 0:2], lhsT=wv2[:, gi * 8 + dc, :], rhs=hT[:, dc, 0:2],
                                                                                     start=(dc == 0), stop=(dc == 7)), reads=[("w", wi), "hThalo"], writes=[("bank", b)])
                        s.op("act", lambda e, b=b, ub=ub: e.copy(out=ub[:, 0:2], in_=banks[b][:, 0:2]), reads=[("bank", b)], writes=[ukey])
                        for tb in range(BPS):
                            b = bankrot.next()
                            for dc in range(8):
                                s.op("pe", lambda e, b=b, dc=dc, gi=gi, tb=tb, wv2=wv2: e.matmul(banks[b][:, :], lhsT=wv2[:, gi * 8 + dc, :],
                                                                                                rhs=hT[:, dc, 2 + tb * 512:2 + (tb + 1) * 512], start=(dc == 0), stop=(dc == 7)),
                                     reads=[("w", wi)] + hT_all, writes=[("bank", b)])
                            s.op("act", lambda e, b=b, ub=ub, tb=tb: e.copy(out=ub[:, 2 + tb * 512:2 + (tb + 1) * 512], in_=banks[b][:, :]), reads=[("bank", b)], writes=[ukey])
                        idx = gi * 22 + j
                        tt, tkey = (tg, "tg") if gi == 0 else (tv, "tv")
                        s.op("act", lambda e, ub=ub, tt=tt, idx=idx: e.activation(out=tt[:], in_=ub[:, 2:2 + SBK], func=AF.Identity, scale=fcw[:, 88 + idx:89 + idx],
                                                                                 bias=fcw[:, 132 + idx:133 + idx]), reads=[ukey, "fcw"], writes=[tkey])
                        for k in (0, 1):
                            s.op("dve", lambda e, ub=ub, tt=tt, idx=idx, k=k: e.scalar_tensor_tensor(out=tt[:], in0=ub[:, k:k + SBK], scalar=fcw[:, k * 44 + idx:k * 44 + idx + 1],
                                                                                                    in1=tt[:], op0=ALU.mult, op1=ALU.add), reads=[ukey, "fcw", tkey], writes=[tkey])
                    s.op("act", lambda e: e.activation(out=tg[:], in_=tg[:], func=AF.Silu), reads=["tg"], writes=["tg"])
                    s.op("dve", lambda e, j=j: e.tensor_tensor(out=actv[:, j, :], in0=tg[:], in1=tv[:], op=ALU.mult), reads=["tg", "tv"], writes=[("act", j)])
                for dh in range(2):
                    wdt = big2[:, 0:22 * 256].rearrange("p (j c) -> p j c", j=22)
                    for q in range(2):
                        s.op("pq", lambda e, dh=dh, q=q, wdt=wdt: e.dma_start(out=wdt, in_=wdv[:, :, dh * 512 + q * 256:dh * 512 + (q + 1) * 256]), writes=["wd"])
                        c0 = dh * 512 + q * 256

                        def ld(i, c0=c0):
                            xi = i % 2
                            r0 = t0 + i * 128
                            s.op("sp", lambda e, xi=xi, r0=r0, c0=c0: e.dma_start(out=xs_t[xi][:, 0:256], in_=xmid[r0:r0 + 128, c0:c0 + 256]), reads=["xmid"], writes=[("xs", xi)])
                        ld(0)
                        for i in range(TPS):
                            if i + 1 < TPS:
                                ld(i + 1)
                            b = bankrot.next(); xi = i % 2
                            r0 = t0 + i * 128
                            for j in range(22):
                                s.op("pe", lambda e, b=b, j=j, i=i, wdt=wdt: e.matmul(banks[b][:, 0:256], lhsT=actv[:, j, i * 128:(i + 1) * 128], rhs=wdt[:, j, :],
                                                                                     start=(j == 0), stop=(j == 21)), reads=["wd", ("act", j)], writes=[("bank", b)])
                            s.op("dve", lambda e, b=b, xi=xi: e.tensor_tensor(out=xs_t[xi][:, 0:256], in0=xs_t[xi][:, 0:256], in1=banks[b][:, 0:256], op=ALU.add),
                                 reads=[("bank", b), ("xs", xi)], writes=[("xs", xi)])
                            s.op("sp", lambda e, xi=xi, r0=r0, c0=c0: e.dma_start(out=xdst[r0:r0 + 128, c0:c0 + 256], in_=xs_t[xi][:, 0:256]),
                                 reads=[("xs", xi)], writes=[xdst.tensor.name])

        xcur = x_in
        for l in range(DEPTH):
            phase_A(l, xcur)
            s.barrier()
            phase_B(l)
            s.barrier()
            phase_C(l, xcur)
            s.barrier()
            phase_D(l, xbuf[l % 2])
            s.barrier()
            xcur = xbuf[l % 2]
        load_gain(final_g[0:1, :])
        for sb in range(NSB):
            norm_tiles(xcur, sb, to_hT=False, dst_dram=out)
        s.finalize()
    return nc, s


_CACHE = {}


def kernel(**inputs):
    x = np.ascontiguousarray(np.asarray(inputs["x"], dtype=np.float32))
    B, S, _ = x.shape
    depth = int(np.asarray(inputs["ln1_g"]).shape[0])
    key = (S, depth)
    if key not in _CACHE:
        _CACHE[key] = build(S, depth)[0]
    nc = _CACHE[key]
    consts = host_constants(S)
    shared = {}
    for k, v in inputs.items():
        if k == "x":
            continue
        a = np.ascontiguousarray(np.asarray(v, dtype=np.float32))
        if k == "final_g":
            a = a.reshape(1, -1)
        shared[k] = a
    shared.update(consts)
    in_maps = []
    for c in range(8):
        m = dict(shared)
        m["x"] = x[c % B]
        in_maps.append(m)
    res = run_bass_kernel_spmd(nc, in_maps, core_ids=list(range(8)))
    return np.stack([np.asarray(res.results[b]["out"], dtype=np.float32) for b in range(B)], axis=0)
```

```python
import contextlib
import math
import numpy as np
import ml_dtypes
import concourse.bass as bass
import concourse.mybir as mybir
from concourse.bass_utils import run_bass_kernel_spmd

F32 = mybir.dt.float32
BF16 = mybir.dt.bfloat16
AF = mybir.ActivationFunctionType
ALU = mybir.AluOpType
AX = mybir.AxisListType

COMPUTE = ("pe", "act", "dve", "pool")
DMAQ = ("sp", "pq")
NSEM_DMA = 8
NEGB = -30000.0
D = 1024
DIN = 8968
DFF = 2816
EPS = 1e-6


class Sched:
    def __init__(self, nc):
        self.nc = nc
        self.ins = []

    def op(self, eng, fn, reads=(), writes=()):
        self.ins.append(dict(eng=eng, fn=fn, reads=tuple(reads), writes=tuple(writes)))

    def barrier(self):
        self.ins.append(dict(eng="bar", fn=None, reads=(), writes=()))

    def finalize(self):
        nc = self.nc
        ins = self.ins
        last_writer = {}
        readers = {}
        real = []
        pend = None
        tails = {}
        seen_after = set()
        for it in ins:
            if it["eng"] == "bar":
                pend = set()
                for v in tails.values():
                    pend.update(v)
                seen_after = set()
                continue
            st = "pool" if it["eng"] == "pq" else it["eng"]
            it["bar_deps"] = set()
            if pend is not None and st not in seen_after:
                it["bar_deps"] = set(pend)
                seen_after.add(st)
            idx = len(real)
            real.append(it)
            if it["eng"] in DMAQ:
                tails.setdefault(it["eng"], []).append(idx)
                tails[it["eng"]] = tails[it["eng"]][-NSEM_DMA:]
            else:
                tails[it["eng"]] = [idx]
        ins = self.ins = real
        for i, it in enumerate(ins):
            deps = set(it["bar_deps"])
            for k in it["reads"]:
                if k in last_writer:
                    deps.add(last_writer[k])
            for k in it["writes"]:
                if k in last_writer:
                    deps.add(last_writer[k])
                deps.update(readers.get(k, ()))
            deps.discard(i)
            for k in it["reads"]:
                readers.setdefault(k, []).append(i)
            for k in it["writes"]:
                last_writer[k] = i
                readers[k] = []
            deps = set(j for j in deps if not (ins[j]["eng"] == "pe" and it["eng"] == "pe"))
            best = {}
            keep = set()
            for j in deps:
                ej = ins[j]["eng"]
                if ej in DMAQ:
                    keep.add(j)
                else:
                    best[ej] = max(best.get(ej, -1), j)
            keep.update(best.values())
            it["deps"] = keep

        def stream(e):
            return "pool" if e == "pq" else e

        signal = [False] * len(ins)
        for it in ins:
            for j in it["deps"]:
                signal[j] = True
        cnt = {e: 0 for e in COMPUTE}
        dcnt = {q: 0 for q in DMAQ}
        for i, it in enumerate(ins):
            e = it["eng"]
            if e in DMAQ:
                it["dma_k"] = dcnt[e]
                dcnt[e] += 1
            elif signal[i]:
                cnt[e] += 1
                it["sig"] = cnt[e]
        self.stats = dict(n=len(ins), sig=dict(cnt), dma=dict(dcnt))
        with contextlib.ExitStack() as es:
            sems = {e: es.enter_context(nc.semaphore("s_" + e)) for e in COMPUTE}
            dsems = {q: [es.enter_context(nc.semaphore("d_%s%d" % (q, n))) for n in range(NSEM_DMA)]
                     for q in DMAQ}
            block = es.enter_context(nc.Block())
            streams = {"pe": [], "act": [], "dve": [], "pool": [], "sp": []}
            for i, it in enumerate(ins):
                streams[stream(it["eng"])].append(i)

            def emit_stream(sname, eobj):
                waited = {}

                def wait(key, sem, val):
                    if waited.get(key, 0) >= val:
                        return
                    waited[key] = val
                    eobj.wait_ge(sem, val)

                for i in streams[sname]:
                    it = ins[i]
                    for j in sorted(it["deps"]):
                        jt = ins[j]
                        if jt["eng"] in DMAQ:
                            k = jt["dma_k"]
                            wait((jt["eng"], k % NSEM_DMA), dsems[jt["eng"]][k % NSEM_DMA],
                                 16 * (k // NSEM_DMA + 1))
                        else:
                            wait(jt["eng"], sems[jt["eng"]], jt["sig"])
                    if it["eng"] in DMAQ:
                        q = it["eng"]
                        k = it["dma_k"]
                        if k >= NSEM_DMA:
                            wait((q, k % NSEM_DMA), dsems[q][k % NSEM_DMA], 16 * (k // NSEM_DMA))
                        it["fn"](eobj).then_inc(dsems[q][k % NSEM_DMA], 16)
                    else:
                        h = it["fn"](eobj)
                        if "sig" in it:
                            h.then_inc(sems[it["eng"]], 1)
                for q in DMAQ:
                    if stream(q) != sname:
                        continue
                    n = dcnt[q]
                    for s_ in range(min(n, NSEM_DMA)):
                        kmax = ((n - 1 - s_) // NSEM_DMA) * NSEM_DMA + s_
                        wait((q, s_), dsems[q][s_], 16 * (kmax // NSEM_DMA + 1))

            @block.tensor
            def _(e):
                emit_stream("pe", e)

            @block.scalar
            def _(e):
                emit_stream("act", e)

            @block.vector
            def _(e):
                emit_stream("dve", e)

            @block.gpsimd
            def _(e):
                emit_stream("pool", e)

            @block.sync
            def _(e):
                emit_stream("sp", e)


class Rot:
    def __init__(self, items):
        self.items = list(items)
        self.i = 0

    def next(self):
        it = self.items[self.i % len(self.items)]
        self.i += 1
        return it


def rel_bucket_np(d):
    n = np.maximum(d, 0)
    nf = np.maximum(n, 1).astype(np.float32)
    large = 16 + (np.log(nf / np.float32(16)) / np.float32(math.log(128 / 16)) * np.float32(16)).astype(np.int32)
    large = np.minimum(large, 31)
    return np.where(n < 16, n, large)


def host_constants(S):
    NT, NBLK = S // 128, S // 256
    L = 1152
    dist = np.arange(L) - 511
    onehot = np.zeros((32, L), np.float32)
    bk = rel_bucket_np(dist)
    for dd in range(L):
        if dist[dd] >= 0:
            onehot[bk[dd], dd] = 1.0
    mmul = np.zeros((17, L), np.float32)
    madd = np.zeros((17, L), np.float32)
    swa_ok = (dist >= 0) & (dist < 128)
    cau_ok = dist >= 0
    mmul[0:8] = swa_ok
    madd[0:8] = np.where(swa_ok, 0.0, NEGB)
    mmul[8:16] = cau_ok
    madd[8:16] = np.where(cau_ok, 0.0, NEGB)
    madd[16] = np.where(cau_ok, 0.0, NEGB)
    ismoba = np.zeros((17, 1), np.float32)
    ismoba[8:16] = 1.0
    own = (np.arange(NT) // 2)[:, None]
    nn = np.arange(NBLK)[None, :]
    vmask = np.where(nn < own, 0.0, -1e30).astype(np.float32).reshape(1, NT * NBLK)
    omask = np.where(nn == own, 1.0, 0.0).astype(np.float32).reshape(1, NT * NBLK)
    return dict(c_onehot=onehot, c_mmul=mmul, c_madd=madd, c_ismoba=ismoba, c_vmask=vmask, c_omask=omask)


def build(S, DEPTH, dbg=False):
    nc = bass.Bass("TRN2", target_bir_lowering=False)
    NT, NB, NBLK = S // 128, S // 512, S // 256
    SBK = 1024
    NSB = S // SBK
    TPS = SBK // 128
    BPS = SBK // 512

    def din(name, shape, dt=F32):
        return nc.dram_tensor(name, list(shape), dt, kind="ExternalInput").ap()

    def dscr(name, shape, dt):
        return nc.dram_tensor(name, list(shape), dt, kind=("ExternalOutput" if dbg else "Internal")).ap()

    x_in = din("x", [S, D])
    ln1_g = din("ln1_g", [DEPTH, D]); w_in = din("w_in", [DEPTH, D, DIN])
    b_gate = din("b_gate", [DEPTH, 4, D]); b_fgate = din("b_fgate", [DEPTH, 8]); sinks = din("sinks", [DEPTH, 8])
    conv_w = din("conv_w", [DEPTH, 31, 512]); conv_b = din("conv_b", [DEPTH, 512])
    conv_ln_g = din("conv_ln_g", [DEPTH, 512]); conv_ln_b = din("conv_ln_b", [DEPTH, 512])
    w_br = din("w_br", [DEPTH, 4, 512, D]); w_out = din("w_out", [DEPTH, D, D]); ln2_g = din("ln2_g", [DEPTH, D])
    w_up = din("w_up", [DEPTH, D, 2 * DFF]); ffn_conv_w = din("ffn_conv_w", [DEPTH, 3, 2 * DFF])
    ffn_conv_b = din("ffn_conv_b", [DEPTH, 2 * DFF]); w_down = din("w_down", [DEPTH, DFF, D])
    rel_bias = din("rel_bias", [32, 16]); final_g = din("final_g", [1, D])
    c_onehot = din("c_onehot", [32, 1152]); c_mmul = din("c_mmul", [17, 1152]); c_madd = din("c_madd", [17, 1152])
    c_ismoba = din("c_ismoba", [17, 1]); c_vmask = din("c_vmask", [1, NT * NBLK]); c_omask = din("c_omask", [1, NT * NBLK])
    out = nc.dram_tensor("out", [S, D], F32, kind="ExternalOutput").ap()

    qTa = dscr("qTa", [512, S], BF16); kTa = dscr("kTa", [128, S], BF16); va = dscr("va", [S, 128], BF16)
    uTb = dscr("uTb", [512, S], F32); cvT = dscr("cvT", [512, S], F32)
    qTc = dscr("qTc", [512, S], BF16); kTc = dscr("kTc", [512, S], BF16); vc = dscr("vc", [S, 512], BF16)
    fTc = dscr("fTc", [8, S], F32)
    qTd = dscr("qTd", [512, S], BF16); kTd = dscr("kTd", [512, S], BF16); vd = dscr("vd", [S, 512], BF16)
    brT = dscr("brT", [4, 512, S], BF16)
    xmid = dscr("xmid", [S, D], F32)
    xbuf = [dscr("xbuf0", [S, D], F32), dscr("xbuf1", [S, D], F32)]
    Fd = dscr("Fd", [17, 1152], BF16)

    s = Sched(nc)
    es = contextlib.ExitStack()
    with es:
        def T(name, shape, dt):
            return es.enter_context(nc.sbuf_tensor(name, list(shape), dt))

        banks = [es.enter_context(nc.psum_tensor("bank%d" % i, [128, 512], F32)) for i in range(8)]
        bankrot = Rot(range(8))
        ARENA = 47488
        arena = T("arena", [128, ARENA], F32)

        class Carver:
            def __init__(self):
                self.off = 0

            def get(self, P, n, dt):
                nb = n * (2 if dt == BF16 else 4)
                nb = (nb + 63) // 64 * 64
                a, b_ = self.off // 4, (self.off + nb) // 4
                self.off += nb
                assert self.off <= ARENA * 4, (self.off, ARENA * 4)
                v = arena[0:P, a:b_]
                if dt == BF16:
                    v = v.bitcast(BF16)
                return v[:, 0:n]

        cd = Carver()
        ca = Carver()

        identb = T("identb", [128, 128], BF16)
        antib = T("antib", [128, 128], BF16)
        identf = T("identf", [128, 128], F32)
        onesm = T("onesm", [128, 128], F32)
        onesf = T("onesf", [128, 128], F32)
        e127 = T("e127", [128, 128], F32)
        lstrict = T("lstrict", [128, 128], F32)
        for (t, nm, val) in ((identb, "identb", 1.0), (identf, "identf", 1.0), (e127, "e127", 1.0), (lstrict, "lstrict", 1.0)):
            s.op("pool", lambda e, t=t, val=val: e.memset(t[:], val), writes=[nm])
        s.op("pool", lambda e: e.memset(onesm[:], 1.0 / 512), writes=["onesm"])
        s.op("pool", lambda e: e.memset(antib[:], 1.0), writes=["antib"])
        s.op("pool", lambda e: e.affine_select(out=antib[:], in_=antib[:], pattern=[[1, 128]], compare_op=ALU.is_equal,
                                               fill=0.0, base=-127, channel_multiplier=1), reads=["antib"], writes=["antib"])
        s.op("pool", lambda e: e.memset(onesf[:], 1.0), writes=["onesf"])
        s.op("pool", lambda e: e.affine_select(out=identb[:], in_=identb[:], pattern=[[-1, 128]], compare_op=ALU.is_equal,
                                               fill=0.0, base=0, channel_multiplier=1), reads=["identb"], writes=["identb"])
        s.op("pool", lambda e: e.affine_select(out=identf[:], in_=identf[:], pattern=[[-1, 128]], compare_op=ALU.is_equal,
                                               fill=0.0, base=0, channel_multiplier=1), reads=["identf"], writes=["identf"])
        s.op("pool", lambda e: e.affine_select(out=e127[:], in_=e127[:], pattern=[[0, 128]], compare_op=ALU.is_ge,
                                               fill=0.0, base=-127, channel_multiplier=1), reads=["e127"], writes=["e127"])
        s.op("pool", lambda e: e.affine_select(out=lstrict[:], in_=lstrict[:], pattern=[[1, 128]], compare_op=ALU.is_ge,
                                               fill=0.0, base=-1, channel_multiplier=-1), reads=["lstrict"], writes=["lstrict"])

        def transpose_rows(dst, dst_key, src_ap, R, eng="sp"):
            stg = ldstage_rot.next()
            b = bankrot.next()
            s.op(eng, lambda e: e.dma_start(out=stg[0:R, :], in_=src_ap), writes=[("ldstage", id(stg))])
            s.op("pe", lambda e: e.transpose(out=banks[b][:, 0:R], in_=stg[0:R, :], identity=identf[0:R, 0:R]),
                 reads=[("ldstage", id(stg)), "identf"], writes=[("bank", b)])
            s.op("dve", lambda e: e.tensor_copy(out=dst, in_=banks[b][:, 0:R]), reads=[("bank", b)], writes=[dst_key])

        ldstage = [T("ldstage%d" % i, [128, 128], F32) for i in range(2)]
        ldstage_rot = Rot(ldstage)

        cs = Carver()
        relb = T("relb", [32, 17], F32)
        oneh = cs.get(32, 1152, F32)
        mmul = cs.get(17, 1152, F32)
        madd = cs.get(17, 1152, F32)
        sub31 = T("sub31", [17, 2], F32)
        ftmp = cs.get(17, 1152, F32)
        fbf = cs.get(17, 1152, BF16)
        s.op("pool", lambda e: e.memset(relb[:], 0.0), writes=["relb"])
        s.op("pool", lambda e: e.memset(sub31[:], 0.0), writes=["sub31"])
        s.op("sp", lambda e: e.dma_start(out=relb[:, 0:16], in_=rel_bias), reads=["relb"], writes=["relb"])
        s.op("sp", lambda e: e.dma_start(out=oneh[:], in_=c_onehot), writes=["oneh"])
        s.op("sp", lambda e: e.dma_start(out=mmul[:], in_=c_mmul), writes=["mmul"])
        s.op("sp", lambda e: e.dma_start(out=madd[:], in_=c_madd), writes=["madd"])
        s.op("sp", lambda e: e.dma_start(out=sub31[0:16, 0:1], in_=rel_bias[31:32, :].rearrange("o h -> h o")),
             reads=["sub31"], writes=["sub31"])
        s.op("sp", lambda e: e.dma_start(out=sub31[:, 1:2], in_=c_ismoba), reads=["sub31"], writes=["sub31"])
        s.op("dve", lambda e: e.tensor_tensor(out=sub31[:, 0:1], in0=sub31[:, 0:1], in1=sub31[:, 1:2], op=ALU.mult),
             reads=["sub31"], writes=["sub31"])
        for c0 in (0, 512, 1024):
            wdt = min(512, 1152 - c0)
            b = bankrot.next()
            s.op("pe", lambda e, b=b, c0=c0, wdt=wdt: e.matmul(banks[b][0:17, 0:wdt], lhsT=relb[:, :], rhs=oneh[:, c0:c0 + wdt],
                                                               start=True, stop=True), reads=["relb", "oneh"], writes=[("bank", b)])
            s.op("dve", lambda e, b=b, c0=c0, wdt=wdt: e.tensor_scalar(out=ftmp[:, c0:c0 + wdt], in0=banks[b][0:17, 0:wdt],
                                                                      scalar1=sub31[:, 0:1], scalar2=None, op0=ALU.subtract),
                 reads=[("bank", b), "sub31"], writes=["ftmp"])
        s.op("dve", lambda e: e.tensor_tensor(out=ftmp[:], in0=ftmp[:], in1=mmul[:], op=ALU.mult), reads=["ftmp", "mmul"], writes=["ftmp"])
        s.op("dve", lambda e: e.tensor_tensor(out=fbf[:], in0=ftmp[:], in1=madd[:], op=ALU.add), reads=["ftmp", "madd"], writes=["fbf"])
        s.op("sp", lambda e: e.dma_start(out=Fd, in_=fbf[:]), reads=["fbf"], writes=["Fd"])
        s.barrier()

        gbc = T("gbc", [128, D], F32)
        ssm = T("ssm", [128, 8], F32)
        colv = T("colv", [128, 512], F32)
        xs_t = [cd.get(128, D, F32) for i in range(2)]
        xs_rot = Rot(range(2))
        junk = cd.get(128, D, F32)
        hb_t = [cd.get(128, D, BF16) for i in range(2)]
        hb_rot = Rot(range(2))
        hT = cd.get(128, 8 * (2 + SBK), BF16).rearrange("p (a b) -> p a b", a=8)
        wbuf = [cd.get(128, 16 * 256, BF16) for i in range(5)]
        wrot = Rot(range(5))
        stg = [cd.get(128, 1024, F32) for i in range(3)]
        stgrot = Rot(range(3))
        big = cd.get(128, 22 * SBK, BF16)
        big2 = cd.get(128, 8 * SBK, BF16)
        rec = cd.get(128, 512, F32)
        osb2 = cd.get(128, 512, F32)
        wd2 = cd.get(128, 22 * 256, BF16)
        gsig = [cd.get(128, 512, F32) for i in range(2)]
        gsrot = Rot(range(2))
        wdrot = Rot(range(2))
        ug = cd.get(128, 2 + SBK, F32); uv = cd.get(128, 2 + SBK, F32)
        tg = cd.get(128, SBK, F32); tv = cd.get(128, SBK, F32)
        fcw = T("fcw", [128, 4 * 44], F32)

        def load_gain(g_ap):
            s.op("sp", lambda e: e.dma_start(out=gbc[:], in_=g_ap.partition_broadcast(128)), writes=["gbc"])

        def norm_tiles(src, sb, to_hT=True, dst_dram=None):
            for i in range(TPS):
                xi = xs_rot.next(); hi = hb_rot.next()
                r0 = sb * SBK + i * 128
                s.op("sp", lambda e, xi=xi, r0=r0: e.dma_start(out=xs_t[xi][:], in_=src[r0:r0 + 128, :]), writes=[("xs", xi)])
                s.op("act", lambda e, xi=xi: e.activation(out=junk[:], in_=xs_t[xi][:], func=AF.Square, accum_out=ssm[:, 0:1]),
                     reads=[("xs", xi)], writes=["junk", "ssm"])
                s.op("act", lambda e: e.activation(out=ssm[:, 1:2], in_=ssm[:, 0:1], func=AF.Sqrt, scale=1.0 / D, bias=EPS),
                     reads=["ssm"], writes=["ssm"])
                s.op("dve", lambda e: e.reciprocal(out=ssm[:, 2:3], in_=ssm[:, 1:2]), reads=["ssm"], writes=["ssm"])
                if not to_hT:
                    s.op("dve", lambda e, xi=xi: e.scalar_tensor_tensor(out=junk[:], in0=xs_t[xi][:], scalar=ssm[:, 2:3], in1=gbc[:],
                                                                        op0=ALU.mult, op1=ALU.mult),
                         reads=[("xs", xi), "ssm", "gbc", "junk"], writes=["junk"])
                    s.op("sp", lambda e, r0=r0: e.dma_start(out=dst_dram[r0:r0 + 128, :], in_=junk[:]), reads=["junk"])
                    continue
                s.op("dve", lambda e, xi=xi, hi=hi: e.scalar_tensor_tensor(out=hb_t[hi][:], in0=xs_t[xi][:], scalar=ssm[:, 2:3], in1=gbc[:],
                                                                           op0=ALU.mult, op1=ALU.mult),
                     reads=[("xs", xi), "ssm", "gbc"], writes=[("hb", hi)])
                for half in range(2):
                    b = bankrot.next()
                    pb = banks[b][:].bitcast(BF16)
                    for q in range(4):
                        dc = half * 4 + q
                        s.op("pe", lambda e, pb=pb, q=q, dc=dc, hi=hi: e.transpose(out=pb[:, q * 128:(q + 1) * 128],
                                                                                  in_=hb_t[hi][:, dc * 128:(dc + 1) * 128], identity=identb[:]),
                             reads=[("hb", hi), "identb"], writes=[("bank", b)])
                    eng = "act" if half == 0 else "dve"
                    dstv = hT[:, half * 4:half * 4 + 4, 2 + i * 128:2 + (i + 1) * 128]
                    srcv = pb[:, 0:512].rearrange("p (q c) -> p q c", q=4)
                    if eng == "act":
                        s.op("act", lambda e, dstv=dstv, srcv=srcv: e.copy(out=dstv, in_=srcv), reads=[("bank", b)], writes=[("hT", i)])
                    else:
                        s.op("dve", lambda e, dstv=dstv, srcv=srcv: e.tensor_copy(out=dstv, in_=srcv), reads=[("bank", b)], writes=[("hT", i)])

        hT_all = [("hT", i) for i in range(TPS)]

        def wload(wi, src3, ncols, nk):
            v = wbuf[wi][:, 0:nk * ncols].rearrange("p (k c) -> p k c", k=nk)
            s.op("pq", lambda e: e.dma_start(out=v, in_=src3), writes=[("w", wi)])
            return v

        def phase_A(l, xsrc):
            load_gain(ln1_g[l:l + 1, :])
            wv = w_in[l].rearrange("(dc p) c -> p dc c", p=128)
            segs = [(0, 512, qTa, 0.125), (512, 128, kTa, 1.0), (1792, 512, qTc, 0.125), (2304, 512, kTc, 1.0),
                    (3336, 512, qTd, 0.125), (3848, 512, kTd, 1.0)]
            for sb in range(NSB):
                norm_tiles(xsrc, sb)
                t0 = sb * SBK
                for (c0, wd, dst, scl) in segs:
                    wi = wrot.next()
                    wt = wload(wi, wv[:, :, c0:c0 + wd], wd, 8)
                    for ct in range(wd // 128):
                        si = stgrot.next()
                        sv = stg[si][:].bitcast(BF16)
                        for tb in range(BPS):
                            b = bankrot.next()
                            for dc in range(8):
                                s.op("pe", lambda e, b=b, wt=wt, dc=dc, ct=ct, tb=tb: e.matmul(
                                    banks[b][:, :], lhsT=wt[:, dc, ct * 128:(ct + 1) * 128], rhs=hT[:, dc, 2 + tb * 512:2 + (tb + 1) * 512],
                                    start=(dc == 0), stop=(dc == 7)), reads=[("w", wi)] + hT_all, writes=[("bank", b)])
                            s.op("act", lambda e, b=b, sv=sv, tb=tb, scl=scl: e.activation(out=sv[:, tb * 512:(tb + 1) * 512], in_=banks[b][:, :],
                                                                                             func=AF.Copy, scale=scl),
                                 reads=[("bank", b)], writes=[("stg", si)])
                        s.op("sp", lambda e, dst=dst, ct=ct, sv=sv, t0=t0: e.dma_start(out=dst[ct * 128:(ct + 1) * 128, t0:t0 + SBK], in_=sv[:, 0:SBK]),
                             reads=[("stg", si)], writes=[(id(dst.tensor) if False else dst.tensor.name)])
                wi = wrot.next()
                wt = wload(wi, wv[:, :, 3328:3336], 8, 8)
                si = stgrot.next()
                for tb in range(BPS):
                    b = bankrot.next()
                    for dc in range(8):
                        s.op("pe", lambda e, b=b, wt=wt, dc=dc, tb=tb: e.matmul(banks[b][0:8, :], lhsT=wt[:, dc, 0:8],
                                                                               rhs=hT[:, dc, 2 + tb * 512:2 + (tb + 1) * 512], start=(dc == 0), stop=(dc == 7)),
                             reads=[("w", wi)] + hT_all, writes=[("bank", b)])
                    s.op("act", lambda e, b=b, si=si, tb=tb: e.copy(out=stg[si][0:8, tb * 512:(tb + 1) * 512], in_=banks[b][0:8, :]),
                         reads=[("bank", b)], writes=[("stg", si)])
                s.op("sp", lambda e, si=si, t0=t0: e.dma_start(out=fTc[:, t0:t0 + SBK], in_=stg[si][0:8, 0:SBK]), reads=[("stg", si)], writes=["fTc"])
                wia = wrot.next(); wta = wload(wia, wv[:, :, 768:1280], 512, 8)
                wig = wrot.next(); wtg = wload(wig, wv[:, :, 1280:1792], 512, 8)
                for ct in range(4):
                    si = stgrot.next()
                    for tb in range(BPS):
                        ba = bankrot.next(); bg = bankrot.next()
                        for (bb, wt_, wi_) in ((ba, wta, wia), (bg, wtg, wig)):
                            for dc in range(8):
                                s.op("pe", lambda e, bb=bb, wt_=wt_, dc=dc, ct=ct, tb=tb: e.matmul(
                                    banks[bb][:, :], lhsT=wt_[:, dc, ct * 128:(ct + 1) * 128], rhs=hT[:, dc, 2 + tb * 512:2 + (tb + 1) * 512],
                                    start=(dc == 0), stop=(dc == 7)), reads=[("w", wi_)] + hT_all, writes=[("bank", bb)])
                        sj = stgrot.next()
                        s.op("act", lambda e, bg=bg, sj=sj: e.activation(out=stg[sj][:, 0:512], in_=banks[bg][:, :], func=AF.Sigmoid),
                             reads=[("bank", bg)], writes=[("stg", sj)])
                        s.op("dve", lambda e, ba=ba, sj=sj, si=si, tb=tb: e.tensor_tensor(out=stg[si][:, tb * 512:(tb + 1) * 512],
                                                                                         in0=banks[ba][:, :], in1=stg[sj][:, 0:512], op=ALU.mult),
                             reads=[("bank", ba), ("stg", sj)], writes=[("stg", si)])
                    s.op("sp", lambda e, si=si, ct=ct, t0=t0: e.dma_start(out=uTb[ct * 128:(ct + 1) * 128, t0:t0 + SBK], in_=stg[si][:, 0:SBK]),
                         reads=[("stg", si)], writes=["uTb"])
                for (c0, wd, dst) in ((640, 128, va), (2816, 512, vc), (4360, 512, vd)):
                    wi = wrot.next()
                    wt = wload(wi, wv[:, :, c0:c0 + wd], wd, 8)
                    for i in range(TPS):
                        b = bankrot.next(); si = stgrot.next()
                        sv = stg[si][:].bitcast(BF16)
                        for dc in range(8):
                            s.op("pe", lambda e, b=b, wt=wt, dc=dc, i=i, wd=wd: e.matmul(banks[b][:, 0:wd], lhsT=hT[:, dc, 2 + i * 128:2 + (i + 1) * 128],
                                                                                       rhs=wt[:, dc, 0:wd], start=(dc == 0), stop=(dc == 7)),
                                 reads=[("w", wi), ("hT", i)], writes=[("bank", b)])
                        s.op("dve", lambda e, b=b, sv=sv, wd=wd: e.tensor_copy(out=sv[:, 0:wd], in_=banks[b][:, 0:wd]), reads=[("bank", b)], writes=[("stg", si)])
                        r0 = t0 + i * 128
                        s.op("sp", lambda e, dst=dst, sv=sv, wd=wd, r0=r0: e.dma_start(out=dst[r0:r0 + 128, :], in_=sv[:, 0:wd]),
                             reads=[("stg", si)], writes=[dst.tensor.name])

        QT = [ca.get(96, S, BF16) for i in range(2)]
        KT = [ca.get(96, S, BF16) for i in range(2)]
        VA = [ca.get(128, NT * 65, BF16).rearrange("p (a b) -> p a b", b=65) for i in range(2)]
        TAB = [ca.get(128, 1024, BF16) for i in range(2)]
        PT = [ca.get(128, 512, BF16) for i in range(3)]
        RES = [ca.get(64, 512, BF16) for i in range(2)]
        hrot = Rot(range(2)); ptrot = Rot(range(3)); resrot = Rot(range(2))
        arec = ca.get(128, 512, F32)
        osb = ca.get(64, 512, F32)
        esink = T("esink", [128, 8], F32)
        bfb = T("bfb", [128, 8], F32)
        fx = ca.get(NT, 128, F32); fl = ca.get(NT, 128, F32); fcs = ca.get(NT, 128, F32)
        cT = ca.get(128, NT, F32); cref = ca.get(128, NB, F32); cbq = ca.get(128, NT, F32)
        foff = T("foff", [NT, 2], F32)
        vmask = T("vmask", [128, NT * NBLK], BF16); omask = T("omask", [128, NT * NBLK], BF16)
        GW = min(NT, 512 // NBLK, 16)
        gm = ca.get(128, GW * NBLK, F32); sel = ca.get(128, GW * NBLK, F32)
        selb = ca.get(128, GW * 96, BF16).rearrange("p (a b) -> p a b", b=96)
        mx8 = T("mx8", [128, 8], F32)
        kmean = ca.get(64, NBLK, F32); kmh = ca.get(64, NBLK, BF16); kml = ca.get(64, NBLK, BF16)
        CH = min(S, 4096)
        ubuf = ca.get(128, 30 + CH, F32); cacc = ca.get(128, CH, F32)
        cwT = T("cwT", [128, 124], F32)
        amsk = ca.get(128, NT * NBLK, F32)

        s.op("sp", lambda e: e.dma_start(out=amsk[:], in_=c_vmask.partition_broadcast(128)), writes=["amsk"])
        s.op("dve", lambda e: e.tensor_copy(out=vmask[:], in_=amsk[:]), reads=["amsk"], writes=["vmask"])
        s.op("sp", lambda e: e.dma_start(out=amsk[:], in_=c_omask.partition_broadcast(128)), reads=["amsk"], writes=["amsk"])
        s.op("dve", lambda e: e.tensor_copy(out=omask[:], in_=amsk[:]), reads=["amsk"], writes=["omask"])
        s.barrier()

        def attn_init():
            for i in range(2):
                s.op("pool", lambda e, i=i: e.memset(VA[i][:, :, 64:65], 1.0), writes=[("VA", i)])
                s.op("pool", lambda e, i=i: e.memset(KT[i][64:96, :], 1.0), writes=[("KT", i)])
                s.op("pool", lambda e, i=i: e.affine_select(out=KT[i][64:96, :], in_=KT[i][64:96, :], pattern=[[1, S]], compare_op=ALU.is_ge,
                                                           fill=0.0, base=0, channel_multiplier=-256), reads=[("KT", i)], writes=[("KT", i)])
                s.op("pool", lambda e, i=i: e.affine_select(out=KT[i][64:96, :], in_=KT[i][64:96, :], pattern=[[-1, S]], compare_op=ALU.is_ge,
                                                           fill=0.0, base=255, channel_multiplier=256), reads=[("KT", i)], writes=[("KT", i)])
            s.op("pool", lambda e: e.memset(selb[:], 0.0), writes=["selb"])

        cTs = [cT, ca.get(128, NT, F32)]
        crefs = [cref, ca.get(128, NB, F32)]
        cbqs = [cbq, ca.get(128, NT, F32)]
        cbqrot = Rot(range(2))
        sb_rot = Rot([0, 1, 2]); ob_rot = Rot([3, 4])

        def head_params(kind, h):
            qsrc, ksrc, vsrc = {0: (qTa, kTa, va), 2: (qTc, kTc, vc), 3: (qTd, kTd, vd)}[kind]
            kvh = h // 4 if kind == 0 else h
            KA = 96
            frow = {0: h, 2: 16, 3: 8 + h}[kind]
            return qsrc, ksrc, vsrc, kvh, KA, frow

        def attn_loads(kind, h, hi):
            qsrc, ksrc, vsrc, kvh, KA, frow = head_params(kind, h)
            s.op("sp", lambda e: e.dma_start(out=QT[hi][0:64, :], in_=qsrc[h * 64:(h + 1) * 64, :]), reads=[qsrc.tensor.name], writes=[("QT", hi)])
            s.op("sp", lambda e: e.dma_start(out=KT[hi][0:64, :], in_=ksrc[kvh * 64:(kvh + 1) * 64, :]), reads=[ksrc.tensor.name], writes=[("KT", hi)])
            s.op("sp", lambda e: e.dma_start(out=VA[hi][:, :, 0:64], in_=vsrc[:, kvh * 64:(kvh + 1) * 64].rearrange("(kt p) c -> p kt c", p=128)),
                 reads=[vsrc.tensor.name], writes=[("VA", hi)])
            tsrc = bass.AP(tensor=Fd.tensor, offset=frow * 1152, ap=[[1, 128], [1, 1024]])
            s.op("sp", lambda e: e.dma_start(out=TAB[hi][:], in_=tsrc), reads=["Fd"], writes=[("TAB", hi)])
            if kind != 3:
                s.op("pool", lambda e: e.memset(QT[hi][64:96, :], 0.0), writes=[("QT", hi)])
            if kind == 2:
                s.op("sp", lambda e: e.dma_start(out=fx[:], in_=fTc[h:h + 1, :].rearrange("o (kt j) -> (o kt) j", j=128)), reads=["fTc"], writes=["fx"])

        def attn_prep(kind, h, hi):
            cT_, cref_ = cTs[hi], crefs[hi]
            if kind == 2:
                s.op("act", lambda e: e.activation(out=fl[:], in_=fx[:], func=AF.Sigmoid, bias=bfb[0:NT, h:h + 1]), reads=["fx", "bfb"], writes=["fl"])
                s.op("act", lambda e: e.activation(out=fl[:], in_=fl[:], func=AF.Ln), reads=["fl"], writes=["fl"])
                s.op("dve", lambda e: e.tensor_tensor_scan(out=fcs[:], data0=onesf[0:NT, :], data1=fl[:], initial=0.0, op0=ALU.mult, op1=ALU.add),
                     reads=["fl", "onesf"], writes=["fcs"])
                b = bankrot2.next()
                s.op("pe", lambda e, b=b: e.matmul(banks[b][0:NT, 0:1], lhsT=lstrict[0:NT, 0:NT], rhs=fcs[:, 127:128], start=True, stop=True),
                     reads=["lstrict", "fcs"], writes=[("bank", b)])
                s.op("dve", lambda e, b=b: e.tensor_copy(out=foff[:, 0:1], in_=banks[b][0:NT, 0:1]), reads=[("bank", b)], writes=["foff"])
                s.op("dve", lambda e: e.tensor_scalar(out=fcs[:], in0=fcs[:], scalar1=foff[:, 0:1], scalar2=None, op0=ALU.add),
                     reads=["fcs", "foff"], writes=["fcs"])
                b2 = bankrot2.next()
                s.op("pe", lambda e, b2=b2: e.transpose(out=banks[b2][:, 0:NT], in_=fcs[:], identity=identf[0:NT, 0:NT]),
                     reads=["fcs", "identf"], writes=[("bank", b2)])
                s.op("dve", lambda e, b2=b2: e.tensor_copy(out=cT_[:], in_=banks[b2][:, 0:NT]), reads=[("bank", b2)], writes=[("cT", hi)])
                b3 = bankrot2.next()
                s.op("pe", lambda e, b3=b3: e.matmul(banks[b3][:, 0:NB], lhsT=e127[:], rhs=cT_[:, 3::4], start=True, stop=True),
                     reads=["e127", ("cT", hi)], writes=[("bank", b3)])
                s.op("dve", lambda e, b3=b3: e.tensor_copy(out=cref_[:], in_=banks[b3][:, 0:NB]), reads=[("bank", b3)], writes=[("cref", hi)])
            if kind == 3:
                s.op("dve", lambda e: e.tensor_reduce(out=kmean[:], in_=KT[hi][0:64, :].rearrange("p (n j) -> p n j", j=256), axis=AX.X, op=ALU.add),
                     reads=[("KT", hi)], writes=["kmean"])
                s.op("dve", lambda e: e.tensor_scalar(out=kmean[:], in0=kmean[:], scalar1=1.0 / 256, scalar2=None, op0=ALU.mult), reads=["kmean"], writes=["kmean"])
                s.op("dve", lambda e: e.tensor_copy(out=kmh[:], in_=kmean[:]), reads=["kmean"], writes=["kmh"])
                s.op("dve", lambda e: e.tensor_tensor(out=kml[:], in0=kmean[:], in1=kmh[:], op=ALU.subtract), reads=["kmean", "kmh"], writes=["kml"])
                for g0 in range(0, NT, GW):
                    b = bankrot2.next()
                    for ii in range(GW):
                        i = g0 + ii
                        s.op("pe", lambda e, b=b, ii=ii, i=i: e.matmul(banks[b][:, ii * NBLK:(ii + 1) * NBLK], lhsT=QT[hi][0:64, i * 128:(i + 1) * 128],
                                                                      rhs=kmh[:, :], start=True, stop=False), reads=[("QT", hi), "kmh"], writes=[("bank", b)])
                        s.op("pe", lambda e, b=b, ii=ii, i=i: e.matmul(banks[b][:, ii * NBLK:(ii + 1) * NBLK], lhsT=QT[hi][0:64, i * 128:(i + 1) * 128],
                                                                      rhs=kml[:, :], start=False, stop=True), reads=[("QT", hi), "kml"], writes=[("bank", b)])
                    s.op("dve", lambda e, b=b, g0=g0: e.tensor_tensor(out=gm[:], in0=banks[b][:, 0:GW * NBLK], in1=vmask[:, g0 * NBLK:(g0 + GW) * NBLK], op=ALU.add),
                         reads=[("bank", b), "vmask"], writes=["gm"])
                    for ii in range(GW):
                        s.op("dve", lambda e, ii=ii: e.max(out=mx8[:], in_=gm[:, ii * NBLK:(ii + 1) * NBLK]), reads=["gm"], writes=["mx8"])
                        s.op("dve", lambda e, ii=ii: e.tensor_scalar(out=sel[:, ii * NBLK:(ii + 1) * NBLK], in0=gm[:, ii * NBLK:(ii + 1) * NBLK],
                                                                    scalar1=mx8[:, 2:3], scalar2=None, op0=ALU.is_ge), reads=["gm", "mx8"], writes=["sel"])
                    s.op("dve", lambda e, g0=g0: e.tensor_tensor(out=sel[:], in0=sel[:], in1=omask[:, g0 * NBLK:(g0 + GW) * NBLK], op=ALU.max),
                         reads=["sel", "omask"], writes=["sel"])
                    s.op("dve", lambda e: e.tensor_scalar(out=selb[:, :, 64:64 + NBLK], in0=sel[:].rearrange("p (g n) -> p g n", n=NBLK),
                                                          scalar1=-NEGB, scalar2=NEGB, op0=ALU.mult, op1=ALU.add), reads=["sel"], writes=["selb"])
                    for i4 in range(0, GW, 4):
                        b2 = bankrot2.next()
                        pb = banks[b2][:].bitcast(BF16)
                        for q in range(4):
                            s.op("pe", lambda e, pb=pb, q=q, i4=i4: e.transpose(out=pb[0:96, q * 128:(q + 1) * 128], in_=selb[:, i4 + q, :], identity=identb[:]),
                                 reads=["selb", "identb"], writes=[("bank", b2)])
                        c0 = (g0 + i4) * 128
                        s.op("act", lambda e, pb=pb, c0=c0: e.copy(out=QT[hi][64:96, c0:c0 + 512], in_=pb[64:96, 0:512]), reads=[("bank", b2)], writes=[("QT", hi)])

        bankrot2 = Rot([5, 6])

        def attn_main(kind, h, hi, mid_fn=None, conv_it=None):
            qsrc, ksrc, vsrc, kvh, KA, frow = head_params(kind, h)
            cT_, cref_ = cTs[hi], crefs[hi]
            pending = [None]

            def fin2():
                ob, qb = pending[0]
                pending[0] = None
                s.op("pe", lambda e: e.matmul(banks[7][0:64, :], lhsT=onesf[64:65, 0:64], rhs=arec[64:65, :], start=True, stop=True),
                     reads=["onesf", "arec"], writes=[("bank", 7)])
                s.op("act", lambda e, ob=ob: e.copy(out=osb[:], in_=banks[ob][0:64, :]), reads=[("bank", ob)], writes=["osb"])
                ri = resrot.next()
                s.op("dve", lambda e, ri=ri: e.tensor_tensor(out=RES[ri][:], in0=osb[:], in1=banks[7][0:64, :], op=ALU.mult),
                     reads=["osb", ("bank", 7)], writes=[("RES", ri)])
                s.op("sp", lambda e, ri=ri, qb=qb: e.dma_start(out=brT[kind, h * 64:(h + 1) * 64, qb * 512:(qb + 1) * 512], in_=RES[ri][:]),
                     reads=[("RES", ri)], writes=["brT"])

            for qb in range(NB):
                kt_lo = max(0, 4 * qb - 1) if kind == 0 else 0
                kt_hi = 4 * qb + 3
                tiles = list(range(kt_lo, kt_hi + 1))
                cq = None
                if kind == 2:
                    def mk_cbq(qq):
                        cqi_ = qq % 2
                        cq_ = cbqs[cqi_]
                        s.op("dve", lambda e, qq=qq, cq_=cq_: e.tensor_scalar(out=cq_[:], in0=cT_[:], scalar1=-1.0, scalar2=cref_[:, qq:qq + 1], op0=ALU.mult, op1=ALU.add),
                             reads=[("cT", hi), ("cref", hi)], writes=[("cbq", cqi_)])
                    if qb == 0:
                        mk_cbq(0)
                    cqi = qb % 2
                    cq = cbqs[cqi]
                ob = ob_rot.next()
                sbank = {}

                def issue_S(kt, qb=qb, sbank=sbank):
                    near = kt >= 4 * qb - 1
                    sbk_ = sb_rot.next()
                    sbank[kt] = sbk_
                    s.op("pe", lambda e, sbk_=sbk_, kt=kt, qb=qb, near=near: e.matmul(banks[sbk_][:, :], lhsT=KT[hi][0:KA, kt * 128:(kt + 1) * 128],
                                                                                    rhs=QT[hi][0:KA, qb * 512:(qb + 1) * 512], start=True, stop=not near),
                         reads=[("KT", hi), ("QT", hi)], writes=[("bank", sbk_)])
                    if near:
                        off = 512 * qb - 128 * kt + 384
                        s.op("pe", lambda e, sbk_=sbk_, off=off: e.matmul(banks[sbk_][:, :], lhsT=antib[:], rhs=TAB[hi][:, off:off + 512], start=False, stop=True),
                             reads=[("TAB", hi), "antib"], writes=[("bank", sbk_)])

                issue_S(tiles[0])
                issue_S(tiles[1])
                for idx, kt in enumerate(tiles):
                    sbk_ = sbank[kt]
                    pi = ptrot.next()
                    if kind == 2:
                        s.op("act", lambda e, sbk_=sbk_, pi=pi, kt=kt, cq=cq: e.activation(out=PT[pi][:], in_=banks[sbk_][:, :], func=AF.Exp, bias=cq[:, kt:kt + 1]),
                             reads=[("bank", sbk_), ("cbq", cqi)], writes=[("PT", pi)])
                    else:
                        s.op("act", lambda e, sbk_=sbk_, pi=pi: e.activation(out=PT[pi][:], in_=banks[sbk_][:, :], func=AF.Exp),
                             reads=[("bank", sbk_)], writes=[("PT", pi)])
                    if idx + 2 < len(tiles):
                        issue_S(tiles[idx + 2])
                    s.op("pe", lambda e, ob=ob, pi=pi, kt=kt, kt_lo=kt_lo, kt_hi=kt_hi: e.matmul(banks[ob][0:65, :], lhsT=VA[hi][:, kt, 0:65], rhs=PT[pi][:],
                                                                      start=(kt == kt_lo), stop=(kt == kt_hi)),
                         reads=[("VA", hi), ("PT", pi)], writes=[("bank", ob)])
                    if idx == 1 and pending[0] is not None:
                        fin2()
                    if idx == 1 and kind == 2 and qb + 1 < NB:
                        mk_cbq(qb + 1)
                if kind == 0:
                    s.op("dve", lambda e, ob=ob: e.tensor_scalar(out=arec[64:65, :], in0=banks[ob][64:65, :], scalar1=esink[64:65, h:h + 1], scalar2=None, op0=ALU.add),
                         reads=[("bank", ob), "esink"], writes=["arec"])
                    s.op("dve", lambda e: e.reciprocal(out=arec[64:65, :], in_=arec[64:65, :]), reads=["arec"], writes=["arec"])
                else:
                    s.op("dve", lambda e, ob=ob: e.reciprocal(out=arec[64:65, :], in_=banks[ob][64:65, :]), reads=[("bank", ob)], writes=["arec"])
                pending[0] = (ob, qb)
                if conv_it is not None and kind != 0:
                    next(conv_it, None)
                if mid_fn is not None and qb == NB // 2 - 1:
                    mid_fn()
            fin2()

        def conv_branch_gen(l):
            for cc in range(4):
                for c0 in range(0, S, CH):
                    if c0 > 0:
                        s.op("dve", lambda e: e.tensor_copy(out=ubuf[:, 0:30], in_=ubuf[:, CH:CH + 30]), reads=["ubuf"], writes=["ubuf"])
                    else:
                        s.op("dve", lambda e: e.memset(ubuf[:, 0:30], 0.0), reads=["ubuf"], writes=["ubuf"])
                    s.op("sp", lambda e, cc=cc, c0=c0: e.dma_start(out=ubuf[:, 30:30 + CH], in_=uTb[cc * 128:(cc + 1) * 128, c0:c0 + CH]),
                         reads=["uTb", "ubuf"], writes=["ubuf"])
                    s.op("dve", lambda e, cc=cc: e.tensor_scalar(out=cacc[:], in0=ubuf[:, 0:CH], scalar1=cwT[:, cc:cc + 1], scalar2=colv[:, cc:cc + 1],
                                                                op0=ALU.mult, op1=ALU.add), reads=["ubuf", "cwT", "colv"], writes=["cacc"])
                    yield
                    for k in range(1, 31):
                        s.op("dve", lambda e, cc=cc, k=k: e.scalar_tensor_tensor(out=cacc[:], in0=ubuf[:, k:k + CH], scalar=cwT[:, k * 4 + cc:k * 4 + cc + 1],
                                                                                in1=cacc[:], op0=ALU.mult, op1=ALU.add), reads=["ubuf", "cwT", "cacc"], writes=["cacc"])
                        if k < 30:
                            yield
                    s.op("sp", lambda e, cc=cc, c0=c0: e.dma_start(out=cvT[cc * 128:(cc + 1) * 128, c0:c0 + CH], in_=cacc[:]), reads=["cacc"], writes=["cvT"])
                    yield

        def phase_B(l):
            attn_init()
            s.op("sp", lambda e: e.dma_start(out=esink[:], in_=sinks[l:l + 1, :].partition_broadcast(128)), writes=["esink"])
            s.op("act", lambda e: e.activation(out=esink[:], in_=esink[:], func=AF.Exp), reads=["esink"], writes=["esink"])
            s.op("sp", lambda e: e.dma_start(out=bfb[:], in_=b_fgate[l:l + 1, :].partition_broadcast(128)), writes=["bfb"])
            transpose_rows(cwT[:, 0:124], "cwT", conv_w[l].rearrange("k (cc p) -> (k cc) p", p=128), 124)
            transpose_rows(colv[:, 0:4], "colv", conv_b[l:l + 1, :].rearrange("o (cc p) -> (o cc) p", p=128), 4)
            s.barrier()
            conv_it = conv_branch_gen(l)
            heads = [(2, h) for h in range(8)] + [(3, h) for h in range(8)] + [(0, h) for h in range(8)]
            his = [i % 2 for i in range(len(heads))]
            attn_loads(heads[0][0], heads[0][1], his[0])
            attn_prep(heads[0][0], heads[0][1], his[0])
            for i, (kind, h) in enumerate(heads):
                mid = None
                if i + 1 < len(heads):
                    nk, nh = heads[i + 1]
                    attn_loads(nk, nh, his[i + 1])
                    mid = (lambda nk=nk, nh=nh, nhi=his[i + 1]: attn_prep(nk, nh, nhi))
                attn_main(kind, h, his[i], mid_fn=mid, conv_it=conv_it)
            for _ in conv_it:
                pass

        def phase_C(l, xsrc):
            load_gain(ln1_g[l:l + 1, :])
            transpose_rows(colv[:, 0:32], "colv", b_gate[l].rearrange("n (dc p) -> (n dc) p", p=128), 32)
            transpose_rows(colv[:, 32:36], "colv", conv_ln_g[l:l + 1, :].rearrange("o (cc p) -> (o cc) p", p=128), 4)
            transpose_rows(colv[:, 36:40], "colv", conv_ln_b[l:l + 1, :].rearrange("o (cc p) -> (o cc) p", p=128), 4)
            wgv = w_in[l].rearrange("(dc p) c -> p dc c", p=128)
            wbv = w_br[l].rearrange("n (cc p) d -> p (n cc) d", p=128)
            wov = w_out[l].rearrange("(dc p) d -> p dc d", p=128)
            brv = big[:, 0:16 * SBK].rearrange("p (k t) -> p k t", k=16)
            mgv = big2[:].rearrange("p (k t) -> p k t", k=8)
            for sb in range(NSB):
                t0 = sb * SBK
                norm_tiles(xsrc, sb)
                for n in (0, 2, 3):
                    for cc in range(4):
                        s.op("sp", lambda e, n=n, cc=cc, t0=t0: e.dma_start(out=brv[:, n * 4 + cc, :], in_=brT[n, cc * 128:(cc + 1) * 128, t0:t0 + SBK]),
                             reads=["brT"], writes=[("br", n * 4 + cc)])
                for tb in range(BPS):
                    ys = []
                    for cc in range(4):
                        si = stgrot.next() if cc < 3 else None
                        ys.append(si)
                    ytile = [stg[ys[0]][:, 0:512], stg[ys[1]][:, 0:512], stg[ys[2]][:, 0:512], junk[:, 0:512]]
                    ykey = [("stg", ys[0]), ("stg", ys[1]), ("stg", ys[2]), "junk"]
                    sq = [stg[ys[0]][:, 512:1024], stg[ys[1]][:, 512:1024], stg[ys[2]][:, 512:1024], junk[:, 512:1024]]
                    bm = bankrot.next(); bq = bankrot.next()
                    for cc in range(4):
                        s.op("sp", lambda e, cc=cc, tb=tb, t0=t0, ytile=ytile: e.dma_start(out=ytile[cc], in_=cvT[cc * 128:(cc + 1) * 128, t0 + tb * 512:t0 + (tb + 1) * 512]),
                             reads=["cvT"], writes=[ykey[cc]])
                        s.op("act", lambda e, cc=cc, sq=sq, ytile=ytile: e.activation(out=sq[cc], in_=ytile[cc], func=AF.Square), reads=[ykey[cc]], writes=[ykey[cc]])
                    for cc in range(4):
                        s.op("pe", lambda e, cc=cc, bm=bm, ytile=ytile: e.matmul(banks[bm][:, :], lhsT=onesm[:], rhs=ytile[cc], start=(cc == 0), stop=(cc == 3)),
                             reads=["onesm", ykey[cc]], writes=[("bank", bm)])
                    for cc in range(4):
                        s.op("pe", lambda e, cc=cc, bq=bq, sq=sq: e.matmul(banks[bq][:, :], lhsT=onesm[:], rhs=sq[cc], start=(cc == 0), stop=(cc == 3)),
                             reads=["onesm", ykey[cc]], writes=[("bank", bq)])
                    s.op("act", lambda e, bm=bm: e.activation(out=rec[:], in_=banks[bm][:, :], func=AF.Square), reads=[("bank", bm)], writes=["rec"])
                    s.op("dve", lambda e, bq=bq: e.tensor_tensor(out=rec[:], in0=banks[bq][:, :], in1=rec[:], op=ALU.subtract), reads=[("bank", bq), "rec"], writes=["rec"])
                    s.op("act", lambda e: e.activation(out=rec[:], in_=rec[:], func=AF.Sqrt, bias=EPS), reads=["rec"], writes=["rec"])
                    s.op("dve", lambda e: e.reciprocal(out=rec[:], in_=rec[:]), reads=["rec"], writes=["rec"])
                    for cc in range(4):
                        s.op("dve", lambda e, cc=cc, bm=bm, ytile=ytile: e.tensor_tensor(out=ytile[cc], in0=ytile[cc], in1=banks[bm][:, :], op=ALU.subtract),
                             reads=[ykey[cc], ("bank", bm)], writes=[ykey[cc]])
                        s.op("dve", lambda e, cc=cc, ytile=ytile: e.tensor_tensor(out=ytile[cc], in0=ytile[cc], in1=rec[:], op=ALU.mult), reads=[ykey[cc], "rec"], writes=[ykey[cc]])
                        s.op("act", lambda e, cc=cc, tb=tb, ytile=ytile: e.activation(out=brv[:, 4 + cc, tb * 512:(tb + 1) * 512], in_=ytile[cc], func=AF.Silu,
                                                                        scale=colv[:, 32 + cc:33 + cc], bias=colv[:, 36 + cc:37 + cc]),
                             reads=[ykey[cc], "colv"], writes=[("br", 4 + cc)])
                for dg in range(4):
                    for pr in range(2):
                        wi = wrot.next()
                        wbt = wload(wi, wbv[:, pr * 8:pr * 8 + 8, dg * 256:(dg + 1) * 256], 256, 8)
                        wj = wrot.next()
                        c0 = 4872 + pr * 2048 + dg * 256
                        va_ = wbuf[wj][:, 0:16 * 256].rearrange("p (k c) -> p k c", k=16)
                        s.op("pq", lambda e, va_=va_, c0=c0: e.dma_start(out=va_[:, 0:8, :], in_=wgv[:, :, c0:c0 + 256]), writes=[("w", wj)])
                        s.op("pq", lambda e, va_=va_, c0=c0: e.dma_start(out=va_[:, 8:16, :], in_=wgv[:, :, c0 + 1024:c0 + 1280]), reads=[("w", wj)], writes=[("w", wj)])
                        for n in (2 * pr, 2 * pr + 1):
                            k0 = (n % 2) * 8
                            for dt in range(2):
                                dcol = dg * 2 + dt
                                for tb in range(BPS):
                                    bp = bankrot.next(); bg = bankrot.next()
                                    for cc in range(4):
                                        s.op("pe", lambda e, bp=bp, n=n, cc=cc, dt=dt, tb=tb, wbt=wbt: e.matmul(
                                            banks[bp][:, :], lhsT=wbt[:, (n % 2) * 4 + cc, dt * 128:(dt + 1) * 128], rhs=brv[:, n * 4 + cc, tb * 512:(tb + 1) * 512],
                                            start=(cc == 0), stop=(cc == 3)), reads=[("w", wi), ("br", n * 4 + cc)], writes=[("bank", bp)])
                                    for dc in range(8):
                                        s.op("pe", lambda e, bg=bg, dc=dc, dt=dt, tb=tb, va_=va_, k0=k0: e.matmul(
                                            banks[bg][:, :], lhsT=va_[:, k0 + dc, dt * 128:(dt + 1) * 128], rhs=hT[:, dc, 2 + tb * 512:2 + (tb + 1) * 512],
                                            start=(dc == 0), stop=(dc == 7)), reads=[("w", wj)] + hT_all, writes=[("bank", bg)])
                                    gs = gsrot.next()
                                    s.op("act", lambda e, bg=bg, n=n, dcol=dcol, gs=gs: e.activation(out=gsig[gs][:], in_=banks[bg][:, :], func=AF.Sigmoid,
                                                                                                     bias=colv[:, n * 8 + dcol:n * 8 + dcol + 1]),
                                         reads=[("bank", bg), "colv"], writes=[("gsig", gs)])
                                    mslice = mgv[:, dcol, tb * 512:(tb + 1) * 512]
                                    if n == 0:
                                        s.op("dve", lambda e, bp=bp, mslice=mslice, gs=gs: e.tensor_tensor(out=mslice, in0=banks[bp][:, :], in1=gsig[gs][:], op=ALU.mult),
                                             reads=[("bank", bp), ("gsig", gs)], writes=[("mg", dcol)])
                                    else:
                                        s.op("dve", lambda e, bp=bp, gs=gs: e.tensor_tensor(out=osb2[:], in0=banks[bp][:, :], in1=gsig[gs][:], op=ALU.mult),
                                             reads=[("bank", bp), ("gsig", gs)], writes=["osb2"])
                                        s.op("dve", lambda e, mslice=mslice: e.tensor_tensor(out=mslice, in0=mslice, in1=osb2[:], op=ALU.add),
                                             reads=["osb2", ("mg", dcol)], writes=[("mg", dcol)])
                for dh in range(2):
                    wi = wrot.next()
                    wot = wload(wi, wov[:, :, dh * 512:(dh + 1) * 512], 512, 8)

                    def ld(i, dh=dh):
                        xi = i % 2
                        r0 = t0 + i * 128
                        s.op("sp", lambda e, xi=xi, r0=r0, dh=dh: e.dma_start(out=xs_t[xi][:, 0:512], in_=xsrc[r0:r0 + 128, dh * 512:(dh + 1) * 512]), writes=[("xs", xi)])
                    ld(0)
                    for i in range(TPS):
                        if i + 1 < TPS:
                            ld(i + 1)
                        b = bankrot.next(); xi = i % 2
                        r0 = t0 + i * 128
                        for dc in range(8):
                            s.op("pe", lambda e, b=b, dc=dc, i=i, wot=wot: e.matmul(banks[b][:, :], lhsT=mgv[:, dc, i * 128:(i + 1) * 128], rhs=wot[:, dc, :],
                                                                                   start=(dc == 0), stop=(dc == 7)), reads=[("w", wi), ("mg", dc)], writes=[("bank", b)])
                        s.op("dve", lambda e, b=b, xi=xi: e.tensor_tensor(out=xs_t[xi][:, 0:512], in0=xs_t[xi][:, 0:512], in1=banks[b][:, :], op=ALU.add),
                             reads=[("bank", b), ("xs", xi)], writes=[("xs", xi)])
                        s.op("sp", lambda e, xi=xi, r0=r0, dh=dh: e.dma_start(out=xmid[r0:r0 + 128, dh * 512:(dh + 1) * 512], in_=xs_t[xi][:, 0:512]),
                             reads=[("xs", xi)], writes=["xmid"])

        def phase_D(l, xdst):
            load_gain(ln2_g[l:l + 1, :])
            for k in range(3):
                transpose_rows(fcw[:, k * 44:(k + 1) * 44], "fcw", ffn_conv_w[l, k:k + 1, :].rearrange("o (j p) -> (o j) p", p=128), 44)
            transpose_rows(fcw[:, 132:176], "fcw", ffn_conv_b[l:l + 1, :].rearrange("o (j p) -> (o j) p", p=128), 44)
            wuv = w_up[l].rearrange("(dc p) c -> p dc c", p=128)
            wdv = w_down[l].rearrange("(j p) d -> p j d", p=128)
            actv = big[:, 0:22 * SBK].rearrange("p (j t) -> p j t", j=22)
            for sb in range(NSB):
                t0 = sb * SBK
                if sb == 0:
                    s.op("dve", lambda e: e.memset(hT[:, :, 0:2], 0.0), reads=hT_all + ["hThalo"], writes=["hThalo"])
                else:
                    s.op("dve", lambda e: e.tensor_copy(out=hT[:, :, 0:2], in_=hT[:, :, SBK:SBK + 2]), reads=hT_all + ["hThalo"], writes=["hThalo"])
                norm_tiles(xmid, sb)
                for j in range(22):
                    wi = wrot.next()
                    wv2 = wbuf[wi][:, 0:16 * 128].rearrange("p (k c) -> p k c", k=16)
                    s.op("pq", lambda e, wv2=wv2, j=j: e.dma_start(out=wv2[:, 0:8, :], in_=wuv[:, :, j * 128:(j + 1) * 128]), writes=[("w", wi)])
                    s.op("pq", lambda e, wv2=wv2, j=j: e.dma_start(out=wv2[:, 8:16, :], in_=wuv[:, :, DFF + j * 128:DFF + (j + 1) * 128]), reads=[("w", wi)], writes=[("w", wi)])
                    for (gi, ub, ukey) in ((0, ug, "ug"), (1, uv, "uv")):
                        b = bankrot.next()
                        for dc in range(8):
                            s.op("pe", lambda e, b=b, dc=dc, gi=gi, wv2=wv2: e.matmul(banks[b][:, 0:2], lhsT=wv2[:, gi * 8 + dc, :], rhs=hT[:, dc, 0:2],
                                                                                     start=(dc == 0), stop=(dc == 7)), reads=[("w", wi), "hThalo"], writes=[("bank", b)])
                        s.op("act", lambda e, b=b, ub=ub: e.copy(out=ub[:, 0:2], in_=banks[b][:, 0:2]), reads=[("bank", b)], writes=[ukey])
                        for tb in range(BPS):
                            b = bankrot.next()
                            for dc in range(8):
                                s.op("pe", lambda e, b=b, dc=dc, gi=gi, tb=tb, wv2=wv2: e.matmul(banks[b][:, :], lhsT=wv2[:, gi * 8 + dc, :],
                                                                                                rhs=hT[:, dc, 2 + tb * 512:2 + (tb + 1) * 512], start=(dc == 0), stop=(dc == 7)),
                                     reads=[("w", wi)] + hT_all, writes=[("bank", b)])
                            s.op("act", lambda e, b=b, ub=ub, tb=tb: e.copy(out=ub[:, 2 + tb * 512:2 + (tb + 1) * 512], in_=banks[b][:, :]), reads=[("bank", b)], writes=[ukey])
                        idx = gi * 22 + j
                        tt, tkey = (tg, "tg") if gi == 0 else (tv, "tv")
                        s.op("act", lambda e, ub=ub, tt=tt, idx=idx: e.activation(out=tt[:], in_=ub[:, 2:2 + SBK], func=AF.Identity, scale=fcw[:, 88 + idx:89 + idx],
                                                                                 bias=fcw[:, 132 + idx:133 + idx]), reads=[ukey, "fcw"], writes=[tkey])
                        for k in (0, 1):
                            s.op("dve", lambda e, ub=ub, tt=tt, idx=idx, k=k: e.scalar_tensor_tensor(out=tt[:], in0=ub[:, k:k + SBK], scalar=fcw[:, k * 44 + idx:k * 44 + idx + 1],
                                                                                                    in1=tt[:], op0=ALU.mult, op1=ALU.add), reads=[ukey, "fcw", tkey], writes=[tkey])
                    s.op("act", lambda e: e.activation(out=tg[:], in_=tg[:], func=AF.Silu), reads=["tg"], writes=["tg"])
                    s.op("dve", lambda e, j=j: e.tensor_tensor(out=actv[:, j, :], in0=tg[:], in1=tv[:], op=ALU.mult), reads=["tg", "tv"], writes=[("act", j)])
                for dh in range(2):
                    for q in range(2):
                        wdi = wdrot.next()
                        wdt = (big2[:, 0:22 * 256] if wdi == 0 else wd2[:, :]).rearrange("p (j c) -> p j c", j=22)
                        wdk = ("wd", wdi)
                        s.op("pq", lambda e, dh=dh, q=q, wdt=wdt: e.dma_start(out=wdt, in_=wdv[:, :, dh * 512 + q * 256:dh * 512 + (q + 1) * 256]), writes=[wdk])
                        c0 = dh * 512 + q * 256

                        def ld(i, c0=c0):
                            xi = i % 2
                            r0 = t0 + i * 128
                            s.op("sp", lambda e, xi=xi, r0=r0, c0=c0: e.dma_start(out=xs_t[xi][:, 0:256], in_=xmid[r0:r0 + 128, c0:c0 + 256]), reads=["xmid"], writes=[("xs", xi)])
                        ld(0)
                        for i in range(TPS):
                            if i + 1 < TPS:
                                ld(i + 1)
                            b = bankrot.next(); xi = i % 2
                            r0 = t0 + i * 128
                            for j in range(22):
                                s.op("pe", lambda e, b=b, j=j, i=i, wdt=wdt: e.matmul(banks[b][:, 0:256], lhsT=actv[:, j, i * 128:(i + 1) * 128], rhs=wdt[:, j, :],
                                                                                     start=(j == 0), stop=(j == 21)), reads=[wdk, ("act", j)], writes=[("bank", b)])
                            s.op("dve", lambda e, b=b, xi=xi: e.tensor_tensor(out=xs_t[xi][:, 0:256], in0=xs_t[xi][:, 0:256], in1=banks[b][:, 0:256], op=ALU.add),
                                 reads=[("bank", b), ("xs", xi)], writes=[("xs", xi)])
                            s.op("sp", lambda e, xi=xi, r0=r0, c0=c0: e.dma_start(out=xdst[r0:r0 + 128, c0:c0 + 256], in_=xs_t[xi][:, 0:256]),
                                 reads=[("xs", xi)], writes=[xdst.tensor.name])

        xcur = x_in
        for l in range(DEPTH):
            phase_A(l, xcur)
            s.barrier()
            phase_B(l)
            s.barrier()
            phase_C(l, xcur)
            s.barrier()
            phase_D(l, xbuf[l % 2])
            s.barrier()
            xcur = xbuf[l % 2]
        load_gain(final_g[0:1, :])
        for sb in range(NSB):
            norm_tiles(xcur, sb, to_hT=False, dst_dram=out)
        s.finalize()
    return nc, s


_CACHE = {}


def kernel(**inputs):
    x = np.ascontiguousarray(np.asarray(inputs["x"], dtype=np.float32))
    B, S, _ = x.shape
    depth = int(np.asarray(inputs["ln1_g"]).shape[0])
    key = (S, depth)
    if key not in _CACHE:
        _CACHE[key] = build(S, depth)[0]
    nc = _CACHE[key]
    consts = host_constants(S)
    shared = {}
    for k, v in inputs.items():
        if k == "x":
            continue
        a = np.ascontiguousarray(np.asarray(v, dtype=np.float32))
        if k == "final_g":
            a = a.reshape(1, -1)
        shared[k] = a
    shared.update(consts)
    in_maps = []
    for c in range(8):
        m = dict(shared)
        m["x"] = x[c % B]
        in_maps.append(m)
    res = run_bass_kernel_spmd(nc, in_maps, core_ids=list(range(8)))
    return np.stack([np.asarray(res.results[b]["out"], dtype=np.float32) for b in range(B)], axis=0)
```
